# Optimizing a Trainium2 kernel written in Bass

```python
import jax, jax.numpy as jnp
from jax import lax
import numpy as np

D_MODEL = 1024
BATCH = 2
SEQ = 8192
DEPTH = 2

GRID_W = 64
CTX_LEN = 256
EPS = 1e-6
N_MOD = 9

MIX_W = D_MODEL
POOL_W = MIX_W // 4
GLA_W = MIX_W // 4
ATT_W = MIX_W // 4
FNET_W = MIX_W - POOL_W - GLA_W - ATT_W

POOL_WINDOWS = (2, 4, 8, 16)
POOL_GW = POOL_W // len(POOL_WINDOWS)

GLA_HEADS = 4
GLA_DV = GLA_W // GLA_HEADS
GLA_DK = GLA_DV // 2
GLA_RANK = 16
GLA_TAU = 16.0
GLA_CHUNK = 64

ATT_HD = 64
ATT_QH = ATT_W // ATT_HD
ATT_KVH = 2
ATT_GROUP = ATT_QH // ATT_KVH
ROPE_FREQS = ATT_HD // 4
ROPE_THETA = 10000.0
Q_BLOCK = 128

FNET_HEADS = 4
FNET_HD = FNET_W // FNET_HEADS

D_FF = 2816

PROJ_SIZES = (POOL_W,
              GLA_HEADS * GLA_DK, GLA_HEADS * GLA_DK, GLA_W, GLA_RANK, GLA_RANK, GLA_W,
              ATT_QH * ATT_HD, ATT_KVH * ATT_HD, ATT_KVH * ATT_HD,
              FNET_W)
D_IN = sum(PROJ_SIZES)

kernel_name = 'hybrid_parallel_group_diffusion_block'


def rmsnorm(x, g):
    xf = x.astype(jnp.float32)
    y = xf * lax.rsqrt(jnp.mean(xf * xf, axis=-1, keepdims=True) + EPS)
    return (y * g.astype(jnp.float32)).astype(x.dtype)


def adaln(h, g, shift, scale):
    return rmsnorm(h, g) * (1 + scale) + shift


def swiglu(h, wg, wu, wd):
    return (jax.nn.silu(h @ wg) * (h @ wu)) @ wd


def half_ffn(h, g, m, wg, wu, wd):
    return h + 0.5 * m[2] * swiglu(adaln(h, g, m[0], m[1]), wg, wu, wd)


def split_proj(p):
    cuts = [int(v) for v in np.cumsum(PROJ_SIZES)[:-1]]
    return jnp.split(p, cuts, axis=-1)


def pool_mix(u, w_pool, s_pool):
    B, N, _ = u.shape
    uf = u.astype(jnp.float32)
    cs = jnp.concatenate([jnp.zeros((B, 1, POOL_W), jnp.float32), jnp.cumsum(uf, axis=1)], axis=1)
    t = jnp.arange(N)
    groups = []
    for gi, w in enumerate(POOL_WINDOWS):
        sl = slice(gi * POOL_GW, (gi + 1) * POOL_GW)
        lo = jnp.clip(t - w // 2, 0, N)
        hi = jnp.clip(t + w - w // 2, 0, N)
        cnt = (hi - lo).astype(jnp.float32)[None, :, None]
        csg = cs[:, :, sl]
        groups.append((csg[:, hi] - csg[:, lo]) / cnt - uf[:, :, sl])
    m = jnp.stack(groups, axis=2).astype(u.dtype)
    y = jnp.einsum('bngc,gcd->bngd', m, w_pool).reshape(B, N, POOL_W)
    return y * s_pool


def gla_scan(q, k, v, g, s0):
    B, H, N, _ = k.shape
    C = GLA_CHUNK
    nc = N // C
    with_out = q is not None

    def to_chunks(a):
        return jnp.moveaxis(a.reshape(B, H, nc, C, a.shape[-1]), 2, 0)

    mask = jnp.tril(jnp.ones((C, C), dtype=bool))

    def step(S, inp):
        if with_out:
            qi, ki, vi, gi = inp
        else:
            ki, vi, gi = inp
        b = jnp.cumsum(gi.astype(jnp.float32), axis=2)
        b_last = b[:, :, -1:, :]
        kf = ki.astype(jnp.float32)
        vf = vi.astype(jnp.float32)
        S_new = jnp.exp(b_last[:, :, 0, :, None]) * S + jnp.einsum('bhck,bhcv->bhkv', kf * jnp.exp(b_last - b), vf)
        if not with_out:
            return S_new, None
        qf = qi.astype(jnp.float32)
        o_inter = jnp.einsum('bhck,bhkv->bhcv', qf * jnp.exp(b), S)
        diff = b[:, :, :, None, :] - b[:, :, None, :, :]
        decay = jnp.exp(jnp.where(mask[:, :, None], diff, -jnp.inf))
        A = jnp.einsum('bhik,bhjk,bhijk->bhij', qf, kf, decay)
        return S_new, o_inter + jnp.einsum('bhij,bhjv->bhiv', A, vf)

    xs = (to_chunks(q), to_chunks(k), to_chunks(v), to_chunks(g)) if with_out else (to_chunks(k), to_chunks(v), to_chunks(g))
    S, o = lax.scan(step, s0, xs)
    if not with_out:
        return None, S
    return jnp.moveaxis(o, 0, 2).reshape(B, H, N, v.shape[-1]), S


def gla_prep(parts, wa, ba):
    q, k, v, rf, rb, og = parts
    B, N, _ = q.shape

    def hd(a, d):
        return a.reshape(B, N, GLA_HEADS, d).transpose(0, 2, 1, 3)

    def log_decay(r, w, b):
        return hd(jax.nn.log_sigmoid((r @ w + b).astype(jnp.float32)) / GLA_TAU, GLA_DK)

    return (hd(q, GLA_DK) * GLA_DK ** -0.5, hd(k, GLA_DK), hd(v, GLA_DV),
            log_decay(rf, wa[0], ba[0]), log_decay(rb, wa[1], ba[1]), og)


def gla_out(o, og, g):
    B, H, N, _ = o.shape
    o = rmsnorm(o, g).transpose(0, 2, 1, 3).reshape(B, N, GLA_W).astype(og.dtype)
    return o * jax.nn.silu(og)


def gla_mix(c_parts, l_parts, wa, ba, g_norm, ctx_out):
    qc, kc, vc, fc, bc, ogc = gla_prep(c_parts, wa, ba)
    ql, kl, vl, fl, bl, ogl = gla_prep(l_parts, wa, ba)
    B = ql.shape[0]
    s0 = jnp.zeros((B, GLA_HEADS, GLA_DK, GLA_DV), jnp.float32)
    rev = lambda a: jnp.flip(a, axis=2)
    oc_f, sc_f = gla_scan(qc if ctx_out else None, kc, vc, fc, s0)
    oc_b, sc_b = gla_scan(rev(qc) if ctx_out else None, rev(kc), rev(vc), rev(bc), s0)
    ol_f, _ = gla_scan(ql, kl, vl, fl, sc_f)
    ol_b, _ = gla_scan(rev(ql), rev(kl), rev(vl), rev(bl), sc_b)
    y_lat = gla_out(ol_f + rev(ol_b), ogl, g_norm)
    if not ctx_out:
        return None, y_lat
    return gla_out(oc_f + rev(oc_b), ogc, g_norm), y_lat


def rope_tables(n):
    rows = n // GRID_W
    row_id = jnp.repeat(jnp.arange(rows, dtype=jnp.float32), GRID_W)
    col_id = jnp.tile(jnp.arange(GRID_W, dtype=jnp.float32), rows)
    freqs = ROPE_THETA ** (-jnp.arange(ROPE_FREQS, dtype=jnp.float32) / ROPE_FREQS)
    ang = jnp.stack([row_id[:, None] * freqs, col_id[:, None] * freqs], axis=1)
    return jnp.cos(ang), jnp.sin(ang)


def rope_2d(x, cos, sin):
    B, N, H, _ = x.shape
    xr = x.astype(jnp.float32).reshape(B, N, H, 2, 2, ROPE_FREQS)
    x1, x2 = xr[..., 0, :], xr[..., 1, :]
    c = cos[None, :, None]
    s = sin[None, :, None]
    out = jnp.stack([x1 * c - x2 * s, x1 * s + x2 * c], axis=-2)
    return out.reshape(B, N, H, ATT_HD).astype(x.dtype)


def gqa_mix(qc, kc, vc, ql, kl, vl, q_g, k_g, cos, sin, ctx_out):
    B, L, _ = kc.shape
    N = ql.shape[1]
    scale = ATT_HD ** -0.5

    def heads(a, h):
        return a.reshape(a.shape[0], a.shape[1], h, ATT_HD)

    kc = rmsnorm(heads(kc, ATT_KVH), k_g)
    vc = heads(vc, ATT_KVH)
    kl = rope_2d(rmsnorm(heads(kl, ATT_KVH), k_g), cos, sin)
    vl = heads(vl, ATT_KVH)
    ql = rope_2d(rmsnorm(heads(ql, ATT_QH), q_g), cos, sin)
    keys = jnp.concatenate([kl, kc], axis=1)
    vals = jnp.concatenate([vl, vc], axis=1)
    nb = N // Q_BLOCK
    qb = ql.reshape(B, nb, Q_BLOCK, ATT_KVH, ATT_GROUP, ATT_HD).swapaxes(0, 1)

    def attend(q_blk):
        s = jnp.einsum('bqkgd,bskd->bkgqs', q_blk, keys).astype(jnp.float32) * scale
        p = jax.nn.softmax(s, axis=-1).astype(vals.dtype)
        return jnp.einsum('bkgqs,bskd->bqkgd', p, vals)

    ol = lax.map(attend, qb).swapaxes(0, 1).reshape(B, N, ATT_W)
    if not ctx_out:
        return None, ol
    qc5 = rmsnorm(heads(qc, ATT_QH), q_g).reshape(B, L, ATT_KVH, ATT_GROUP, ATT_HD)
    s = jnp.einsum('blkgd,bmkd->bkglm', qc5, kc).astype(jnp.float32) * scale
    p = jax.nn.softmax(s, axis=-1).astype(vc.dtype)
    oc = jnp.einsum('bkglm,bmkd->blkgd', p, vc).reshape(B, L, ATT_W)
    return oc, ol


def fnet_mix(u, w_f):
    B, N, _ = u.shape
    z = u.astype(jnp.float32).reshape(B, N, FNET_HEADS, FNET_HD).transpose(0, 2, 1, 3)
    f = jnp.fft.fft2(z, norm='ortho').real
    f = f.transpose(0, 2, 1, 3).reshape(B, N, FNET_W).astype(u.dtype)
    return f @ w_f


def token_mix(pc, pl, pool_w, pool_scale, gla_wa, gla_ba, gla_norm, q_g, k_g, fnet_w, cos, sin, ctx_out):
    cp = split_proj(pc)
    lp = split_proj(pl)
    gc, gl = gla_mix(cp[1:7], lp[1:7], gla_wa, gla_ba, gla_norm, ctx_out)
    ac, al = gqa_mix(cp[7], cp[8], cp[9], lp[7], lp[8], lp[9], q_g, k_g, cos, sin, ctx_out)
    y_lat = jnp.concatenate([pool_mix(lp[0], pool_w, pool_scale), gl, al, fnet_mix(lp[10], fnet_w)], axis=-1)
    if not ctx_out:
        return None, y_lat
    y_ctx = jnp.concatenate([pool_mix(cp[0], pool_w, pool_scale), gc, ac, fnet_mix(cp[10], fnet_w)], axis=-1)
    return y_ctx, y_lat


def setup_inputs(seed: int = 0) -> dict:
    key = jax.random.key(seed)
    ks = jax.random.split(key, 24)
    f32 = jnp.float32
    L, D = DEPTH, D_MODEL

    def nrm(k, shape, s):
        return jax.random.normal(k, shape, f32) * s

    return {
        'x': nrm(ks[0], (BATCH, SEQ, D), 1.0),
        'c': nrm(ks[1], (BATCH, D), 1.0),
        'ctx': nrm(ks[2], (BATCH, CTX_LEN, D), 1.0),
        'c_ctx': nrm(ks[3], (D,), 1.0),
        'w_mod': nrm(ks[4], (L, D, N_MOD * D), 0.5 * D ** -0.5),
        'b_mod': nrm(ks[5], (L, N_MOD * D), 0.01),
        'norm_g': 1.0 + nrm(ks[6], (L, 3, D), 0.02),
        'ffn_wg': nrm(ks[7], (L, 2, D, D_FF), D ** -0.5),
        'ffn_wu': nrm(ks[8], (L, 2, D, D_FF), D ** -0.5),
        'ffn_wd': nrm(ks[9], (L, 2, D_FF, D), D_FF ** -0.5),
        'w_in': nrm(ks[10], (L, D, D_IN), D ** -0.5),
        'w_out': nrm(ks[11], (L, MIX_W, D), MIX_W ** -0.5),
        'pool_w': nrm(ks[12], (L, len(POOL_WINDOWS), POOL_GW, POOL_GW), POOL_GW ** -0.5),
        'pool_scale': 1.0 + nrm(ks[13], (L, POOL_W), 0.1),
        'gla_wa': nrm(ks[14], (L, 2, GLA_RANK, GLA_HEADS * GLA_DK), GLA_RANK ** -0.5),
        'gla_ba': nrm(ks[15], (L, 2, GLA_HEADS * GLA_DK), 0.1),
        'gla_norm': 1.0 + nrm(ks[16], (L, GLA_DV), 0.02),
        'att_qnorm': 1.0 + nrm(ks[17], (L, ATT_HD), 0.02),
        'att_knorm': 1.0 + nrm(ks[18], (L, ATT_HD), 0.02),
        'fnet_w': nrm(ks[19], (L, FNET_W, FNET_W), FNET_W ** -0.5),
        'final_norm': 1.0 + nrm(ks[20], (D,), 0.02),
    }


def reference(x, c, ctx, c_ctx, w_mod, b_mod, norm_g, ffn_wg, ffn_wu, ffn_wd, w_in, w_out,
              pool_w, pool_scale, gla_wa, gla_ba, gla_norm, att_qnorm, att_knorm, fnet_w, final_norm):
    N = x.shape[1]
    cos, sin = rope_tables(N)
    xc = ctx
    for i in range(DEPTH):
        ctx_out = i < DEPTH - 1
        ml = jnp.split((jax.nn.silu(c) @ w_mod[i] + b_mod[i])[:, None, :], N_MOD, axis=-1)
        mc = jnp.split(jax.nn.silu(c_ctx) @ w_mod[i] + b_mod[i], N_MOD, axis=-1)
        x = half_ffn(x, norm_g[i, 0], ml[0:3], ffn_wg[i, 0], ffn_wu[i, 0], ffn_wd[i, 0])
        xc = half_ffn(xc, norm_g[i, 0], mc[0:3], ffn_wg[i, 0], ffn_wu[i, 0], ffn_wd[i, 0])
        pl = adaln(x, norm_g[i, 1], ml[3], ml[4]) @ w_in[i]
        pc = adaln(xc, norm_g[i, 1], mc[3], mc[4]) @ w_in[i]
        yc, yl = token_mix(pc, pl, pool_w[i], pool_scale[i], gla_wa[i], gla_ba[i], gla_norm[i],
                           att_qnorm[i], att_knorm[i], fnet_w[i], cos, sin, ctx_out)
        x = x + ml[5] * (yl @ w_out[i])
        x = half_ffn(x, norm_g[i, 2], ml[6:9], ffn_wg[i, 1], ffn_wu[i, 1], ffn_wd[i, 1])
        if ctx_out:
            xc = xc + mc[5] * (yc @ w_out[i])
            xc = half_ffn(xc, norm_g[i, 2], mc[6:9], ffn_wg[i, 1], ffn_wu[i, 1], ffn_wd[i, 1])
    return rmsnorm(x, final_norm)
```

```python
import numpy as np
from contextlib import ExitStack
import concourse.bass as bass
import concourse.mybir as mybir
from concourse.bass_utils import run_bass_kernel_spmd

F32 = mybir.dt.float32
BF16 = mybir.dt.bfloat16
F32R = mybir.dt.float32r
AF = mybir.ActivationFunctionType
ALU = mybir.AluOpType

D = 1024
NLAT = 8192
NCTX = 256
NTOK = NLAT + NCTX
TL = 2048
TC = 64
NT = TL + TC
DFF = 2816
NJ = DFF // 128
EPS = 1e-6
NHC = 608
TT = [(0, 512, 0), (512, 512, 0), (1024, 512, 0), (1536, 512, 0), (2048, 64, 1)]

_c = {}
_o = 0
for _n, _w in [("ident", 128), ("ones", 128), ("maskF", 128), ("maskB", 128), ("band", 640),
               ("dA1", 128), ("dA2", 128), ("C128", 128), ("S128", 128), ("Tc", 64), ("Ts", 64),
               ("chC", 64), ("chS", 64), ("perm", 64), ("C256", 512), ("S256", 512)]:
    _c[_n] = (_o, _w)
    _o += _w
NCST = _o
V_C = 0
V_BMOD = 32
V_NG = 160
V_FN = 208
NVEC = 216
H_PW = 0
H_PS = 64
H_WA = 65
H_BA = 129
H_GN = 130
H_QN = 131
H_KN = 132
NHP = 133


class Buf:
    __slots__ = ("name", "w", "r")

    def __init__(self, name=""):
        self.name = name
        self.w = None
        self.r = []


class Chan:
    __slots__ = ("sem", "count", "key")

    def __init__(self, sem, key):
        self.sem = sem
        self.count = 0
        self.key = key


class Sched:
    COMPUTE = ("pe", "act", "dve", "pool")

    def __init__(self, nc, es):
        self.nc = nc
        self.es = es
        self.eng = {"pe": nc.tensor, "act": nc.scalar, "dve": nc.vector, "pool": nc.gpsimd, "sp": nc.sync}
        self.sem = {}
        self.cnt = {}
        for e in self.COMPUTE:
            self.sem[e] = es.enter_context(nc.semaphore("s_" + e))
            self.cnt[e] = 0
        self.waited = {e: {} for e in self.eng}
        self.nchan = 0
        self.chans = []
        self.extra = []
        self.need_barrier = False
        self.n_wait = 0
        self.n_ins = 0

    def chan(self, name):
        s = self.es.enter_context(self.nc.semaphore("c_%s_%d" % (name, self.nchan)))
        self.nchan += 1
        c = Chan(s, "c%d" % self.nchan)
        self.chans.append(c)
        return c

    def barrier(self):
        for e in self.eng:
            for e2 in self.COMPUTE:
                self._wait(e, (e2, self.sem[e2], self.cnt[e2]))
            for c in self.chans:
                self._wait(e, (c.key, c.sem, c.count))

    def _wait(self, e, tok):
        if tok is None:
            return
        key, sem, val = tok
        if val <= 0 or self.waited[e].get(key, 0) >= val:
            return
        self.eng[e].wait_ge(sem, val)
        self.waited[e][key] = val
        self.n_wait += 1

    def _deps(self, e, reads, writes, skip=None):
        for b in reads:
            if b.w is not None and b.w[0] != skip:
                self._wait(e, b.w)
        for b in writes:
            if b.w is not None and b.w[0] != skip:
                self._wait(e, b.w)
            for t in b.r:
                if t[0] != skip:
                    self._wait(e, t)

    def _mark(self, tok, reads, writes):
        for b in reads:
            b.r.append(tok)
            if len(b.r) > 16:
                d = {}
                for t in b.r:
                    if t[0] not in d or d[t[0]][2] < t[2]:
                        d[t[0]] = t
                b.r = list(d.values())
        for b in writes:
            b.w = tok
            b.r = []

    def op(self, e, fn, reads=(), writes=()):
        self._deps(e, reads, writes, skip="pe" if e == "pe" else None)
        ins = fn()
        self.cnt[e] += 1
        ins.then_inc(self.sem[e], 1)
        tok = (e, self.sem[e], self.cnt[e])
        self._mark(tok, reads, writes)
        self.n_ins += 1
        return tok

    def dma(self, ch, out, in_, reads=(), writes=(), q="sp", nowait=False, **kw):
        if not nowait:
            self._wait(q, (ch.key, ch.sem, ch.count))
        self._deps(q, reads, writes, skip=ch.key if nowait else None)
        ins = self.eng[q].dma_start(out=out, in_=in_, **kw)
        ch.count += 16
        ins.then_inc(ch.sem, 16)
        tok = (ch.key, ch.sem, ch.count)
        self._mark(tok, reads, writes)
        self.n_ins += 1
        return tok

    def finish(self, q, bufs):
        for b in bufs:
            self._wait(q, b.w)
            for t in b.r:
                self._wait(q, t)


def build(nlayers=2, stop=None, dbg=()):
    nc = bass.Bass("TRN2", target_bir_lowering=False)

    def din(name, shape, dt=F32):
        return nc.dram_tensor(name, list(shape), dt, kind="ExternalInput").ap()

    xin = din("xin", [NT + 64, D])
    vecs_d = din("vecs", [128, NVEC])
    cst_d = din("cst", [128, NCST])
    rope_d = din("rope", [2, 64, NLAT])
    hp_d = din("hp", [2, 128, NHP])
    wmod_d = din("w_mod_s", [2, D, 2304])
    wg_d = din("ffn_wg", [2, 2, D, DFF])
    wu_d = din("ffn_wu", [2, 2, D, DFF])
    wd_d = din("ffn_wd", [2, 2, DFF, D])
    winh_d = din("w_in_h", [2, D, NHC])
    wout_d = din("w_out", [2, D, D])
    fw_d = din("fnet_w", [2, 256, 256])
    yout = nc.dram_tensor("yout", [TL, D], F32, kind="ExternalOutput").ap()
    dbg_out = {}
    for name, shape, dt in dbg:
        dbg_out[name] = nc.dram_tensor("dbg_" + name, list(shape), dt, kind="ExternalOutput").ap()
    HN = [512, 512, 512, 512, 64]
    hxp = [nc.dram_tensor("hx%d" % t, [D, HN[t]], BF16).ap() for t in range(5)]
    hgp = [nc.dram_tensor("hg%d" % t, [4 * D, HN[t]], BF16).ap() for t in range(5)]
    yxp = [nc.dram_tensor("yx%d" % t, [64, NLAT], BF16).ap() for t in range(4)] + [nc.dram_tensor("yx4", [256, NCTX], BF16).ap()]
    ygp = [nc.dram_tensor("yg%d" % t, [4 * 64, NLAT], BF16).ap() for t in range(4)] + [nc.dram_tensor("yg4", [4 * 256, NCTX], BF16).ap()]
    fsc = nc.dram_tensor("fsc", [2, 64, NLAT], BF16).ap()
    B_hxp = [Buf("hx%d" % t) for t in range(5)]
    B_hgp = [Buf("hg%d" % t) for t in range(5)]
    B_yxp = [Buf("yx%d" % t) for t in range(5)]
    B_ygp = [Buf("yg%d" % t) for t in range(5)]
    B_fsc = Buf("fsc")
    oh_d = din("oh", [128, 4])

    es = ExitStack()

    class Scope(ExitStack):
        def __exit__(self, *a):
            S.need_barrier = True
            return super().__exit__(*a)

    with es:
        S = Sched(nc, es)
        cc_sem = es.enter_context(nc.semaphore("cc"))
        cc_n = [0]

        uid = [0]

        def sb(name, shape, dt, stack=es):
            uid[0] += 1
            if S.need_barrier:
                S.barrier()
                S.need_barrier = False
            return stack.enter_context(nc.sbuf_tensor("s%d_%s" % (uid[0], name), list(shape), dt))

        V, A, P, G = nc.vector, nc.scalar, nc.tensor, nc.gpsimd

        pbank = [es.enter_context(nc.psum_tensor("pb%d" % i, [128, 512], F32)) for i in range(8)]
        pbuf = [Buf("pb%d" % i) for i in range(8)]
        prr = [0]
        nrot = [8]

        def bank():
            i = prr[0] % nrot[0]
            prr[0] += 1
            return pbank[i], pbuf[i]

        xT = sb("xT", [128, 8, NT], F32)
        B_x = [[Buf("x%d_%d" % (t, c)) for c in range(8)] for t in range(len(TT))]
        vecs = sb("vecs", [128, NVEC], F32)
        B_vecs = Buf("vecs")
        cstb = sb("cstb", [128, NCST], BF16)
        B_cstb = Buf("cstb")
        cstf = sb("cstf", [128, 320], F32)
        B_cstf = Buf("cstf")
        onesf = sb("onesf", [128, 128], F32)
        hpf = sb("hpf", [128, 2, NHP], F32)
        hpb = sb("hpb", [128, 2, NHP], BF16)
        B_hp = Buf("hp")
        modT = sb("modT", [128, 2, 72, 2], F32)
        B_mod = Buf("mod")
        epsb = sb("epsb", [128, 1], F32)
        B_eps = Buf("eps")
        modA = sb("modA", [128, 2, 3, 8, 2], F32)
        modG = sb("modG", [128, 2, 3, 8, 2], F32)
        sq = sb("sq", [128, 2, 512], BF16)
        B_sq = [Buf("sq0"), Buf("sq1")]
        rstd = sb("rstd", [128, 512], F32)
        B_rstd = Buf("rstd")
        hs = sb("hs", [128, 2, 512], F32)
        B_hs = [Buf("hs0"), Buf("hs1")]
        ch_c = [S.chan("k%d" % i) for i in range(6)]
        ch_st = [S.chan("st%d" % i) for i in range(4)]
        CFO = {"ident": 0, "Tc": 128, "Ts": 192, "perm": 256}

        S.dma(ch_c[0], vecs[:], vecs_d, writes=[B_vecs])
        S.dma(ch_c[2], hpf[:], hp_d.rearrange("l p n -> p l n"), writes=[B_hp])
        for i, nm in enumerate(["ident", "Tc", "Ts", "perm"]):
            o, w = _c[nm]
            S.dma(ch_c[3], cstf[:, CFO[nm]:CFO[nm] + w], cst_d[:, o:o + w], writes=[B_cstf])
        with Scope() as st:
            cst = sb("cst", [128, NCST], F32, st)
            B_cst = Buf("cst")
            S.dma(ch_c[1], cst[:], cst_d, writes=[B_cst])
            S.op("dve", lambda: V.tensor_copy(out=cstb[:], in_=cst[:]), reads=[B_cst], writes=[B_cstb])
        S.op("dve", lambda: V.tensor_copy(out=hpb[:], in_=hpf[:]), reads=[B_hp], writes=[B_hp])
        S.op("pool", lambda: G.memset(epsb[:], EPS), writes=[B_eps])
        S.op("pool", lambda: G.memset(onesf[:], 1.0), writes=[B_cstf])

        def C(name, parts=128, lo=0, hi=None):
            o, w = _c[name]
            hi = w if hi is None else hi
            return cstb[0:parts, o + lo:o + hi]

        def CF(name, parts=128, lo=0, hi=None):
            o = CFO[name]
            return cstf[0:parts, o + lo:o + hi]

        with Scope() as st:
            xs = [sb("xs%d" % i, [128, D], F32, st) for i in range(2)]
            B_xs = [Buf("xs0"), Buf("xs1")]
            ch_x = [S.chan("x0"), S.chan("x1")]
            nblk = (NT + 127) // 128
            for bi in range(nblk):
                r0 = bi * 128
                n = min(128, NT - r0)
                k = bi % 2
                S.dma(ch_x[k], xs[k][:, :], xin[r0:r0 + 128, :], writes=[B_xs[k]])
                ti = min(r0 // 512, 4)
                for half in range(2):
                    pb, pbf = bank()
                    for c4 in range(4):
                        c = half * 4 + c4
                        S.op("pe", lambda c=c, c4=c4, pb=pb, k=k, n=n: P.transpose(
                            out=pb[:, c4 * 128:c4 * 128 + 128], in_=xs[k][:, c * 128:(c + 1) * 128],
                            identity=CF("ident", 128, 0, 128)), reads=[B_xs[k], B_cstf], writes=[pbf])
                    src = pb[:, :].rearrange("p (c n) -> p c n", c=4)[:, :, 0:n]
                    dst = xT[:, half * 4:half * 4 + 4, r0:r0 + n]
                    if half == 0:
                        S.op("dve", lambda src=src, dst=dst: V.tensor_copy(out=dst, in_=src), reads=[pbf], writes=B_x[ti][0:4])
                    else:
                        S.op("act", lambda src=src, dst=dst: A.copy(out=dst, in_=src), reads=[pbf], writes=B_x[ti][4:8])

        def collective(src, dst, Bsrc, Bdst):
            S._deps("pool", [Bsrc], [Bdst])
            ins = G.collective_compute("AllGather", ALU.bypass, replica_groups=[[0, 1, 2, 3], [4, 5, 6, 7]],
                                       ins=[src], outs=[dst], dma_qos="P3")
            cc_n[0] += 1
            ins.then_inc(cc_sem, 1)
            S._mark(("cc", cc_sem, cc_n[0]), [Bsrc], [Bdst])
            S.extra = [("cc", cc_sem, cc_n[0])]

        sc = sb("sc", [128, 8, 2], F32)
        S.op("act", lambda: A.activation(out=sc[:].rearrange("p a b -> p (a b)"), in_=vecs[:, V_C:V_C + 16], func=AF.Silu),
             reads=[B_vecs], writes=[B_mod])
        mx = nc.dram_tensor("mx", [128, 72], F32).ap()
        mg = nc.dram_tensor("mg", [4 * 128, 72], F32).ap()
        B_mx, B_mg = Buf("mx"), Buf("mg")
        with Scope() as st:
            wm = [sb("wm%d" % i, [128, 8, 1152], F32, st) for i in range(2)]
            B_wm = [Buf("wm0"), Buf("wm1")]
            ch_w = [S.chan("wm0"), S.chan("wm1")]
            modS = sb("modS", [128, 2, 18, 2], F32, st)
            B_ms = Buf("modS")
            modAll = sb("modAll", [128, 4, 72], F32, st)
            B_ma = Buf("modAll")
            it = 0
            for li in range(2):
                for hf in range(2):
                    k = it % 2
                    it += 1
                    S.dma(ch_w[k], wm[k][:], wmod_d[li, :, hf * 1152:(hf + 1) * 1152].rearrange("(k p) n -> p k n", p=128), writes=[B_wm[k]])
                    pb, pbf = bank()
                    for cb in range(9):
                        for kk in range(8):
                            S.op("pe", lambda cb=cb, kk=kk, pb=pb, k=k: P.matmul(
                                pb[:, cb * 2:cb * 2 + 2], lhsT=wm[k][:, kk, cb * 128:(cb + 1) * 128], rhs=sc[:, kk, :],
                                start=(kk == 0), stop=(kk == 7)), reads=[B_wm[k], B_mod], writes=[pbf])
                    bm = vecs[:, V_BMOD + li * 18 + hf * 9:V_BMOD + li * 18 + hf * 9 + 9]
                    S.op("dve", lambda pb=pb, li=li, hf=hf, bm=bm: V.tensor_tensor(
                        out=modS[:, li, hf * 9:(hf + 1) * 9, :], in0=pb[:, 0:18].rearrange("p (c t) -> p c t", t=2),
                        in1=bm.unsqueeze(2).to_broadcast([128, 9, 2]), op=ALU.add), reads=[pbf, B_vecs], writes=[B_ms])
            S.dma(ch_c[3], mx, modS[:].rearrange("p a b c -> p (a b c)"), reads=[B_ms], writes=[B_mx])
            collective(mx, mg, B_mx, B_mg)
            S.dma(ch_c[3], modAll[:], mg.rearrange("(r p) n -> p r n", p=128), reads=[B_mg], writes=[B_ma])
            for li in range(nlayers):
                S.op("dve", lambda li=li: V.tensor_copy(
                    out=modT[:, li, :, :].rearrange("p (r c) v -> p r (c v)", r=4), in_=modAll[:, :, li * 36:(li + 1) * 36]),
                    reads=[B_ma], writes=[B_mod])

        def mod(li, j, c, kind):
            return modT[:, li, j * 8 + c, kind:kind + 1]

        for li in range(nlayers):
            for n3 in range(3):
                ng = vecs[:, V_NG + (li * 3 + n3) * 8:V_NG + (li * 3 + n3) * 8 + 8]
                jscale = n3 * 3 + 1
                jgate = n3 * 3 + 2
                S.op("dve", lambda li=li, n3=n3, ng=ng, jscale=jscale: V.scalar_tensor_tensor(
                    out=modA[:, li, n3, :, :], in0=modT[:, li, jscale * 8:(jscale + 1) * 8, :], scalar=1.0,
                    in1=ng.unsqueeze(2).to_broadcast([128, 8, 2]), op0=ALU.add, op1=ALU.mult),
                    reads=[B_mod, B_vecs], writes=[B_mod])
                S.op("dve", lambda li=li, n3=n3, jgate=jgate: V.tensor_scalar(
                    out=modG[:, li, n3, :, :], in0=modT[:, li, jgate * 8:(jgate + 1) * 8, :],
                    scalar1=(1.0 if n3 == 1 else 0.5), scalar2=None, op0=ALU.mult), reads=[B_mod], writes=[B_mod])

        sqi = [0]

        def col_rstd(src_fn, srcbufs, nparts, nch, n, inv_dim, rstd=rstd, B_rstd=B_rstd):
            pb, pbf = bank()
            for c in range(nch):
                s2 = sqi[0] % 2
                sqi[0] += 1
                S.op("act", lambda c=c, s2=s2: A.activation(out=sq[0:nparts, s2, 0:n], in_=src_fn(c), func=AF.Square),
                     reads=srcbufs, writes=[B_sq[s2]])
                S.op("pe", lambda c=c, pb=pb, s2=s2: P.matmul(pb[0:nparts, 0:n], lhsT=C("ones", nparts, 0, nparts),
                                                              rhs=sq[0:nparts, s2, 0:n], start=(c == 0), stop=(c == nch - 1)),
                     reads=[B_sq[s2], B_cstb], writes=[pbf])
            S.op("act", lambda pb=pb: A.activation(out=rstd[0:nparts, 0:n], in_=pb[0:nparts, 0:n], func=AF.Ln,
                                                   bias=epsb[0:nparts, :], scale=inv_dim),
                 reads=[pbf, B_eps], writes=[B_rstd])
            S.op("act", lambda: A.activation(out=rstd[0:nparts, 0:n], in_=rstd[0:nparts, 0:n], func=AF.Exp, scale=-0.5),
                 reads=[B_rstd], writes=[B_rstd])

        hsi = [0]

        def adaln(li, n3, ti, dst_fn, dstbufs):
            t0, n, kind = TT[ti]
            col_rstd(lambda c: xT[:, c, t0:t0 + n], list(B_x[ti]), 128, 8, n, 1.0 / D)
            for c in range(8):
                h2 = hsi[0] % 2
                hsi[0] += 1
                S.op("dve", lambda c=c, h2=h2: V.scalar_tensor_tensor(
                    out=hs[:, h2, 0:n], in0=xT[:, c, t0:t0 + n], scalar=modA[:, li, n3, c, kind:kind + 1],
                    in1=rstd[:, 0:n], op0=ALU.mult, op1=ALU.mult), reads=[B_x[ti][c], B_rstd, B_mod], writes=[B_hs[h2]])
                S.op("act", lambda c=c, h2=h2: A.activation(out=dst_fn(c), in_=hs[:, h2, 0:n], func=AF.Identity,
                                                            bias=mod(li, n3 * 3, c, kind), scale=1.0),
                     reads=[B_hs[h2], B_mod], writes=dstbufs)

        def half_ffn(li, hi, tiles, hT, B_h, st):
            n3 = 0 if hi == 0 else 2
            NB = 2
            wgs = sb("wgs", [128, 8, NB * 128], F32, st)
            wus = sb("wus", [128, 8, NB * 128], F32, st)
            wds = sb("wds", [128, NB, D], F32, st)
            wgb = [sb("wgb%d" % i, [128, 8, NB * 128], BF16, st) for i in range(2)]
            wub = [sb("wub%d" % i, [128, 8, NB * 128], BF16, st) for i in range(2)]
            wdb = [sb("wdb%d" % i, [128, NB, D], BF16, st) for i in range(2)]
            sg = [sb("sg%d" % i, [128, 512], F32, st) for i in range(2)]
            aT = [sb("aT%d" % i, [128, NB, 512], BF16, st) for i in range(2)]
            Bw = {k: Buf(k) for k in ["wgs", "wus", "wds"]}
            Bs = {k: [Buf(k + "0"), Buf(k + "1")] for k in ["wgb", "wub", "wdb", "sg", "aT"]}
            chs = {k: S.chan(k) for k in ["wgs", "wus", "wds"]}
            nblk = NJ // NB

            def load(jb):
                j0 = jb * NB * 128
                S.dma(chs["wgs"], wgs[:], wg_d[li, hi, :, j0:j0 + NB * 128].rearrange("(k p) n -> p k n", p=128), writes=[Bw["wgs"]])
                S.dma(chs["wus"], wus[:], wu_d[li, hi, :, j0:j0 + NB * 128].rearrange("(k p) n -> p k n", p=128), writes=[Bw["wus"]])
                S.dma(chs["wds"], wds[:], wd_d[li, hi, j0:j0 + NB * 128, :].rearrange("(j p) n -> p j n", p=128), writes=[Bw["wds"]])

            def cast(jb):
                k = jb % 2
                S.op("act", lambda: A.copy(out=wgb[k][:], in_=wgs[:]), reads=[Bw["wgs"]], writes=[Bs["wgb"][k]])
                S.op("act", lambda: A.copy(out=wub[k][:], in_=wus[:]), reads=[Bw["wus"]], writes=[Bs["wub"][k]])
                S.op("act", lambda: A.copy(out=wdb[k][:], in_=wds[:]), reads=[Bw["wds"]], writes=[Bs["wdb"][k]])

            load(0)
            cast(0)
            it = 0
            for jb in range(nblk):
                k = jb % 2
                if jb + 1 < nblk:
                    load(jb + 1)
                def stage1(ti, a):
                    t0, n, kind = TT[ti]
                    if jb == 0:
                        adaln(li, n3, ti, lambda c, t0=t0, n=n: hT[:, c, t0:t0 + n], [B_h[ti]])
                    for jj in range(NB):
                        pg, pgf = bank()
                        pu, puf = bank()
                        for kk in range(8):
                            S.op("pe", lambda kk=kk, pg=pg, jj=jj: P.matmul(
                                pg[:, 0:n], lhsT=wgb[k][:, kk, jj * 128:(jj + 1) * 128], rhs=hT[:, kk, t0:t0 + n],
                                start=(kk == 0), stop=(kk == 7)), reads=[Bs["wgb"][k], B_h[ti]], writes=[pgf])
                        for kk in range(8):
                            S.op("pe", lambda kk=kk, pu=pu, jj=jj: P.matmul(
                                pu[:, 0:n], lhsT=wub[k][:, kk, jj * 128:(jj + 1) * 128], rhs=hT[:, kk, t0:t0 + n],
                                start=(kk == 0), stop=(kk == 7)), reads=[Bs["wub"][k], B_h[ti]], writes=[puf])
                        s2 = jj % 2
                        S.op("act", lambda pg=pg, s2=s2: A.activation(out=sg[s2][:, 0:n], in_=pg[:, 0:n], func=AF.Silu),
                             reads=[pgf], writes=[Bs["sg"][s2]])
                        S.op("dve", lambda pu=pu, s2=s2, jj=jj: V.tensor_tensor(
                            out=aT[a][:, jj, 0:n], in0=pu[:, 0:n], in1=sg[s2][:, 0:n], op=ALU.mult),
                            reads=[puf, Bs["sg"][s2]], writes=[Bs["aT"][a]])

                def stage2(ti, a):
                    t0, n, kind = TT[ti]
                    for c in range(8):
                        po, pof = bank()
                        for jj in range(NB):
                            S.op("pe", lambda jj=jj, po=po, c=c: P.matmul(
                                po[:, 0:n], lhsT=wdb[k][:, jj, c * 128:(c + 1) * 128], rhs=aT[a][:, jj, 0:n],
                                start=(jj == 0), stop=(jj == NB - 1)), reads=[Bs["wdb"][k], Bs["aT"][a]], writes=[pof])
                        S.op("dve", lambda po=po, c=c: V.scalar_tensor_tensor(
                            out=xT[:, c, t0:t0 + n], in0=po[:, 0:n], scalar=modG[:, li, n3, c, kind:kind + 1],
                            in1=xT[:, c, t0:t0 + n], op0=ALU.mult, op1=ALU.add),
                            reads=[pof, B_mod, B_x[ti][c]], writes=[B_x[ti][c]])

                tl = list(tiles)
                stage1(tl[0], it % 2)
                for i_, ti in enumerate(tl):
                    a = it % 2
                    it += 1
                    if i_ + 1 < len(tl):
                        stage1(tl[i_ + 1], it % 2)
                    stage2(ti, a)
                    if i_ == min(1, len(tl) - 1) and jb + 1 < nblk:
                        cast(jb + 1)

        def dump(name, src_ap, bufs):
            if name in dbg_out:
                S.dma(ch_st[3], dbg_out[name], src_ap, reads=bufs, writes=[Buf()])

        O_PU, O_GQ, O_GK, O_R, O_OG, O_GV, O_AQ, O_AK, O_AV, O_FZ = 0, 64, 128, 192, 224, 288, 352, 416, 480, 544
        NG = 17
        GORDER = [q * 4 + t for t in range(4) for q in range(4)] + [16]

        def gtile(g):
            return (g * 512, 512) if g < 16 else (NLAT, NCTX)

        class HStream:
            def __init__(self, st):
                self.t = [sb("hst%d" % i, [128, 8, 512], BF16, st) for i in range(2)]
                self.b = [Buf("hst0"), Buf("hst1")]
                self.ch = [S.chan("hst0"), S.chan("hst1")]
                self.i = 0

            def load(self, g):
                k = self.i % 2
                self.i += 1
                if g < 16:
                    q, tt_ = g // 4, g % 4
                    S.dma(self.ch[k], self.t[k][:, :, :], hgp[tt_][q * D:(q + 1) * D, :].rearrange("(c p) n -> p c n", p=128),
                          reads=[B_hgp[tt_]], writes=[self.b[k]])
                else:
                    for q in range(4):
                        S.dma(self.ch[k], self.t[k][:, :, q * 64:(q + 1) * 64],
                              hgp[4][q * D:(q + 1) * D, :].rearrange("(c p) n -> p c n", p=128),
                              reads=[B_hgp[4]], writes=[self.b[k]], nowait=(q > 0))
                return self.t[k], self.b[k]

        def stage_b(li):
            ctx_out = (li < nlayers - 1) or (nlayers == 1)
            qtiles = list(range(NG)) if ctx_out else list(range(16))
            with Scope() as sB:
                winb = sb("winb", [128, 8, NHC], BF16, sB)
                B_win = Buf("winb")
                with Scope() as st:
                    wst = sb("wst", [128, 8, NHC], F32, st)
                    B_wst = Buf("wst")
                    S.dma(ch_c[4], wst[:], winh_d[li].rearrange("(k p) n -> p k n", p=128), writes=[B_wst])
                    S.op("act", lambda: A.copy(out=winb[:], in_=wst[:]), reads=[B_wst], writes=[B_win])
                ystage = [sb("ystage%d" % i, [128, 512], BF16, sB) for i in range(2)]
                B_ys = [Buf("ys0"), Buf("ys1")]
                ysi = [0]

                def proj_fm(ht, hb, col0, ncol, n, pb, pbf):
                    for kk in range(8):
                        S.op("pe", lambda kk=kk: P.matmul(pb[0:ncol, 0:n], lhsT=winb[:, kk, col0:col0 + ncol], rhs=ht[:, kk, 0:n],
                                                          start=(kk == 0), stop=(kk == 7)), reads=[B_win, hb], writes=[pbf])

                def proj_tm(ht, hb, col0, ncol, nsub, pb, pbf):
                    for sub in range(nsub):
                        for kk in range(8):
                            S.op("pe", lambda kk=kk, sub=sub: P.matmul(
                                pb[:, sub * ncol:(sub + 1) * ncol], lhsT=ht[:, kk, sub * 128:(sub + 1) * 128],
                                rhs=winb[:, kk, col0:col0 + ncol], start=(kk == 0), stop=(kk == 7)),
                                reads=[B_win, hb], writes=[pbf])

                def store_y(row0, tok0, n, fn):
                    k = ysi[0] % 2
                    ysi[0] += 1
                    fn(ystage[k][0:64, 0:n], B_ys[k])
                    m_ = row0 // 64
                    if tok0 < NLAT:
                        S.dma(ch_st[k], yxp[m_][0:64, tok0:tok0 + n], ystage[k][0:64, 0:n], reads=[B_ys[k]], writes=[B_yxp[m_]])
                    else:
                        S.dma(ch_st[k], yxp[4][row0:row0 + 64, 0:n], ystage[k][0:64, 0:n], reads=[B_ys[k]], writes=[B_yxp[4]])

                with Scope() as sAB:
                    u_tok = sb("u_tok", [128, 66, 64], BF16, sAB)
                    B_u = Buf("u_tok")
                    mT = [sb("mT%d" % i, [64, 512], BF16, sAB) for i in range(2)]
                    B_m = [Buf("mT0"), Buf("mT1")]
                    zT = [sb("zT%d" % i, [64, 512], BF16, sAB) for i in range(2)]
                    B_z = [Buf("z0"), Buf("z1")]
                    ust = [sb("ust%d" % i, [64, 2, 512], BF16, sAB) for i in range(2)]
                    B_us = [Buf("us0"), Buf("us1")]
                    ch_f = [S.chan("f0"), S.chan("f1")]
                    utok = sb("utok", [128, 2, 2, 64], BF16, sAB)
                    B_ut = Buf("utok")
                    st = sAB

                    def pool_proj(g, ht, hb):
                        t0, n = gtile(g)
                        nsub = n // 128
                        pb, pbf = bank()
                        proj_tm(ht, hb, O_PU, 64, nsub, pb, pbf)
                        S.op("act", lambda: A.copy(
                            out=u_tok[:, g * 4:g * 4 + nsub, :], in_=pb[:, 0:nsub * 64].rearrange("p (s c) -> p s c", c=64)),
                            reads=[pbf], writes=[B_u])

                    def fnet_proj(g, gi_, ht, hb):
                        if g == 16 and not ctx_out:
                            return
                        t0, n = gtile(g)
                        k = gi_ % 2
                        pb, pbf = bank()
                        proj_fm(ht, hb, O_FZ, 64, n, pb, pbf)
                        S.op("act", lambda pb=pb, k=k, n=n: A.copy(out=zT[k][:, 0:n], in_=pb[0:64, 0:n]), reads=[pbf], writes=[B_z[k]])
                        pr, prf = bank()
                        pi, pif = bank()
                        S.op("pe", lambda pr=pr, k=k, n=n: P.matmul(pr[0:64, 0:n], lhsT=C("chC", 64), rhs=zT[k][:, 0:n], start=True, stop=True),
                             reads=[B_cstb, B_z[k]], writes=[prf])
                        S.op("pe", lambda pi=pi, k=k, n=n: P.matmul(pi[0:64, 0:n], lhsT=C("chS", 64), rhs=zT[k][:, 0:n], start=True, stop=True),
                             reads=[B_cstb, B_z[k]], writes=[pif])
                        S.op("dve", lambda pr=pr, k=k, n=n: V.tensor_copy(out=ust[k][:, 0, 0:n], in_=pr[0:64, 0:n]), reads=[prf], writes=[B_us[k]])
                        S.op("act", lambda pi=pi, k=k, n=n: A.copy(out=ust[k][:, 1, 0:n], in_=pi[0:64, 0:n]), reads=[pif], writes=[B_us[k]])
                        if g < 16:
                            S.dma(ch_f[k], fsc[:, :, t0:t0 + n].rearrange("r c t -> c r t"), ust[k][:, :, 0:n], reads=[B_us[k]], writes=[B_fsc])
                        else:
                            pt, ptf = bank()
                            ptb = pt[:, :].bitcast(BF16)
                            for sub in range(2):
                                for ri in range(2):
                                    S.op("pe", lambda sub=sub, ri=ri: P.transpose(
                                        out=ptb[:, (sub * 2 + ri) * 64:(sub * 2 + ri + 1) * 64], in_=ust[k][:, ri, sub * 128:(sub + 1) * 128],
                                        identity=C("ident", 64, 0, 64)), reads=[B_us[k], B_cstb], writes=[ptf])
                            S.op("dve", lambda: V.tensor_copy(out=utok[:].rearrange("p a b c -> p (a b c)"), in_=ptb[:, 0:256]),
                                 reads=[ptf], writes=[B_ut])
                            px, pxf = bank()
                            i = 0
                            for sub in range(2):
                                for ri, nm in ((0, "C256"), (1, "S256")):
                                    S.op("pe", lambda sub=sub, ri=ri, nm=nm, i=i: P.matmul(
                                        px[0:64, 0:256], lhsT=utok[:, sub, ri, :], rhs=C(nm, 128, sub * 256, (sub + 1) * 256),
                                        start=(i == 0), stop=(i == 3)), reads=[B_ut, B_cstb], writes=[pxf])
                                    i += 1
                            store_y(192, NLAT, 256, lambda dst, db, px=px, pxf=pxf: S.op("act", lambda: A.activation(
                                out=dst, in_=px[0:64, 0:256], func=AF.Copy, scale=1.0 / 128.0), reads=[pxf], writes=[db]))

                    with Scope() as st:
                        qT_all = sb("qT_all", [128, NTOK], BF16, st)
                        kT_all = sb("kT_all", [128, NTOK], BF16, st)
                        Vext = sb("Vext", [128, 66, 128], BF16, st)
                        B_q, B_k, B_v = Buf("qT"), Buf("kT"), Buf("Vext")
                        S.op("dve", lambda: V.memset(Vext[:, :, 64:128], 1.0), writes=[B_v])
                        S.op("dve", lambda: V.memset(qT_all[64:128, :], 0.0), writes=[B_q])
                        S.op("dve", lambda: V.memset(kT_all[64:128, :], 0.0), writes=[B_k])
                        with Scope() as st2:
                            hsrc = HStream(st2)
                            cs = [sb("cs%d" % i, [64, 2, 512], F32, st2) for i in range(2)]
                            B_cs = [Buf("cs0"), Buf("cs1")]
                            ch_r = [S.chan("r0"), S.chan("r1")]
                            qn = [sb("qn%d" % i, [64, 512], F32, st2) for i in range(2)]
                            B_qn = [Buf("qn0"), Buf("qn1")]
                            tt4 = [sb("tt%d" % i, [64, 512], F32, st2) for i in range(4)]
                            B_tt4 = [Buf("tt%d" % i) for i in range(4)]
                            rstd_k = sb("rstd_k", [64, 512], F32, st2)
                            B_rstd_k = Buf("rstd_k")
                            nxt = hsrc.load(GORDER[0])
                            for gi_, g in enumerate(GORDER):
                                ht, hb = nxt
                                if gi_ + 1 < NG:
                                    nxt = hsrc.load(GORDER[gi_ + 1])
                                t0, n = gtile(g)
                                nsub = n // 128
                                k = gi_ % 2
                                if g < 16:
                                    S.dma(ch_r[k], cs[k][:], rope_d[:, :, t0:t0 + 512].rearrange("r d t -> d r t"), writes=[B_cs[k]])
                                pv, pvf = bank()
                                proj_tm(ht, hb, O_AV, 64, nsub, pv, pvf)
                                S.op("act", lambda pv=pv, g=g, nsub=nsub: A.copy(
                                    out=Vext[:, g * 4:g * 4 + nsub, 0:64], in_=pv[:, 0:nsub * 64].rearrange("p (s c) -> p s c", c=64)),
                                    reads=[pvf], writes=[B_v])
                                for which, (col0, gcol, dstT, dstB) in enumerate([(O_AQ, H_QN, qT_all, B_q), (O_AK, H_KN, kT_all, B_k)]):
                                    if which == 0 and g == 16 and not ctx_out:
                                        continue
                                    pq, pqf = bank()
                                    proj_fm(ht, hb, col0, 64, n, pq, pqf)
                                    rs_, Brs_ = (rstd, B_rstd) if which == 0 else (rstd_k, B_rstd_k)
                                    tt, B_tt = tt4[which * 2:which * 2 + 2], B_tt4[which * 2:which * 2 + 2]
                                    col_rstd(lambda c, pq=pq, n=n: pq[0:64, 0:n], [pqf], 64, 1, n, 1.0 / 64, rstd=rs_, B_rstd=Brs_)
                                    S.op("dve", lambda pq=pq, n=n, which=which, gcol=gcol: V.scalar_tensor_tensor(
                                        out=qn[which][:, 0:n], in0=pq[0:64, 0:n], scalar=hpf[0:64, li, gcol:gcol + 1], in1=rs_[0:64, 0:n],
                                        op0=ALU.mult, op1=ALU.mult), reads=[pqf, B_hp, Brs_], writes=[B_qn[which]])
                                    if g < 16:
                                        pp, ppf = bank()
                                        S.op("pe", lambda pp=pp, which=which, n=n: P.matmul(pp[0:64, 0:n], lhsT=CF("perm", 64, 0, 64), rhs=qn[which][:, 0:n],
                                                                                            start=True, stop=True), reads=[B_cstf, B_qn[which]], writes=[ppf])
                                        S.op("dve", lambda which=which, k=k, n=n: V.tensor_tensor(out=tt[0][:, 0:n], in0=qn[which][:, 0:n], in1=cs[k][:, 0, 0:n], op=ALU.mult),
                                             reads=[B_qn[which], B_cs[k]], writes=[B_tt[0]])
                                        S.op("dve", lambda pp=pp, k=k, n=n: V.tensor_tensor(out=tt[1][:, 0:n], in0=pp[0:64, 0:n], in1=cs[k][:, 1, 0:n], op=ALU.mult),
                                             reads=[ppf, B_cs[k]], writes=[B_tt[1]])
                                        S.op("dve", lambda dstT=dstT, t0=t0, n=n: V.tensor_tensor(out=dstT[0:64, t0:t0 + n], in0=tt[0][:, 0:n], in1=tt[1][:, 0:n], op=ALU.add),
                                             reads=[B_tt[0], B_tt[1]], writes=[dstB])
                                    else:
                                        S.op("act", lambda dstT=dstT, which=which, t0=t0, n=n: A.copy(out=dstT[0:64, t0:t0 + n], in_=qn[which][:, 0:n]),
                                             reads=[B_qn[which]], writes=[dstB])
                                pool_proj(g, ht, hb)
                                fnet_proj(g, gi_, ht, hb)
                        pT = [sb("pT%d" % i, [128, 512], BF16, st) for i in range(4)]
                        B_pT = [Buf("pT%d" % i) for i in range(4)]
                        pcount = [0]
                        pvs = sb("pvs", [128, 512], F32, st)
                        rec = sb("rec", [64, 512], F32, st)
                        B_pvs, B_rec = Buf("pvs"), Buf("rec")
                        pi_ = 0
                        for qi, g in enumerate(qtiles):
                            t0, n = gtile(g)
                            kbs = list(range(66)) if g < 16 else [64, 65]
                            pacc, paccf = pbank[6 + qi % 2], pbuf[6 + qi % 2]
                            nrot[0] = 6
                            pend = []

                            def issue_s(kb):
                                ps, psf = bank()
                                S.op("pe", lambda: P.matmul(ps[:, 0:n], lhsT=kT_all[:, kb * 128:(kb + 1) * 128], rhs=qT_all[:, t0:t0 + n],
                                                            start=True, stop=True), reads=[B_k, B_q], writes=[psf])
                                k3 = pcount[0] % 4
                                pcount[0] += 1
                                S.op("act", lambda: A.activation(out=pT[k3][:, 0:n], in_=ps[:, 0:n], func=AF.Exp, scale=0.125),
                                     reads=[psf], writes=[B_pT[k3]])
                                pend.append(k3)

                            LOOK = 3
                            for i in range(min(LOOK, len(kbs))):
                                issue_s(kbs[i])
                            for i, kb in enumerate(kbs):
                                if i + LOOK < len(kbs):
                                    issue_s(kbs[i + LOOK])
                                k3 = pend.pop(0)
                                S.op("pe", lambda kb=kb, k3=k3, i=i, nk=len(kbs): P.matmul(
                                    pacc[:, 0:n], lhsT=Vext[:, kb, :], rhs=pT[k3][:, 0:n], start=(i == 0), stop=(i == nk - 1)),
                                    reads=[B_v, B_pT[k3]], writes=[paccf])
                            S.op("dve", lambda pacc=pacc: V.tensor_copy(out=pvs[:, 0:n], in_=pacc[:, 0:n]), reads=[paccf], writes=[B_pvs])
                            pd, pdf = bank()
                            S.op("pe", lambda pd=pd: P.matmul(pd[0:64, 0:n], lhsT=CF("ident", 128, 64, 128), rhs=pvs[:, 0:n], start=True, stop=True),
                                 reads=[B_cstf, B_pvs], writes=[pdf])
                            S.op("dve", lambda pd=pd: V.reciprocal(out=rec[:, 0:n], in_=pd[0:64, 0:n]), reads=[pdf], writes=[B_rec])
                            store_y(128, t0, n, lambda dst, db: S.op("pool", lambda: G.tensor_tensor(
                                out=dst, in0=pvs[0:64, 0:n], in1=rec[:, 0:n], op=ALU.mult), reads=[B_pvs, B_rec], writes=[db]))
                    nrot[0] = 8
                    collective(yxp[2], ygp[2], B_yxp[2], B_ygp[2])
                    if stop == "att":
                        return

                    with Scope() as st:
                        for g in qtiles:
                            t0, n = gtile(g)
                            nsub = n // 128
                            s0, s1 = (0, 63) if g < 16 else (64, 65)
                            pb, pbf = bank()
                            for sub in range(nsub):
                                a = g * 4 + sub
                                terms = []
                                if a > s0:
                                    terms.append((a - 1, 0))
                                terms.append((a, 3 if a == s0 else (4 if a == s1 else 1)))
                                if a < s1:
                                    terms.append((a + 1, 2))
                                for i, (src, blk) in enumerate(terms):
                                    S.op("pe", lambda src=src, blk=blk, i=i, sub=sub, nt=len(terms): P.matmul(
                                        pb[0:64, sub * 128:(sub + 1) * 128], lhsT=u_tok[:, src, :],
                                        rhs=C("band", 128, blk * 128, (blk + 1) * 128), start=(i == 0), stop=(i == nt - 1)),
                                        reads=[B_u, B_cstb], writes=[pbf])
                            k = g % 2
                            S.op("dve", lambda pb=pb, k=k, n=n: V.tensor_copy(out=mT[k][:, 0:n], in_=pb[0:64, 0:n]), reads=[pbf], writes=[B_m[k]])
                            p2, p2f = bank()
                            S.op("pe", lambda p2=p2, k=k, n=n: P.matmul(p2[0:64, 0:n], lhsT=hpb[0:64, li, H_PW:H_PW + 64], rhs=mT[k][:, 0:n],
                                                                        start=True, stop=True), reads=[B_hp, B_m[k]], writes=[p2f])
                            store_y(0, t0, n, lambda dst, db, p2=p2, p2f=p2f, n=n: S.op("act", lambda: A.activation(
                                out=dst, in_=p2[0:64, 0:n], func=AF.Copy, scale=hpf[0:64, li, H_PS:H_PS + 1]),
                                reads=[p2f, B_hp], writes=[db]))
                    collective(yxp[0], ygp[0], B_yxp[0], B_ygp[0])
                    if stop == "pool":
                        return

                    with Scope() as st:
                        Ur = sb("Ur", [64, 64, 128], BF16, st)
                        Ui = sb("Ui", [64, 64, 128], BF16, st)
                        B_U = Buf("U")
                        S.dma(ch_f[0], Ur[:], fsc[0].rearrange("c (h l) -> h c l", l=128), reads=[B_fsc], writes=[B_U])
                        S.dma(ch_f[1], Ui[:], fsc[1].rearrange("c (h l) -> h c l", l=128), reads=[B_fsc], writes=[B_U])
                        Ypr = sb("Ypr", [128, 64, 64], BF16, st)
                        Ypi = sb("Ypi", [128, 64, 64], BF16, st)
                        B_Y = Buf("Y")
                        tw = [sb("tw%d" % i, [128, 4, 64], F32, st) for i in range(4)]
                        B_tw = [Buf("tw%d" % i) for i in range(4)]
                        TcB = CF("Tc", 128, 0, 64).unsqueeze(1).to_broadcast([128, 4, 64])
                        TsB = CF("Ts", 128, 0, 64).unsqueeze(1).to_broadcast([128, 4, 64])
                        for grp in range(16):
                            pb, pbf = bank()
                            for ci in range(4):
                                c = grp * 4 + ci
                                S.op("pe", lambda c=c, ci=ci, pb=pb: P.matmul(pb[:, ci * 128:(ci + 1) * 128], lhsT=Ur[:, c, :], rhs=C("dA1", 64),
                                                                              start=True, stop=False), reads=[B_U, B_cstb], writes=[pbf])
                                S.op("pe", lambda c=c, ci=ci, pb=pb: P.matmul(pb[:, ci * 128:(ci + 1) * 128], lhsT=Ui[:, c, :], rhs=C("dA2", 64),
                                                                              start=False, stop=True), reads=[B_U, B_cstb], writes=[pbf])
                            pv4 = pb[:, :].rearrange("p (c r k) -> p c r k", c=4, r=2)
                            Yr, Yi = pv4[:, :, 0, :], pv4[:, :, 1, :]
                            S.op("dve", lambda Yr=Yr: V.tensor_tensor(out=tw[0][:], in0=Yr, in1=TcB, op=ALU.mult), reads=[pbf, B_cstf], writes=[B_tw[0]])
                            S.op("dve", lambda Yi=Yi: V.tensor_tensor(out=tw[1][:], in0=Yi, in1=TsB, op=ALU.mult), reads=[pbf, B_cstf], writes=[B_tw[1]])
                            S.op("pool", lambda grp=grp: G.tensor_tensor(out=Ypr[:, grp * 4:grp * 4 + 4, :], in0=tw[0][:], in1=tw[1][:], op=ALU.add),
                                 reads=[B_tw[0], B_tw[1]], writes=[B_Y])
                            S.op("dve", lambda Yi=Yi: V.tensor_tensor(out=tw[2][:], in0=Yi, in1=TcB, op=ALU.mult), reads=[pbf, B_cstf], writes=[B_tw[2]])
                            S.op("dve", lambda Yr=Yr: V.tensor_tensor(out=tw[3][:], in0=Yr, in1=TsB, op=ALU.mult), reads=[pbf, B_cstf], writes=[B_tw[3]])
                            S.op("pool", lambda grp=grp: G.tensor_tensor(out=Ypi[:, grp * 4:grp * 4 + 4, :], in0=tw[2][:], in1=tw[3][:], op=ALU.subtract),
                                 reads=[B_tw[2], B_tw[3]], writes=[B_Y])
                        Fst = [sb("Fst%d" % i, [128, 8, 64], BF16, st) for i in range(2)]
                        B_F = [Buf("F0"), Buf("F1")]
                        for blk in range(8):
                            k = blk % 2
                            pb, pbf = bank()
                            S.op("pe", lambda pb=pb, blk=blk: P.matmul(pb[:, :], lhsT=C("C128"), rhs=Ypr[:, blk * 8:blk * 8 + 8, :].rearrange("p c k -> p (c k)"),
                                                                       start=True, stop=False), reads=[B_Y, B_cstb], writes=[pbf])
                            S.op("pe", lambda pb=pb, blk=blk: P.matmul(pb[:, :], lhsT=C("S128"), rhs=Ypi[:, blk * 8:blk * 8 + 8, :].rearrange("p c k -> p (c k)"),
                                                                       start=False, stop=True), reads=[B_Y, B_cstb], writes=[pbf])
                            S.op("act", lambda pb=pb, k=k: A.activation(out=Fst[k][:].rearrange("p c k -> p (c k)"), in_=pb[:, :], func=AF.Copy,
                                                                        scale=float(1.0 / np.sqrt(8192.0 * 64.0))), reads=[pbf], writes=[B_F[k]])
                            S.dma(ch_f[k], yxp[3][blk * 8:blk * 8 + 8, :].rearrange("c (a b) -> a c b", b=64),
                                  Fst[k][:, :, :], reads=[B_F[k]], writes=[B_yxp[3]])
                    collective(yxp[3], ygp[3], B_yxp[3], B_ygp[3])
                    if stop == "fnet":
                        return


                with Scope() as st:
                    qg_all = sb("qg_all", [64, NTOK], BF16, st)
                    kg_all = sb("kg_all", [64, NTOK], BF16, st)
                    v_tok = sb("v_tok", [128, 66, 64], BF16, st)
                    KV_all = sb("KV_all", [64, 66, 64], F32, st)
                    ET_all = sb("ET_all", [64, 66], F32, st)
                    Sprev = sb("Sprev", [64, 66, 64], BF16, st)
                    Srun = sb("Srun", [64, 64], F32, st)
                    negba = sb("negba", [64, 1], F32, st)
                    B_qg, B_kg, B_vt, B_KV, B_ET, B_Sp, B_Sr = (Buf(n_) for n_ in ["qg", "kg", "vt", "KV", "ET", "Sp", "Sr"])

                    S.op("dve", lambda: V.tensor_scalar(out=negba[:], in0=hpf[0:64, li, H_BA:H_BA + 1], scalar1=-1.0, scalar2=None, op0=ALU.mult),
                         reads=[B_hp], writes=[B_hp])
                    with Scope() as st2:
                        hsrc = HStream(st2)
                        f32t = {nm: sb("g_" + nm, [64, 512], F32, st2) for nm in ["gl", "Pc", "W", "b", "dk"]}
                        Bf = {nm: Buf("g_" + nm) for nm in f32t}
                        for new_, old_ in (("eq", "gl"), ("ek", "Pc"), ("ed", "W")):
                            f32t[new_] = f32t[old_]
                            Bf[new_] = Bf[old_]
                        rT = sb("rT", [32, 512], BF16, st2)
                        B_rT = Buf("rT")
                        khT = sb("khT", [64, 512], BF16, st2)
                        B_kh = Buf("khT")
                        khtok = [sb("khtok%d" % i, [128, 64], BF16, st2) for i in range(2)]
                        B_kt = [Buf("kt0"), Buf("kt1")]
                        nxt = hsrc.load(GORDER[0])
                        for gi_, g in enumerate(GORDER):
                            ht, hb = nxt
                            if gi_ + 1 < NG:
                                nxt = hsrc.load(GORDER[gi_ + 1])
                            t0, n = gtile(g)
                            nsub = n // 128
                            pv, pvf = bank()
                            proj_tm(ht, hb, O_GV, 64, nsub, pv, pvf)
                            S.op("act", lambda pv=pv, g=g, nsub=nsub: A.copy(
                                out=v_tok[:, g * 4:g * 4 + nsub, :], in_=pv[:, 0:nsub * 64].rearrange("p (s c) -> p s c", c=64)),
                                reads=[pvf], writes=[B_vt])
                            pr, prf = bank()
                            proj_fm(ht, hb, O_R, 32, n, pr, prf)
                            S.op("dve", lambda pr=pr, n=n: V.tensor_copy(out=rT[:, 0:n], in_=pr[0:32, 0:n]), reads=[prf], writes=[B_rT])
                            pgt, pgtf = bank()
                            S.op("pe", lambda pgt=pgt, n=n: P.matmul(pgt[0:64, 0:n], lhsT=hpb[0:32, li, H_WA:H_WA + 64], rhs=rT[:, 0:n], start=True, stop=True),
                                 reads=[B_hp, B_rT], writes=[pgtf])
                            S.op("act", lambda pgt=pgt, n=n: A.activation(out=f32t["gl"][:, 0:n], in_=pgt[0:64, 0:n], func=AF.Exp, bias=negba[:, :], scale=-1.0),
                                 reads=[pgtf, B_hp], writes=[Bf["gl"]])
                            S.op("act", lambda n=n: A.activation(out=f32t["gl"][:, 0:n], in_=f32t["gl"][:, 0:n], func=AF.Ln, bias=onesf[0:64, 0:1], scale=1.0),
                                 reads=[Bf["gl"], B_cstf], writes=[Bf["gl"]])
                            S.op("dve", lambda n=n: V.tensor_scalar(out=f32t["gl"][:, 0:n], in0=f32t["gl"][:, 0:n], scalar1=-1.0 / 16.0, scalar2=None, op0=ALU.mult),
                                 reads=[Bf["gl"]], writes=[Bf["gl"]])
                            for sub in range(nsub):
                                sl = slice(sub * 128, (sub + 1) * 128)
                                S.op("dve", lambda sl=sl: V.tensor_tensor_scan(out=f32t["Pc"][:, sl], data0=onesf[0:64, :], data1=f32t["gl"][:, sl],
                                                                               initial=0.0, op0=ALU.mult, op1=ALU.add),
                                     reads=[Bf["gl"], B_cstf], writes=[Bf["Pc"]])
                            for sub in range(nsub):
                                sl = slice(sub * 128, (sub + 1) * 128)
                                last = sub * 128 + 127
                                S.op("dve", lambda sl=sl, last=last: V.tensor_scalar(out=f32t["W"][:, sl], in0=f32t["Pc"][:, sl], scalar1=f32t["Pc"][:, last:last + 1],
                                                                                    scalar2=None, op0=ALU.subtract), reads=[Bf["Pc"]], writes=[Bf["W"]])
                            S.op("act", lambda g=g, nsub=nsub, n=n: A.activation(
                                out=ET_all[:, g * 4:g * 4 + nsub], in_=f32t["Pc"][:, 0:n].rearrange("p (s t) -> p s t", t=128)[:, :, 127], func=AF.Exp),
                                reads=[Bf["Pc"]], writes=[B_ET])
                            S.op("dve", lambda n=n: V.tensor_tensor(out=f32t["b"][32:64, 0:n], in0=f32t["gl"][32:64, 0:n], in1=f32t["W"][32:64, 0:n], op=ALU.subtract),
                                 reads=[Bf["gl"], Bf["W"]], writes=[Bf["b"]])
                            S.op("dve", lambda n=n: V.tensor_tensor(out=f32t["dk"][32:64, 0:n], in0=f32t["Pc"][32:64, 0:n], in1=f32t["gl"][32:64, 0:n], op=ALU.subtract),
                                 reads=[Bf["Pc"], Bf["gl"]], writes=[Bf["dk"]])
                            S.op("act", lambda n=n: A.activation(out=f32t["eq"][32:64, 0:n], in_=f32t["b"][32:64, 0:n], func=AF.Exp), reads=[Bf["b"], Bf["dk"]], writes=[Bf["eq"]])
                            S.op("act", lambda n=n: A.activation(out=f32t["eq"][0:32, 0:n], in_=f32t["Pc"][0:32, 0:n], func=AF.Exp), reads=[Bf["Pc"]], writes=[Bf["eq"]])
                            S.op("act", lambda n=n: A.activation(out=f32t["ed"][0:32, 0:n], in_=f32t["W"][0:32, 0:n], func=AF.Exp, scale=-1.0), reads=[Bf["W"]], writes=[Bf["ed"]])
                            S.op("act", lambda n=n: A.activation(out=f32t["ed"][32:64, 0:n], in_=f32t["dk"][32:64, 0:n], func=AF.Exp), reads=[Bf["dk"]], writes=[Bf["ed"]])
                            S.op("act", lambda n=n: A.activation(out=f32t["ek"][0:32, 0:n], in_=f32t["Pc"][0:32, 0:n], func=AF.Exp, scale=-1.0), reads=[Bf["Pc"]], writes=[Bf["ek"]])
                            S.op("act", lambda n=n: A.activation(out=f32t["ek"][32:64, 0:n], in_=f32t["b"][32:64, 0:n], func=AF.Exp, scale=-1.0), reads=[Bf["b"]], writes=[Bf["ek"]])
                            pq, pqf = bank()
                            proj_fm(ht, hb, O_GQ, 64, n, pq, pqf)
                            S.op("dve", lambda pq=pq, t0=t0, n=n: V.scalar_tensor_tensor(out=qg_all[:, t0:t0 + n], in0=pq[0:64, 0:n], scalar=float(32 ** -0.5),
                                                                                        in1=f32t["eq"][:, 0:n], op0=ALU.mult, op1=ALU.mult),
                                 reads=[pqf, Bf["eq"]], writes=[B_qg])
                            pk, pkf = bank()
                            proj_fm(ht, hb, O_GK, 64, n, pk, pkf)
                            S.op("dve", lambda pk=pk, t0=t0, n=n: V.tensor_tensor(out=kg_all[:, t0:t0 + n], in0=pk[0:64, 0:n], in1=f32t["ek"][:, 0:n], op=ALU.mult),
                                 reads=[pkf, Bf["ek"]], writes=[B_kg])
                            S.op("dve", lambda pk=pk, n=n: V.tensor_tensor(out=khT[:, 0:n], in0=pk[0:64, 0:n], in1=f32t["ed"][:, 0:n], op=ALU.mult),
                                 reads=[pkf, Bf["ed"]], writes=[B_kh])
                            pkv, pkvf = bank()
                            for sub in range(nsub):
                                k2 = sub % 2
                                pt, ptf = bank()
                                ptb = pt[:, :].bitcast(BF16)
                                S.op("pe", lambda sub=sub, ptb=ptb: P.transpose(out=ptb[:, 0:64], in_=khT[:, sub * 128:(sub + 1) * 128], identity=C("ident", 64, 0, 64)),
                                     reads=[B_kh, B_cstb], writes=[ptf])
                                S.op("act", lambda ptb=ptb, k2=k2: A.copy(out=khtok[k2][:], in_=ptb[:, 0:64]), reads=[ptf], writes=[B_kt[k2]])
                                S.op("pe", lambda sub=sub, k2=k2, pkv=pkv, g=g: P.matmul(pkv[0:64, sub * 64:(sub + 1) * 64], lhsT=khtok[k2][:], rhs=v_tok[:, g * 4 + sub, :],
                                                                                       start=True, stop=True), reads=[B_kt[k2], B_vt], writes=[pkvf])
                            S.op("dve", lambda pkv=pkv, g=g, nsub=nsub: V.tensor_copy(out=KV_all[:, g * 4:g * 4 + nsub, :],
                                                                                      in_=pkv[0:64, 0:nsub * 64].rearrange("p (s c) -> p s c", c=64)),
                                 reads=[pkvf], writes=[B_KV])
                    S.op("dve", lambda: V.memset(Srun[:], 0.0), writes=[B_Sr])
                    order_f = [64, 65] + list(range(64))
                    order_b = [65, 64] + list(range(63, -1, -1))
                    for s_ in range(66):
                        for (lo, hi_, order) in ((0, 32, order_f), (32, 64, order_b)):
                            cidx = order[s_]
                            S.op("dve", lambda lo=lo, hi_=hi_, cidx=cidx: V.tensor_copy(out=Sprev[lo:hi_, cidx, :], in_=Srun[lo:hi_, :]),
                                 reads=[B_Sr], writes=[B_Sp])
                            S.op("dve", lambda lo=lo, hi_=hi_, cidx=cidx: V.scalar_tensor_tensor(
                                out=Srun[lo:hi_, :], in0=Srun[lo:hi_, :], scalar=ET_all[lo:hi_, cidx:cidx + 1], in1=KV_all[lo:hi_, cidx, :],
                                op0=ALU.mult, op1=ALU.add), reads=[B_Sr, B_ET, B_KV], writes=[B_Sr])
                    with Scope() as st2:
                        hsrc = HStream(st2)
                        qFt = [sb("qFt%d" % i, [64, 128], BF16, st2) for i in range(2)]
                        qBt = [sb("qBt%d" % i, [64, 128], BF16, st2) for i in range(2)]
                        B_qF = [Buf("qF0"), Buf("qF1")]
                        B_qB = [Buf("qB0"), Buf("qB1")]
                        for i in range(2):
                            S.op("pool", lambda i=i: G.memset(qFt[i][:], 0.0), writes=[B_qF[i]])
                            S.op("pool", lambda i=i: G.memset(qBt[i][:], 0.0), writes=[B_qB[i]])
                        Af = [sb("Af%d" % i, [128, 128], BF16, st2) for i in range(2)]
                        Ab = [sb("Ab%d" % i, [128, 128], BF16, st2) for i in range(2)]
                        B_Af = [Buf("Af0"), Buf("Af1")]
                        B_Ab = [Buf("Ab0"), Buf("Ab1")]
                        ogs = sb("ogs", [64, 512], F32, st2)
                        yn = sb("yn", [64, 512], F32, st2)
                        B_og, B_yn = Buf("ogs"), Buf("yn")
                        nxt = hsrc.load(qtiles[0])
                        for gi_, g in enumerate(qtiles):
                            ht, hb = nxt
                            if gi_ + 1 < len(qtiles):
                                nxt = hsrc.load(qtiles[gi_ + 1])
                            t0, n = gtile(g)
                            nsub = n // 128
                            pog, pogf = bank()
                            proj_fm(ht, hb, O_OG, 64, n, pog, pogf)
                            S.op("act", lambda pog=pog, n=n: A.activation(out=ogs[:, 0:n], in_=pog[0:64, 0:n], func=AF.Silu), reads=[pogf], writes=[B_og])
                            po, pof = bank()
                            for sub in range(nsub):
                                cidx = g * 4 + sub
                                tk = slice(t0 + sub * 128, t0 + (sub + 1) * 128)
                                k2 = sub % 2
                                pa, paf = bank()
                                S.op("act", lambda k2=k2, tk=tk: A.copy(out=qFt[k2][0:32, :], in_=qg_all[0:32, tk]), reads=[B_qg], writes=[B_qF[k2]])
                                S.op("act", lambda k2=k2, tk=tk: A.copy(out=qBt[k2][32:64, :], in_=qg_all[32:64, tk]), reads=[B_qg], writes=[B_qB[k2]])
                                S.op("pe", lambda pa=pa, tk=tk, k2=k2: P.matmul(pa[:, 0:128], lhsT=kg_all[:, tk], rhs=qFt[k2][:], start=True, stop=True),
                                     reads=[B_kg, B_qF[k2]], writes=[paf])
                                S.op("pe", lambda pa=pa, tk=tk, k2=k2: P.matmul(pa[:, 128:256], lhsT=kg_all[:, tk], rhs=qBt[k2][:], start=True, stop=True),
                                     reads=[B_kg, B_qB[k2]], writes=[paf])
                                S.op("dve", lambda pa=pa, k2=k2: V.tensor_tensor(out=Af[k2][:], in0=pa[:, 0:128], in1=C("maskF"), op=ALU.mult),
                                     reads=[paf, B_cstb], writes=[B_Af[k2]])
                                S.op("dve", lambda pa=pa, k2=k2: V.tensor_tensor(out=Ab[k2][:], in0=pa[:, 128:256], in1=C("maskB"), op=ALU.mult),
                                     reads=[paf, B_cstb], writes=[B_Ab[k2]])
                                osl = po[0:64, sub * 128:(sub + 1) * 128]
                                S.op("pe", lambda osl=osl, cidx=cidx, k2=k2: P.matmul(osl, lhsT=v_tok[:, cidx, :], rhs=Af[k2][:], start=True, stop=False),
                                     reads=[B_vt, B_Af[k2]], writes=[pof])
                                S.op("pe", lambda osl=osl, cidx=cidx, k2=k2: P.matmul(osl, lhsT=v_tok[:, cidx, :], rhs=Ab[k2][:], start=False, stop=False),
                                     reads=[B_vt, B_Ab[k2]], writes=[pof])
                                S.op("pe", lambda osl=osl, cidx=cidx, tk=tk: P.matmul(osl, lhsT=Sprev[:, cidx, :], rhs=qg_all[:, tk], start=False, stop=True),
                                     reads=[B_Sp, B_qg], writes=[pof])
                            col_rstd(lambda c, po=po, n=n: po[0:64, 0:n], [pof], 64, 1, n, 1.0 / 64)
                            S.op("dve", lambda po=po, n=n: V.scalar_tensor_tensor(out=yn[:, 0:n], in0=po[0:64, 0:n], scalar=hpf[0:64, li, H_GN:H_GN + 1],
                                                                                in1=rstd[0:64, 0:n], op0=ALU.mult, op1=ALU.mult),
                                 reads=[pof, B_hp, B_rstd], writes=[B_yn])
                            store_y(64, t0, n, lambda dst, db, n=n: S.op("pool", lambda: G.tensor_tensor(
                                out=dst, in0=yn[:, 0:n], in1=ogs[:, 0:n], op=ALU.mult), reads=[B_yn, B_og], writes=[db]))
                collective(yxp[1], ygp[1], B_yxp[1], B_ygp[1])
                if stop == "gla":
                    return


        oh = sb("oh", [128, 4], F32)
        selI = sb("selI", [128, 4, 128], BF16)
        B_sel = Buf("sel")
        S.dma(ch_c[5], oh[:], oh_d, writes=[B_sel])
        for d_ in range(4):
            S.op("dve", lambda d_=d_: V.tensor_scalar(out=selI[:, d_, :], in0=CF("ident", 128, 0, 128), scalar1=oh[:, d_:d_ + 1],
                                                   scalar2=None, op0=ALU.mult), reads=[B_sel, B_cstf], writes=[B_sel])

        def stage_c(li, tiles):
            with Scope() as st:
                wob = sb("wob", [128, 8, D], BF16, st)
                fwb = sb("fwb", [128, 2, 256], BF16, st)
                B_wo, B_fw = Buf("wob"), Buf("fwb")
                with Scope() as s2:
                    wos = sb("wos", [128, 8, D], F32, s2)
                    fws = sb("fws", [128, 2, 256], F32, s2)
                    B_wos, B_fws = Buf("wos"), Buf("fws")
                    S.dma(ch_c[4], wos[:], wout_d[li].rearrange("(k p) n -> p k n", p=128), writes=[B_wos])
                    S.dma(ch_c[5], fws[:], fw_d[li].rearrange("(k p) n -> p k n", p=128), writes=[B_fws])
                    S.op("act", lambda: A.copy(out=wob[:, 0:4, :], in_=wos[:, 0:4, :]), reads=[B_wos], writes=[B_wo])
                    S.op("dve", lambda: V.tensor_copy(out=wob[:, 4:8, :], in_=wos[:, 4:8, :]), reads=[B_wos], writes=[B_wo])
                    S.op("act", lambda: A.copy(out=fwb[:], in_=fws[:]), reads=[B_fws], writes=[B_fw])
                yc_all = [sb("yc%d" % i, [128, 8, 512], BF16, st) for i in range(8)]
                B_yc_all = [Buf("yc%d" % i) for i in range(8)]
                ch_y_all = [S.chan("y%d" % i) for i in range(8)]
                ysel = sb("ysel", [128, 8, 512], BF16, st)
                B_ysel = Buf("ysel")
                yf = sb("yf", [128, 2, 512], BF16, st)
                B_yf = Buf("yf")
                for tci, ti in enumerate(tiles):
                    t0, n, kind = TT[ti]
                    yc = yc_all[(tci % 2) * 4:(tci % 2) * 4 + 4]
                    B_yc = B_yc_all[(tci % 2) * 4:(tci % 2) * 4 + 4]
                    ch_y = ch_y_all[(tci % 2) * 4:(tci % 2) * 4 + 4]
                    for d_ in range(4):
                        for m_ in range(4):
                            for qp in range(2):
                                kc = m_ * 2 + qp
                                if kind == 0:
                                    S.dma(ch_y[d_], yc[d_][:, kc, 0:n], ygp[m_][qp * 128:(qp + 1) * 128, d_ * 2048 + t0:d_ * 2048 + t0 + n],
                                          reads=[B_ygp[m_]], writes=[B_yc[d_]], nowait=(kc > 0))
                                else:
                                    for ql in range(2):
                                        r0_ = (2 * qp + ql) * 256 + m_ * 64
                                        S.dma(ch_y[d_], yc[d_][ql * 64:(ql + 1) * 64, kc, 0:n], ygp[4][r0_:r0_ + 64, d_ * 64:d_ * 64 + n],
                                              reads=[B_ygp[4]], writes=[B_yc[d_]], nowait=(kc > 0 or ql > 0))
                    for c in range(8):
                        pb, pbf = bank()
                        for d_ in range(4):
                            S.op("pe", lambda pb=pb, d_=d_, c=c: P.matmul(pb[:, 0:n], lhsT=selI[:, d_, :], rhs=yc[d_][:, c, 0:n],
                                                                          start=(d_ == 0), stop=(d_ == 3)), reads=[B_sel, B_yc[d_]], writes=[pbf])
                        if c % 2 == 0:
                            S.op("act", lambda pb=pb, c=c: A.copy(out=ysel[:, c, 0:n], in_=pb[:, 0:n]), reads=[pbf], writes=[B_ysel])
                        else:
                            S.op("dve", lambda pb=pb, c=c: V.tensor_copy(out=ysel[:, c, 0:n], in_=pb[:, 0:n]), reads=[pbf], writes=[B_ysel])
                    for dc in range(2):
                        pb, pbf = bank()
                        for kc in range(2):
                            S.op("pe", lambda pb=pb, kc=kc, dc=dc: P.matmul(pb[:, 0:n], lhsT=fwb[:, kc, dc * 128:(dc + 1) * 128], rhs=ysel[:, 6 + kc, 0:n],
                                                                            start=(kc == 0), stop=(kc == 1)), reads=[B_fw, B_ysel], writes=[pbf])
                        S.op("act", lambda pb=pb, dc=dc: A.copy(out=yf[:, dc, 0:n], in_=pb[:, 0:n]), reads=[pbf], writes=[B_yf])
                    for c in range(8):
                        po, pof = bank()
                        for kc in range(8):
                            rhs = ysel[:, kc, 0:n] if kc < 6 else yf[:, kc - 6, 0:n]
                            S.op("pe", lambda po=po, kc=kc, c=c, rhs=rhs: P.matmul(po[:, 0:n], lhsT=wob[:, kc, c * 128:(c + 1) * 128], rhs=rhs,
                                                                                  start=(kc == 0), stop=(kc == 7)), reads=[B_wo, B_ysel, B_yf], writes=[pof])
                        S.op("dve", lambda po=po, c=c: V.scalar_tensor_tensor(
                            out=xT[:, c, t0:t0 + n], in0=po[:, 0:n], scalar=modG[:, li, 1, c, kind:kind + 1],
                            in1=xT[:, c, t0:t0 + n], op0=ALU.mult, op1=ALU.add), reads=[pof, B_mod, B_x[ti][c]], writes=[B_x[ti][c]])

        done = False
        for li in range(nlayers):
            lastl = (li == nlayers - 1)
            with Scope() as stA:
                hT = sb("hT", [128, 8, NT], BF16, stA)
                B_h = [Buf("h%d" % t) for t in range(len(TT))]
                half_ffn(li, 0, range(5), hT, B_h, stA)
                if li == 0:
                    dump("x_ffn1", xT[:], sum(B_x, []))
                for ti in range(5):
                    t0, n, kind = TT[ti]
                    adaln(li, 1, ti, lambda c, t0=t0, n=n: hT[:, c, t0:t0 + n], [B_h[ti]])
                    S.dma(ch_st[ti % 2], hxp[ti].rearrange("(c p) n -> p c n", p=128), hT[:, :, t0:t0 + n],
                          reads=[B_h[ti]], writes=[B_hxp[ti]])
                    collective(hxp[ti], hgp[ti], B_hxp[ti], B_hgp[ti])
            if li == 0:
                dump("hg0", hgp[0], [B_hgp[0]])
            if stop == "A":
                break
            stage_b(li)
            if not lastl or nlayers == 1:
                collective(yxp[4], ygp[4], B_yxp[4], B_ygp[4])
            if stop is not None and stop != "C":
                break
            ctiles = range(5) if (not lastl or nlayers == 1) else range(4)
            stage_c(li, ctiles)
            if li == 0:
                dump("x_mix", xT[:], sum(B_x, []))
            if stop == "C":
                break
            with Scope() as stA:
                hT = sb("hT", [128, 8, NT], BF16, stA)
                B_h = [Buf("h%d" % t) for t in range(len(TT))]
                half_ffn(li, 1, ctiles, hT, B_h, stA)
            if li == 0:
                dump("x_out", xT[:], sum(B_x, []))
            done = lastl

        B_out = Buf("yout")
        if done:
            with Scope() as st:
                fin = sb("fin", [128, 8, 512], F32, st)
                B_fin = Buf("fin")
                ot = [sb("ot%d" % i, [128, D], F32, st) for i in range(2)]
                B_ot = [Buf("ot0"), Buf("ot1")]
                for ti in range(4):
                    t0, n, kind = TT[ti]
                    col_rstd(lambda c: xT[:, c, t0:t0 + n], list(B_x[ti]), 128, 8, n, 1.0 / D)
                    for c in range(8):
                        S.op("dve", lambda c=c: V.scalar_tensor_tensor(
                            out=fin[:, c, 0:n], in0=xT[:, c, t0:t0 + n], scalar=vecs[:, V_FN + c:V_FN + c + 1], in1=rstd[:, 0:n],
                            op0=ALU.mult, op1=ALU.mult), reads=[B_x[ti][c], B_vecs, B_rstd], writes=[B_fin])
                    for sub in range(4):
                        k = (ti * 4 + sub) % 2
                        for half in range(2):
                            pb, pbf = bank()
                            for c4 in range(4):
                                c = half * 4 + c4
                                S.op("pe", lambda c=c, c4=c4, pb=pb, sub=sub: P.transpose(
                                    out=pb[:, c4 * 128:(c4 + 1) * 128], in_=fin[:, c, sub * 128:(sub + 1) * 128],
                                    identity=CF("ident", 128, 0, 128)), reads=[B_fin, B_cstf], writes=[pbf])
                            if half == 0:
                                S.op("dve", lambda pb=pb, k=k: V.tensor_copy(out=ot[k][:, 0:512], in_=pb[:, :]), reads=[pbf], writes=[B_ot[k]])
                            else:
                                S.op("act", lambda pb=pb, k=k: A.copy(out=ot[k][:, 512:1024], in_=pb[:, :]), reads=[pbf], writes=[B_ot[k]])
                        r0 = t0 + sub * 128
                        S.dma(ch_st[2 + k], yout[r0:r0 + 128, :], ot[k][:, :], reads=[B_ot[k]], writes=[B_out])

        S.barrier()
        print("instructions", S.n_ins, "waits", S.n_wait, "chans", S.nchan)
    return nc


def _consts(r):
    c = np.zeros((128, NCST), np.float32)

    def put(name, arr, parts=None):
        o, w = _c[name]
        arr = np.asarray(arr, np.float32)
        c[0:arr.shape[0], o:o + arr.shape[1]] = arr

    put("ident", np.eye(128))
    put("ones", np.ones((128, 128)))
    jj, ii = np.meshgrid(np.arange(128), np.arange(128), indexing="ij")
    put("maskF", (jj <= ii))
    put("maskB", (jj >= ii))
    w = (2, 4, 8, 16)[r]
    Nn = 512
    t = np.arange(Nn)
    lo = np.clip(t - w // 2, 0, Nn)
    hi = np.clip(t + w - w // 2, 0, Nn)
    s = np.arange(Nn)[:, None]
    band = ((s >= lo[None, :]) & (s < hi[None, :])) / (hi - lo)[None, :] - np.eye(Nn)
    blocks = [band[0:128, 128:256], band[128:256, 128:256], band[256:384, 128:256], band[0:128, 0:128],
              band[384:512, 384:512]]
    put("band", np.concatenate(blocks, axis=1))
    a64 = 2 * np.pi * np.outer(np.arange(64), np.arange(64)) / 64
    a128 = 2 * np.pi * np.outer(np.arange(128), np.arange(128)) / 128
    put("dA1", np.concatenate([np.cos(a64), -np.sin(a64)], 1))
    put("dA2", np.concatenate([np.sin(a64), np.cos(a64)], 1))
    put("C128", np.cos(a128))
    put("S128", np.sin(a128))
    tw = 2 * np.pi * np.outer(np.arange(128), np.arange(64)) / 8192
    put("Tc", np.cos(tw))
    put("Ts", np.sin(tw))
    put("chC", np.cos(a64))
    put("chS", -np.sin(a64))
    pm = np.zeros((64, 64))
    for d in range(64):
        sec, q = divmod(d, 32)
        partner = sec * 32 + (q + 16) % 32
        pm[partner, d] = 1.0
    put("perm", pm)
    a256 = 2 * np.pi * np.outer(np.arange(256), np.arange(256)) / 256
    put("C256", np.cos(a256).reshape(2, 128, 256).transpose(1, 0, 2).reshape(128, 512))
    put("S256", np.sin(a256).reshape(2, 128, 256).transpose(1, 0, 2).reshape(128, 512))
    return c


def _rope():
    freqs = 10000.0 ** (-np.arange(16, dtype=np.float32) / 16)
    tok = np.arange(NLAT)
    row = (tok // 64).astype(np.float32)
    col = (tok % 64).astype(np.float32)
    out = np.zeros((2, 64, NLAT), np.float32)
    for d in range(64):
        sec, q = divmod(d, 32)
        pos = row if sec == 0 else col
        ang = pos * freqs[q % 16]
        out[0, d] = np.cos(ang)
        out[1, d] = np.sin(ang) * (-1.0 if q < 16 else 1.0)
    return out


def _fm(v):
    return np.ascontiguousarray(np.asarray(v, np.float32).reshape(8, 128).T)


def make_inputs(x, c, ctx, c_ctx, w_mod, b_mod, norm_g, ffn_wg, ffn_wu, ffn_wd, w_in, w_out,
                pool_w, pool_scale, gla_wa, gla_ba, gla_norm, att_qnorm, att_knorm, fnet_w, final_norm):
    f = lambda a: np.ascontiguousarray(np.asarray(a, np.float32))
    x, c, ctx, c_ctx, w_mod, b_mod, norm_g = map(f, (x, c, ctx, c_ctx, w_mod, b_mod, norm_g))
    ffn_wg, ffn_wu, ffn_wd, w_in, w_out = map(f, (ffn_wg, ffn_wu, ffn_wd, w_in, w_out))
    pool_w, pool_scale, gla_wa, gla_ba, gla_norm = map(f, (pool_w, pool_scale, gla_wa, gla_ba, gla_norm))
    att_qnorm, att_knorm, fnet_w, final_norm = map(f, (att_qnorm, att_knorm, fnet_w, final_norm))
    rope = _rope()
    maps = []
    for core in range(8):
        b, r = divmod(core, 4)
        xin = np.concatenate([x[b, r * TL:(r + 1) * TL], ctx[b, r * TC:(r + 1) * TC], np.zeros((64, D), np.float32)], 0)
        vecs = np.zeros((128, NVEC), np.float32)
        vecs[:, V_C:V_C + 16] = np.stack([_fm(c[b]), _fm(c_ctx)], -1).reshape(128, 16)
        for li in range(2):
            vecs[:, V_BMOD + li * 18:V_BMOD + (li + 1) * 18] = b_mod[li, r * 2304:(r + 1) * 2304].reshape(18, 128).T
            for n3 in range(3):
                vecs[:, V_NG + (li * 3 + n3) * 8:V_NG + (li * 3 + n3) * 8 + 8] = _fm(norm_g[li, n3])
        vecs[:, V_FN:V_FN + 8] = _fm(final_norm)
        kv = r // 2
        cols = np.concatenate([
            np.arange(r * 64, r * 64 + 64),
            256 + r * 32 + np.arange(32), 256 + r * 32 + np.arange(32),
            384 + r * 32 + np.arange(32), 384 + r * 32 + np.arange(32),
            768 + np.arange(32),
            800 + r * 64 + np.arange(64),
            512 + r * 64 + np.arange(64),
            1056 + r * 64 + np.arange(64),
            1312 + kv * 64 + np.arange(64),
            1440 + kv * 64 + np.arange(64),
            1568 + r * 64 + np.arange(64)])
        assert cols.size == NHC
        hp = np.zeros((2, 128, NHP), np.float32)
        for li in range(2):
            hp[li, 0:64, H_PW:H_PW + 64] = pool_w[li, r]
            hp[li, 0:64, H_PS] = pool_scale[li, r * 64:(r + 1) * 64]
            hp[li, 0:16, H_WA:H_WA + 32] = gla_wa[li, 0][:, r * 32:(r + 1) * 32]
            hp[li, 16:32, H_WA + 32:H_WA + 64] = gla_wa[li, 1][:, r * 32:(r + 1) * 32]
            hp[li, 0:32, H_BA] = gla_ba[li, 0, r * 32:(r + 1) * 32]
            hp[li, 32:64, H_BA] = gla_ba[li, 1, r * 32:(r + 1) * 32]
            hp[li, 0:64, H_GN] = gla_norm[li]
            hp[li, 0:64, H_QN] = att_qnorm[li]
            hp[li, 0:64, H_KN] = att_knorm[li]
        maps.append({
            "oh": np.tile(np.eye(4, dtype=np.float32)[r][None, :], (128, 1)),
            "xin": np.ascontiguousarray(xin), "vecs": vecs, "cst": _consts(r), "rope": rope, "hp": hp,
            "w_mod_s": np.ascontiguousarray(w_mod[:, :, r * 2304:(r + 1) * 2304]), "ffn_wg": ffn_wg, "ffn_wu": ffn_wu, "ffn_wd": ffn_wd,
            "w_in_h": np.ascontiguousarray(w_in[:, :, cols]), "w_out": w_out, "fnet_w": fnet_w,
        })
    return maps


_NC = None


def kernel(**inputs):
    global _NC
    if _NC is None:
        _NC = build()
    maps = make_inputs(**inputs)
    res = run_bass_kernel_spmd(_NC, maps, core_ids=list(range(8)))
    out = np.zeros((2, NLAT, D), np.float32)
    for core in range(8):
        b, r = divmod(core, 4)
        out[b, r * TL:(r + 1) * TL] = res.results[core]["yout"]
    return out
```

```python
import numpy as np
from contextlib import ExitStack
import concourse.bass as bass
import concourse.mybir as mybir
from concourse.bass_utils import run_bass_kernel_spmd

F32 = mybir.dt.float32
BF16 = mybir.dt.bfloat16
F32R = mybir.dt.float32r
AF = mybir.ActivationFunctionType
ALU = mybir.AluOpType

D = 1024
NLAT = 8192
NCTX = 256
NTOK = NLAT + NCTX
TL = 2048
TC = 64
NT = TL + TC
DFF = 2816
NJ = DFF // 128
EPS = 1e-6
NHC = 608
TT = [(0, 512, 0), (512, 512, 0), (1024, 512, 0), (1536, 512, 0), (2048, 64, 1)]

_c = {}
_o = 0
for _n, _w in [("ident", 128), ("ones", 128), ("maskF", 128), ("maskB", 128), ("band", 640),
               ("dA1", 128), ("dA2", 128), ("C128", 128), ("S128", 128), ("Tc", 64), ("Ts", 64),
               ("chC", 64), ("chS", 64), ("perm", 64), ("C256", 512), ("S256", 512)]:
    _c[_n] = (_o, _w)
    _o += _w
NCST = _o
V_C = 0
V_BMOD = 32
V_NG = 160
V_FN = 208
NVEC = 216
H_PW = 0
H_PS = 64
H_WA = 65
H_BA = 129
H_GN = 130
H_QN = 131
H_KN = 132
NHP = 133


class Buf:
    __slots__ = ("name", "w", "r")

    def __init__(self, name=""):
        self.name = name
        self.w = None
        self.r = []


class Chan:
    __slots__ = ("sem", "count", "key")

    def __init__(self, sem, key):
        self.sem = sem
        self.count = 0
        self.key = key


class Sched:
    COMPUTE = ("pe", "act", "dve", "pool")

    def __init__(self, nc, es):
        self.nc = nc
        self.es = es
        self.eng = {"pe": nc.tensor, "act": nc.scalar, "dve": nc.vector, "pool": nc.gpsimd, "sp": nc.sync}
        self.sem = {}
        self.cnt = {}
        for e in self.COMPUTE:
            self.sem[e] = es.enter_context(nc.semaphore("s_" + e))
            self.cnt[e] = 0
        self.waited = {e: {} for e in self.eng}
        self.nchan = 0
        self.chans = []
        self.extra = []
        self.need_barrier = False
        self.n_wait = 0
        self.n_ins = 0

    def chan(self, name):
        s = self.es.enter_context(self.nc.semaphore("c_%s_%d" % (name, self.nchan)))
        self.nchan += 1
        c = Chan(s, "c%d" % self.nchan)
        self.chans.append(c)
        return c

    def barrier(self):
        for e in self.eng:
            for e2 in self.COMPUTE:
                self._wait(e, (e2, self.sem[e2], self.cnt[e2]))
            for c in self.chans:
                self._wait(e, (c.key, c.sem, c.count))

    def _wait(self, e, tok):
        if tok is None:
            return
        key, sem, val = tok
        if val <= 0 or self.waited[e].get(key, 0) >= val:
            return
        self.eng[e].wait_ge(sem, val)
        self.waited[e][key] = val
        self.n_wait += 1

    def _deps(self, e, reads, writes, skip=None):
        for b in reads:
            if b.w is not None and b.w[0] != skip:
                self._wait(e, b.w)
        for b in writes:
            if b.w is not None and b.w[0] != skip:
                self._wait(e, b.w)
            for t in b.r:
                if t[0] != skip:
                    self._wait(e, t)

    def _mark(self, tok, reads, writes):
        for b in reads:
            b.r.append(tok)
            if len(b.r) > 16:
                d = {}
                for t in b.r:
                    if t[0] not in d or d[t[0]][2] < t[2]:
                        d[t[0]] = t
                b.r = list(d.values())
        for b in writes:
            b.w = tok
            b.r = []

    def op(self, e, fn, reads=(), writes=()):
        self._deps(e, reads, writes, skip="pe" if e == "pe" else None)
        ins = fn()
        self.cnt[e] += 1
        ins.then_inc(self.sem[e], 1)
        tok = (e, self.sem[e], self.cnt[e])
        self._mark(tok, reads, writes)
        self.n_ins += 1
        return tok

    def dma(self, ch, out, in_, reads=(), writes=(), q="sp", nowait=False, **kw):
        if not nowait:
            self._wait(q, (ch.key, ch.sem, ch.count))
        self._deps(q, reads, writes, skip=ch.key if nowait else None)
        ins = self.eng[q].dma_start(out=out, in_=in_, **kw)
        ch.count += 16
        ins.then_inc(ch.sem, 16)
        tok = (ch.key, ch.sem, ch.count)
        self._mark(tok, reads, writes)
        self.n_ins += 1
        return tok

    def finish(self, q, bufs):
        for b in bufs:
            self._wait(q, b.w)
            for t in b.r:
                self._wait(q, t)


def build(nlayers=2, stop=None, dbg=()):
    nc = bass.Bass("TRN2", target_bir_lowering=False)

    def din(name, shape, dt=F32):
        return nc.dram_tensor(name, list(shape), dt, kind="ExternalInput").ap()

    xin = din("xin", [NT + 64, D])
    vecs_d = din("vecs", [128, NVEC])
    cst_d = din("cst", [128, NCST])
    rope_d = din("rope", [2, 64, NLAT])
    hp_d = din("hp", [2, 128, NHP])
    wmod_d = din("w_mod_s", [2, D, 2304])
    wg_d = din("ffn_wg", [2, 2, D, DFF])
    wu_d = din("ffn_wu", [2, 2, D, DFF])
    wd_d = din("ffn_wd", [2, 2, DFF, D])
    winh_d = din("w_in_h", [2, D, NHC])
    wout_d = din("w_out", [2, D, D])
    fw_d = din("fnet_w", [2, 256, 256])
    yout = nc.dram_tensor("yout", [TL, D], F32, kind="ExternalOutput").ap()
    dbg_out = {}
    for name, shape, dt in dbg:
        dbg_out[name] = nc.dram_tensor("dbg_" + name, list(shape), dt, kind="ExternalOutput").ap()
    HN = [512, 512, 512, 512, 64]
    hxp = [nc.dram_tensor("hx%d" % t, [D, HN[t]], BF16).ap() for t in range(5)]
    hgp = [nc.dram_tensor("hg%d" % t, [4 * D, HN[t]], BF16).ap() for t in range(5)]
    yxp = [nc.dram_tensor("yx%d" % t, [64, NLAT], BF16).ap() for t in range(4)] + [nc.dram_tensor("yx4", [256, NCTX], BF16).ap()]
    ygp = [nc.dram_tensor("yg%d" % t, [4 * 64, NLAT], BF16).ap() for t in range(4)] + [nc.dram_tensor("yg4", [4 * 256, NCTX], BF16).ap()]
    fsc = nc.dram_tensor("fsc", [2, 64, NLAT], BF16).ap()
    B_hxp = [Buf("hx%d" % t) for t in range(5)]
    B_hgp = [Buf("hg%d" % t) for t in range(5)]
    B_yxp = [Buf("yx%d" % t) for t in range(5)]
    B_ygp = [Buf("yg%d" % t) for t in range(5)]
    B_fsc = Buf("fsc")
    oh_d = din("oh", [128, 4])

    es = ExitStack()

    class Scope(ExitStack):
        def __exit__(self, *a):
            S.need_barrier = True
            return super().__exit__(*a)

    with es:
        S = Sched(nc, es)
        cc_sem = es.enter_context(nc.semaphore("cc"))
        cc_n = [0]

        uid = [0]

        def sb(name, shape, dt, stack=es):
            uid[0] += 1
            if S.need_barrier:
                S.barrier()
                S.need_barrier = False
            return stack.enter_context(nc.sbuf_tensor("s%d_%s" % (uid[0], name), list(shape), dt))

        V, A, P, G = nc.vector, nc.scalar, nc.tensor, nc.gpsimd

        pbank = [es.enter_context(nc.psum_tensor("pb%d" % i, [128, 512], F32)) for i in range(8)]
        pbuf = [Buf("pb%d" % i) for i in range(8)]
        prr = [0]
        nrot = [8]

        def bank():
            i = prr[0] % nrot[0]
            prr[0] += 1
            return pbank[i], pbuf[i]

        xT = sb("xT", [128, 8, NT], F32)
        B_x = [[Buf("x%d_%d" % (t, c)) for c in range(8)] for t in range(len(TT))]
        vecs = sb("vecs", [128, NVEC], F32)
        B_vecs = Buf("vecs")
        cstb = sb("cstb", [128, NCST], BF16)
        B_cstb = Buf("cstb")
        cstf = sb("cstf", [128, 320], F32)
        B_cstf = Buf("cstf")
        onesf = sb("onesf", [128, 128], F32)
        hpf = sb("hpf", [128, 2, NHP], F32)
        hpb = sb("hpb", [128, 2, NHP], BF16)
        B_hp = Buf("hp")
        modT = sb("modT", [128, 2, 72, 2], F32)
        B_mod = Buf("mod")
        epsb = sb("epsb", [128, 1], F32)
        B_eps = Buf("eps")
        modA = sb("modA", [128, 2, 3, 8, 2], F32)
        modG = sb("modG", [128, 2, 3, 8, 2], F32)
        sq = sb("sq", [128, 2, 512], BF16)
        B_sq = [Buf("sq0"), Buf("sq1")]
        rstd = sb("rstd", [128, 512], F32)
        B_rstd = Buf("rstd")
        hs = sb("hs", [128, 2, 512], F32)
        B_hs = [Buf("hs0"), Buf("hs1")]
        ch_c = [S.chan("k%d" % i) for i in range(6)]
        ch_st = [S.chan("st%d" % i) for i in range(4)]
        CFO = {"ident": 0, "Tc": 128, "Ts": 192, "perm": 256}

        S.dma(ch_c[0], vecs[:], vecs_d, writes=[B_vecs])
        S.dma(ch_c[2], hpf[:], hp_d.rearrange("l p n -> p l n"), writes=[B_hp])
        for i, nm in enumerate(["ident", "Tc", "Ts", "perm"]):
            o, w = _c[nm]
            S.dma(ch_c[3], cstf[:, CFO[nm]:CFO[nm] + w], cst_d[:, o:o + w], writes=[B_cstf])
        with Scope() as st:
            cst = sb("cst", [128, NCST], F32, st)
            B_cst = Buf("cst")
            S.dma(ch_c[1], cst[:], cst_d, writes=[B_cst])
            S.op("dve", lambda: V.tensor_copy(out=cstb[:], in_=cst[:]), reads=[B_cst], writes=[B_cstb])
        S.op("dve", lambda: V.tensor_copy(out=hpb[:], in_=hpf[:]), reads=[B_hp], writes=[B_hp])
        S.op("pool", lambda: G.memset(epsb[:], EPS), writes=[B_eps])
        S.op("pool", lambda: G.memset(onesf[:], 1.0), writes=[B_cstf])

        def C(name, parts=128, lo=0, hi=None):
            o, w = _c[name]
            hi = w if hi is None else hi
            return cstb[0:parts, o + lo:o + hi]

        def CF(name, parts=128, lo=0, hi=None):
            o = CFO[name]
            return cstf[0:parts, o + lo:o + hi]

        with Scope() as st:
            xs = [sb("xs%d" % i, [128, D], F32, st) for i in range(2)]
            B_xs = [Buf("xs0"), Buf("xs1")]
            ch_x = [S.chan("x0"), S.chan("x1")]
            nblk = (NT + 127) // 128
            for bi in range(nblk):
                r0 = bi * 128
                n = min(128, NT - r0)
                k = bi % 2
                S.dma(ch_x[k], xs[k][:, :], xin[r0:r0 + 128, :], writes=[B_xs[k]])
                ti = min(r0 // 512, 4)
                for half in range(2):
                    pb, pbf = bank()
                    for c4 in range(4):
                        c = half * 4 + c4
                        S.op("pe", lambda c=c, c4=c4, pb=pb, k=k, n=n: P.transpose(
                            out=pb[:, c4 * 128:c4 * 128 + 128], in_=xs[k][:, c * 128:(c + 1) * 128],
                            identity=CF("ident", 128, 0, 128)), reads=[B_xs[k], B_cstf], writes=[pbf])
                    src = pb[:, :].rearrange("p (c n) -> p c n", c=4)[:, :, 0:n]
                    dst = xT[:, half * 4:half * 4 + 4, r0:r0 + n]
                    if half == 0:
                        S.op("dve", lambda src=src, dst=dst: V.tensor_copy(out=dst, in_=src), reads=[pbf], writes=B_x[ti][0:4])
                    else:
                        S.op("act", lambda src=src, dst=dst: A.copy(out=dst, in_=src), reads=[pbf], writes=B_x[ti][4:8])

        def collective(src, dst, Bsrc, Bdst):
            S._deps("pool", [Bsrc], [Bdst])
            ins = G.collective_compute("AllGather", ALU.bypass, replica_groups=[[0, 1, 2, 3], [4, 5, 6, 7]],
                                       ins=[src], outs=[dst], dma_qos="P3")
            cc_n[0] += 1
            ins.then_inc(cc_sem, 1)
            S._mark(("cc", cc_sem, cc_n[0]), [Bsrc], [Bdst])
            S.extra = [("cc", cc_sem, cc_n[0])]

        sc = sb("sc", [128, 8, 2], F32)
        S.op("act", lambda: A.activation(out=sc[:].rearrange("p a b -> p (a b)"), in_=vecs[:, V_C:V_C + 16], func=AF.Silu),
             reads=[B_vecs], writes=[B_mod])
        mx = nc.dram_tensor("mx", [128, 72], F32).ap()
        mg = nc.dram_tensor("mg", [4 * 128, 72], F32).ap()
        B_mx, B_mg = Buf("mx"), Buf("mg")
        with Scope() as st:
            wm = [sb("wm%d" % i, [128, 8, 1152], F32, st) for i in range(2)]
            B_wm = [Buf("wm0"), Buf("wm1")]
            ch_w = [S.chan("wm0"), S.chan("wm1")]
            modS = sb("modS", [128, 2, 18, 2], F32, st)
            B_ms = Buf("modS")
            modAll = sb("modAll", [128, 4, 72], F32, st)
            B_ma = Buf("modAll")
            it = 0
            for li in range(2):
                for hf in range(2):
                    k = it % 2
                    it += 1
                    S.dma(ch_w[k], wm[k][:], wmod_d[li, :, hf * 1152:(hf + 1) * 1152].rearrange("(k p) n -> p k n", p=128), writes=[B_wm[k]])
                    pb, pbf = bank()
                    for cb in range(9):
                        for kk in range(8):
                            S.op("pe", lambda cb=cb, kk=kk, pb=pb, k=k: P.matmul(
                                pb[:, cb * 2:cb * 2 + 2], lhsT=wm[k][:, kk, cb * 128:(cb + 1) * 128], rhs=sc[:, kk, :],
                                start=(kk == 0), stop=(kk == 7)), reads=[B_wm[k], B_mod], writes=[pbf])
                    bm = vecs[:, V_BMOD + li * 18 + hf * 9:V_BMOD + li * 18 + hf * 9 + 9]
                    S.op("dve", lambda pb=pb, li=li, hf=hf, bm=bm: V.tensor_tensor(
                        out=modS[:, li, hf * 9:(hf + 1) * 9, :], in0=pb[:, 0:18].rearrange("p (c t) -> p c t", t=2),
                        in1=bm.unsqueeze(2).to_broadcast([128, 9, 2]), op=ALU.add), reads=[pbf, B_vecs], writes=[B_ms])
            S.dma(ch_c[3], mx, modS[:].rearrange("p a b c -> p (a b c)"), reads=[B_ms], writes=[B_mx])
            collective(mx, mg, B_mx, B_mg)
            S.dma(ch_c[3], modAll[:], mg.rearrange("(r p) n -> p r n", p=128), reads=[B_mg], writes=[B_ma])
            for li in range(nlayers):
                S.op("dve", lambda li=li: V.tensor_copy(
                    out=modT[:, li, :, :].rearrange("p (r c) v -> p r (c v)", r=4), in_=modAll[:, :, li * 36:(li + 1) * 36]),
                    reads=[B_ma], writes=[B_mod])

        def mod(li, j, c, kind):
            return modT[:, li, j * 8 + c, kind:kind + 1]

        for li in range(nlayers):
            for n3 in range(3):
                ng = vecs[:, V_NG + (li * 3 + n3) * 8:V_NG + (li * 3 + n3) * 8 + 8]
                jscale = n3 * 3 + 1
                jgate = n3 * 3 + 2
                S.op("dve", lambda li=li, n3=n3, ng=ng, jscale=jscale: V.scalar_tensor_tensor(
                    out=modA[:, li, n3, :, :], in0=modT[:, li, jscale * 8:(jscale + 1) * 8, :], scalar=1.0,
                    in1=ng.unsqueeze(2).to_broadcast([128, 8, 2]), op0=ALU.add, op1=ALU.mult),
                    reads=[B_mod, B_vecs], writes=[B_mod])
                S.op("dve", lambda li=li, n3=n3, jgate=jgate: V.tensor_scalar(
                    out=modG[:, li, n3, :, :], in0=modT[:, li, jgate * 8:(jgate + 1) * 8, :],
                    scalar1=(1.0 if n3 == 1 else 0.5), scalar2=None, op0=ALU.mult), reads=[B_mod], writes=[B_mod])

        sqi = [0]

        def col_rstd(src_fn, srcbufs, nparts, nch, n, inv_dim, rstd=rstd, B_rstd=B_rstd):
            pb, pbf = bank()
            for c in range(nch):
                s2 = sqi[0] % 2
                sqi[0] += 1
                S.op("act", lambda c=c, s2=s2: A.activation(out=sq[0:nparts, s2, 0:n], in_=src_fn(c), func=AF.Square),
                     reads=srcbufs, writes=[B_sq[s2]])
                S.op("pe", lambda c=c, pb=pb, s2=s2: P.matmul(pb[0:nparts, 0:n], lhsT=C("ones", nparts, 0, nparts),
                                                              rhs=sq[0:nparts, s2, 0:n], start=(c == 0), stop=(c == nch - 1)),
                     reads=[B_sq[s2], B_cstb], writes=[pbf])
            S.op("act", lambda pb=pb: A.activation(out=rstd[0:nparts, 0:n], in_=pb[0:nparts, 0:n], func=AF.Ln,
                                                   bias=epsb[0:nparts, :], scale=inv_dim),
                 reads=[pbf, B_eps], writes=[B_rstd])
            S.op("act", lambda: A.activation(out=rstd[0:nparts, 0:n], in_=rstd[0:nparts, 0:n], func=AF.Exp, scale=-0.5),
                 reads=[B_rstd], writes=[B_rstd])

        hsi = [0]

        def adaln(li, n3, ti, dst_fn, dstbufs):
            t0, n, kind = TT[ti]
            col_rstd(lambda c: xT[:, c, t0:t0 + n], list(B_x[ti]), 128, 8, n, 1.0 / D)
            for c in range(8):
                h2 = hsi[0] % 2
                hsi[0] += 1
                S.op("dve", lambda c=c, h2=h2: V.scalar_tensor_tensor(
                    out=hs[:, h2, 0:n], in0=xT[:, c, t0:t0 + n], scalar=modA[:, li, n3, c, kind:kind + 1],
                    in1=rstd[:, 0:n], op0=ALU.mult, op1=ALU.mult), reads=[B_x[ti][c], B_rstd, B_mod], writes=[B_hs[h2]])
                S.op("act", lambda c=c, h2=h2: A.activation(out=dst_fn(c), in_=hs[:, h2, 0:n], func=AF.Identity,
                                                            bias=mod(li, n3 * 3, c, kind), scale=1.0),
                     reads=[B_hs[h2], B_mod], writes=dstbufs)

        def half_ffn(li, hi, tiles, hT, B_h, st):
            n3 = 0 if hi == 0 else 2
            NB = 2
            wgs = sb("wgs", [128, 8, NB * 128], F32, st)
            wus = sb("wus", [128, 8, NB * 128], F32, st)
            wds = sb("wds", [128, NB, D], F32, st)
            wgb = [sb("wgb%d" % i, [128, 8, NB * 128], BF16, st) for i in range(2)]
            wub = [sb("wub%d" % i, [128, 8, NB * 128], BF16, st) for i in range(2)]
            wdb = [sb("wdb%d" % i, [128, NB, D], BF16, st) for i in range(2)]
            sg = [sb("sg%d" % i, [128, 512], F32, st) for i in range(2)]
            aT = [sb("aT%d" % i, [128, NB, 512], BF16, st) for i in range(2)]
            Bw = {k: Buf(k) for k in ["wgs", "wus", "wds"]}
            Bs = {k: [Buf(k + "0"), Buf(k + "1")] for k in ["wgb", "wub", "wdb", "sg", "aT"]}
            chs = {k: S.chan(k) for k in ["wgs", "wus", "wds"]}
            nblk = NJ // NB

            def load(jb):
                j0 = jb * NB * 128
                S.dma(chs["wgs"], wgs[:], wg_d[li, hi, :, j0:j0 + NB * 128].rearrange("(k p) n -> p k n", p=128), writes=[Bw["wgs"]])
                S.dma(chs["wus"], wus[:], wu_d[li, hi, :, j0:j0 + NB * 128].rearrange("(k p) n -> p k n", p=128), writes=[Bw["wus"]])
                S.dma(chs["wds"], wds[:], wd_d[li, hi, j0:j0 + NB * 128, :].rearrange("(j p) n -> p j n", p=128), writes=[Bw["wds"]])

            def cast(jb):
                k = jb % 2
                S.op("act", lambda: A.copy(out=wgb[k][:], in_=wgs[:]), reads=[Bw["wgs"]], writes=[Bs["wgb"][k]])
                S.op("act", lambda: A.copy(out=wub[k][:], in_=wus[:]), reads=[Bw["wus"]], writes=[Bs["wub"][k]])
                S.op("act", lambda: A.copy(out=wdb[k][:], in_=wds[:]), reads=[Bw["wds"]], writes=[Bs["wdb"][k]])

            load(0)
            cast(0)
            it = 0
            for jb in range(nblk):
                k = jb % 2
                if jb + 1 < nblk:
                    load(jb + 1)
                def stage1(ti, a):
                    t0, n, kind = TT[ti]
                    if jb == 0:
                        adaln(li, n3, ti, lambda c, t0=t0, n=n: hT[:, c, t0:t0 + n], [B_h[ti]])
                    for jj in range(NB):
                        pg, pgf = bank()
                        pu, puf = bank()
                        for kk in range(8):
                            S.op("pe", lambda kk=kk, pg=pg, jj=jj: P.matmul(
                                pg[:, 0:n], lhsT=wgb[k][:, kk, jj * 128:(jj + 1) * 128], rhs=hT[:, kk, t0:t0 + n],
                                start=(kk == 0), stop=(kk == 7)), reads=[Bs["wgb"][k], B_h[ti]], writes=[pgf])
                        for kk in range(8):
                            S.op("pe", lambda kk=kk, pu=pu, jj=jj: P.matmul(
                                pu[:, 0:n], lhsT=wub[k][:, kk, jj * 128:(jj + 1) * 128], rhs=hT[:, kk, t0:t0 + n],
                                start=(kk == 0), stop=(kk == 7)), reads=[Bs["wub"][k], B_h[ti]], writes=[puf])
                        s2 = jj % 2
                        S.op("act", lambda pg=pg, s2=s2: A.activation(out=sg[s2][:, 0:n], in_=pg[:, 0:n], func=AF.Silu),
                             reads=[pgf], writes=[Bs["sg"][s2]])
                        S.op("dve", lambda pu=pu, s2=s2, jj=jj: V.tensor_tensor(
                            out=aT[a][:, jj, 0:n], in0=pu[:, 0:n], in1=sg[s2][:, 0:n], op=ALU.mult),
                            reads=[puf, Bs["sg"][s2]], writes=[Bs["aT"][a]])

                def stage2(ti, a):
                    t0, n, kind = TT[ti]
                    for c in range(8):
                        po, pof = bank()
                        for jj in range(NB):
                            S.op("pe", lambda jj=jj, po=po, c=c: P.matmul(
                                po[:, 0:n], lhsT=wdb[k][:, jj, c * 128:(c + 1) * 128], rhs=aT[a][:, jj, 0:n],
                                start=(jj == 0), stop=(jj == NB - 1)), reads=[Bs["wdb"][k], Bs["aT"][a]], writes=[pof])
                        S.op("dve", lambda po=po, c=c: V.scalar_tensor_tensor(
                            out=xT[:, c, t0:t0 + n], in0=po[:, 0:n], scalar=modG[:, li, n3, c, kind:kind + 1],
                            in1=xT[:, c, t0:t0 + n], op0=ALU.mult, op1=ALU.add),
                            reads=[pof, B_mod, B_x[ti][c]], writes=[B_x[ti][c]])

                tl = list(tiles)
                stage1(tl[0], it % 2)
                for i_, ti in enumerate(tl):
                    a = it % 2
                    it += 1
                    if i_ + 1 < len(tl):
                        stage1(tl[i_ + 1], it % 2)
                    stage2(ti, a)
                    if i_ == min(1, len(tl) - 1) and jb + 1 < nblk:
                        cast(jb + 1)

        def dump(name, src_ap, bufs):
            if name in dbg_out:
                S.dma(ch_st[3], dbg_out[name], src_ap, reads=bufs, writes=[Buf()])

        O_PU, O_GQ, O_GK, O_R, O_OG, O_GV, O_AQ, O_AK, O_AV, O_FZ = 0, 64, 128, 192, 224, 288, 352, 416, 480, 544
        NG = 17
        GORDER = [q * 4 + t for t in range(4) for q in range(4)] + [16]

        def gtile(g):
            return (g * 512, 512) if g < 16 else (NLAT, NCTX)

        class HStream:
            def __init__(self, st):
                self.t = [sb("hst%d" % i, [128, 8, 512], BF16, st) for i in range(2)]
                self.b = [Buf("hst0"), Buf("hst1")]
                self.ch = [S.chan("hst0"), S.chan("hst1")]
                self.i = 0

            def load(self, g):
                k = self.i % 2
                self.i += 1
                if g < 16:
                    q, tt_ = g // 4, g % 4
                    S.dma(self.ch[k], self.t[k][:, :, :], hgp[tt_][q * D:(q + 1) * D, :].rearrange("(c p) n -> p c n", p=128),
                          reads=[B_hgp[tt_]], writes=[self.b[k]])
                else:
                    for q in range(4):
                        S.dma(self.ch[k], self.t[k][:, :, q * 64:(q + 1) * 64],
                              hgp[4][q * D:(q + 1) * D, :].rearrange("(c p) n -> p c n", p=128),
                              reads=[B_hgp[4]], writes=[self.b[k]], nowait=(q > 0))
                return self.t[k], self.b[k]

        def stage_b(li):
            ctx_out = (li < nlayers - 1) or (nlayers == 1)
            qtiles = list(range(NG)) if ctx_out else list(range(16))
            with Scope() as sB:
                winb = sb("winb", [128, 8, NHC], BF16, sB)
                B_win = Buf("winb")
                with Scope() as st:
                    wst = sb("wst", [128, 8, NHC], F32, st)
                    B_wst = Buf("wst")
                    S.dma(ch_c[4], wst[:], winh_d[li].rearrange("(k p) n -> p k n", p=128), writes=[B_wst])
                    S.op("act", lambda: A.copy(out=winb[:], in_=wst[:]), reads=[B_wst], writes=[B_win])
                ystage = [sb("ystage%d" % i, [128, 512], BF16, sB) for i in range(2)]
                B_ys = [Buf("ys0"), Buf("ys1")]
                ysi = [0]

                def proj_fm(ht, hb, col0, ncol, n, pb, pbf):
                    for kk in range(8):
                        S.op("pe", lambda kk=kk: P.matmul(pb[0:ncol, 0:n], lhsT=winb[:, kk, col0:col0 + ncol], rhs=ht[:, kk, 0:n],
                                                          start=(kk == 0), stop=(kk == 7)), reads=[B_win, hb], writes=[pbf])

                def proj_tm(ht, hb, col0, ncol, nsub, pb, pbf):
                    for sub in range(nsub):
                        for kk in range(8):
                            S.op("pe", lambda kk=kk, sub=sub: P.matmul(
                                pb[:, sub * ncol:(sub + 1) * ncol], lhsT=ht[:, kk, sub * 128:(sub + 1) * 128],
                                rhs=winb[:, kk, col0:col0 + ncol], start=(kk == 0), stop=(kk == 7)),
                                reads=[B_win, hb], writes=[pbf])

                def store_y(row0, tok0, n, fn):
                    k = ysi[0] % 2
                    ysi[0] += 1
                    fn(ystage[k][0:64, 0:n], B_ys[k])
                    m_ = row0 // 64
                    if tok0 < NLAT:
                        S.dma(ch_st[k], yxp[m_][0:64, tok0:tok0 + n], ystage[k][0:64, 0:n], reads=[B_ys[k]], writes=[B_yxp[m_]])
                    else:
                        S.dma(ch_st[k], yxp[4][row0:row0 + 64, 0:n], ystage[k][0:64, 0:n], reads=[B_ys[k]], writes=[B_yxp[4]])

                with Scope() as sAB:
                    u_tok = sb("u_tok", [128, 66, 64], BF16, sAB)
                    B_u = Buf("u_tok")
                    mT = [sb("mT%d" % i, [64, 512], BF16, sAB) for i in range(2)]
                    B_m = [Buf("mT0"), Buf("mT1")]
                    zT = [sb("zT%d" % i, [64, 512], BF16, sAB) for i in range(2)]
                    B_z = [Buf("z0"), Buf("z1")]
                    ust = [sb("ust%d" % i, [64, 2, 512], BF16, sAB) for i in range(2)]
                    B_us = [Buf("us0"), Buf("us1")]
                    ch_f = [S.chan("f0"), S.chan("f1")]
                    utok = sb("utok", [128, 2, 2, 64], BF16, sAB)
                    B_ut = Buf("utok")
                    st = sAB

                    def pool_proj(g, ht, hb):
                        t0, n = gtile(g)
                        nsub = n // 128
                        pb, pbf = bank()
                        proj_tm(ht, hb, O_PU, 64, nsub, pb, pbf)
                        S.op("act", lambda: A.copy(
                            out=u_tok[:, g * 4:g * 4 + nsub, :], in_=pb[:, 0:nsub * 64].rearrange("p (s c) -> p s c", c=64)),
                            reads=[pbf], writes=[B_u])

                    def fnet_proj(g, gi_, ht, hb):
                        if g == 16 and not ctx_out:
                            return
                        t0, n = gtile(g)
                        k = gi_ % 2
                        pb, pbf = bank()
                        proj_fm(ht, hb, O_FZ, 64, n, pb, pbf)
                        S.op("act", lambda pb=pb, k=k, n=n: A.copy(out=zT[k][:, 0:n], in_=pb[0:64, 0:n]), reads=[pbf], writes=[B_z[k]])
                        pr, prf = bank()
                        pi, pif = bank()
                        S.op("pe", lambda pr=pr, k=k, n=n: P.matmul(pr[0:64, 0:n], lhsT=C("chC", 64), rhs=zT[k][:, 0:n], start=True, stop=True),
                             reads=[B_cstb, B_z[k]], writes=[prf])
                        S.op("pe", lambda pi=pi, k=k, n=n: P.matmul(pi[0:64, 0:n], lhsT=C("chS", 64), rhs=zT[k][:, 0:n], start=True, stop=True),
                             reads=[B_cstb, B_z[k]], writes=[pif])
                        S.op("dve", lambda pr=pr, k=k, n=n: V.tensor_copy(out=ust[k][:, 0, 0:n], in_=pr[0:64, 0:n]), reads=[prf], writes=[B_us[k]])
                        S.op("act", lambda pi=pi, k=k, n=n: A.copy(out=ust[k][:, 1, 0:n], in_=pi[0:64, 0:n]), reads=[pif], writes=[B_us[k]])
                        if g < 16:
                            S.dma(ch_f[k], fsc[:, :, t0:t0 + n].rearrange("r c t -> c r t"), ust[k][:, :, 0:n], reads=[B_us[k]], writes=[B_fsc])
                        else:
                            pt, ptf = bank()
                            ptb = pt[:, :].bitcast(BF16)
                            for sub in range(2):
                                for ri in range(2):
                                    S.op("pe", lambda sub=sub, ri=ri: P.transpose(
                                        out=ptb[:, (sub * 2 + ri) * 64:(sub * 2 + ri + 1) * 64], in_=ust[k][:, ri, sub * 128:(sub + 1) * 128],
                                        identity=C("ident", 64, 0, 64)), reads=[B_us[k], B_cstb], writes=[ptf])
                            S.op("dve", lambda: V.tensor_copy(out=utok[:].rearrange("p a b c -> p (a b c)"), in_=ptb[:, 0:256]),
                                 reads=[ptf], writes=[B_ut])
                            px, pxf = bank()
                            i = 0
                            for sub in range(2):
                                for ri, nm in ((0, "C256"), (1, "S256")):
                                    S.op("pe", lambda sub=sub, ri=ri, nm=nm, i=i: P.matmul(
                                        px[0:64, 0:256], lhsT=utok[:, sub, ri, :], rhs=C(nm, 128, sub * 256, (sub + 1) * 256),
                                        start=(i == 0), stop=(i == 3)), reads=[B_ut, B_cstb], writes=[pxf])
                                    i += 1
                            store_y(192, NLAT, 256, lambda dst, db, px=px, pxf=pxf: S.op("act", lambda: A.activation(
                                out=dst, in_=px[0:64, 0:256], func=AF.Copy, scale=1.0 / 128.0), reads=[pxf], writes=[db]))

                    with Scope() as st:
                        qT_all = sb("qT_all", [128, NTOK], BF16, st)
                        kT_all = sb("kT_all", [128, NTOK], BF16, st)
                        Vext = sb("Vext", [128, 66, 128], BF16, st)
                        B_q, B_k, B_v = Buf("qT"), Buf("kT"), Buf("Vext")
                        S.op("dve", lambda: V.memset(Vext[:, :, 64:128], 1.0), writes=[B_v])
                        S.op("dve", lambda: V.memset(qT_all[64:128, :], 0.0), writes=[B_q])
                        S.op("dve", lambda: V.memset(kT_all[64:128, :], 0.0), writes=[B_k])
                        with Scope() as st2:
                            hsrc = HStream(st2)
                            cs = [sb("cs%d" % i, [64, 2, 512], F32, st2) for i in range(2)]
                            B_cs = [Buf("cs0"), Buf("cs1")]
                            ch_r = [S.chan("r0"), S.chan("r1")]
                            qn = [sb("qn%d" % i, [64, 512], F32, st2) for i in range(2)]
                            B_qn = [Buf("qn0"), Buf("qn1")]
                            tt4 = [sb("tt%d" % i, [64, 512], F32, st2) for i in range(4)]
                            B_tt4 = [Buf("tt%d" % i) for i in range(4)]
                            rstd_k = sb("rstd_k", [64, 512], F32, st2)
                            B_rstd_k = Buf("rstd_k")
                            nxt = hsrc.load(GORDER[0])
                            for gi_, g in enumerate(GORDER):
                                ht, hb = nxt
                                if gi_ + 1 < NG:
                                    nxt = hsrc.load(GORDER[gi_ + 1])
                                t0, n = gtile(g)
                                nsub = n // 128
                                k = gi_ % 2
                                if g < 16:
                                    S.dma(ch_r[k], cs[k][:], rope_d[:, :, t0:t0 + 512].rearrange("r d t -> d r t"), writes=[B_cs[k]])
                                pv, pvf = bank()
                                proj_tm(ht, hb, O_AV, 64, nsub, pv, pvf)
                                S.op("act", lambda pv=pv, g=g, nsub=nsub: A.copy(
                                    out=Vext[:, g * 4:g * 4 + nsub, 0:64], in_=pv[:, 0:nsub * 64].rearrange("p (s c) -> p s c", c=64)),
                                    reads=[pvf], writes=[B_v])
                                for which, (col0, gcol, dstT, dstB) in enumerate([(O_AQ, H_QN, qT_all, B_q), (O_AK, H_KN, kT_all, B_k)]):
                                    if which == 0 and g == 16 and not ctx_out:
                                        continue
                                    pq, pqf = bank()
                                    proj_fm(ht, hb, col0, 64, n, pq, pqf)
                                    rs_, Brs_ = (rstd, B_rstd) if which == 0 else (rstd_k, B_rstd_k)
                                    tt, B_tt = tt4[which * 2:which * 2 + 2], B_tt4[which * 2:which * 2 + 2]
                                    col_rstd(lambda c, pq=pq, n=n: pq[0:64, 0:n], [pqf], 64, 1, n, 1.0 / 64, rstd=rs_, B_rstd=Brs_)
                                    S.op("dve", lambda pq=pq, n=n, which=which, gcol=gcol: V.scalar_tensor_tensor(
                                        out=qn[which][:, 0:n], in0=pq[0:64, 0:n], scalar=hpf[0:64, li, gcol:gcol + 1], in1=rs_[0:64, 0:n],
                                        op0=ALU.mult, op1=ALU.mult), reads=[pqf, B_hp, Brs_], writes=[B_qn[which]])
                                    if g < 16:
                                        pp, ppf = bank()
                                        S.op("pe", lambda pp=pp, which=which, n=n: P.matmul(pp[0:64, 0:n], lhsT=CF("perm", 64, 0, 64), rhs=qn[which][:, 0:n],
                                                                                            start=True, stop=True), reads=[B_cstf, B_qn[which]], writes=[ppf])
                                        S.op("dve", lambda which=which, k=k, n=n: V.tensor_tensor(out=tt[0][:, 0:n], in0=qn[which][:, 0:n], in1=cs[k][:, 0, 0:n], op=ALU.mult),
                                             reads=[B_qn[which], B_cs[k]], writes=[B_tt[0]])
                                        S.op("dve", lambda pp=pp, k=k, n=n: V.tensor_tensor(out=tt[1][:, 0:n], in0=pp[0:64, 0:n], in1=cs[k][:, 1, 0:n], op=ALU.mult),
                                             reads=[ppf, B_cs[k]], writes=[B_tt[1]])
                                        S.op("dve", lambda dstT=dstT, t0=t0, n=n: V.tensor_tensor(out=dstT[0:64, t0:t0 + n], in0=tt[0][:, 0:n], in1=tt[1][:, 0:n], op=ALU.add),
                                             reads=[B_tt[0], B_tt[1]], writes=[dstB])
                                    else:
                                        S.op("act", lambda dstT=dstT, which=which, t0=t0, n=n: A.copy(out=dstT[0:64, t0:t0 + n], in_=qn[which][:, 0:n]),
                                             reads=[B_qn[which]], writes=[dstB])
                                pool_proj(g, ht, hb)
                                fnet_proj(g, gi_, ht, hb)
                        pT = [sb("pT%d" % i, [128, 512], BF16, st) for i in range(4)]
                        B_pT = [Buf("pT%d" % i) for i in range(4)]
                        pcount = [0]
                        pvs = sb("pvs", [128, 512], F32, st)
                        rec = sb("rec", [64, 512], F32, st)
                        B_pvs, B_rec = Buf("pvs"), Buf("rec")
                        pi_ = 0
                        for qi, g in enumerate(qtiles):
                            t0, n = gtile(g)
                            kbs = list(range(66)) if g < 16 else [64, 65]
                            pacc, paccf = pbank[6 + qi % 2], pbuf[6 + qi % 2]
                            nrot[0] = 6
                            pend = []

                            def issue_s(kb):
                                ps, psf = bank()
                                S.op("pe", lambda: P.matmul(ps[:, 0:n], lhsT=kT_all[:, kb * 128:(kb + 1) * 128], rhs=qT_all[:, t0:t0 + n],
                                                            start=True, stop=True), reads=[B_k, B_q], writes=[psf])
                                k3 = pcount[0] % 4
                                pcount[0] += 1
                                S.op("act", lambda: A.activation(out=pT[k3][:, 0:n], in_=ps[:, 0:n], func=AF.Exp, scale=0.125),
                                     reads=[psf], writes=[B_pT[k3]])
                                pend.append(k3)

                            LOOK = 3
                            for i in range(min(LOOK, len(kbs))):
                                issue_s(kbs[i])
                            for i, kb in enumerate(kbs):
                                if i + LOOK < len(kbs):
                                    issue_s(kbs[i + LOOK])
                                k3 = pend.pop(0)
                                S.op("pe", lambda kb=kb, k3=k3, i=i, nk=len(kbs): P.matmul(
                                    pacc[:, 0:n], lhsT=Vext[:, kb, :], rhs=pT[k3][:, 0:n], start=(i == 0), stop=(i == nk - 1)),
                                    reads=[B_v, B_pT[k3]], writes=[paccf])
                            S.op("dve", lambda pacc=pacc: V.tensor_copy(out=pvs[:, 0:n], in_=pacc[:, 0:n]), reads=[paccf], writes=[B_pvs])
                            pd, pdf = bank()
                            S.op("pe", lambda pd=pd: P.matmul(pd[0:64, 0:n], lhsT=CF("ident", 128, 64, 128), rhs=pvs[:, 0:n], start=True, stop=True),
                                 reads=[B_cstf, B_pvs], writes=[pdf])
                            S.op("dve", lambda pd=pd: V.reciprocal(out=rec[:, 0:n], in_=pd[0:64, 0:n]), reads=[pdf], writes=[B_rec])
                            store_y(128, t0, n, lambda dst, db: S.op("pool", lambda: G.tensor_tensor(
                                out=dst, in0=pvs[0:64, 0:n], in1=rec[:, 0:n], op=ALU.mult), reads=[B_pvs, B_rec], writes=[db]))
                    nrot[0] = 8
                    collective(yxp[2], ygp[2], B_yxp[2], B_ygp[2])
                    if stop == "att":
                        return

                    with Scope() as st:
                        for g in qtiles:
                            t0, n = gtile(g)
                            nsub = n // 128
                            s0, s1 = (0, 63) if g < 16 else (64, 65)
                            pb, pbf = bank()
                            for sub in range(nsub):
                                a = g * 4 + sub
                                terms = []
                                if a > s0:
                                    terms.append((a - 1, 0))
                                terms.append((a, 3 if a == s0 else (4 if a == s1 else 1)))
                                if a < s1:
                                    terms.append((a + 1, 2))
                                for i, (src, blk) in enumerate(terms):
                                    S.op("pe", lambda src=src, blk=blk, i=i, sub=sub, nt=len(terms): P.matmul(
                                        pb[0:64, sub * 128:(sub + 1) * 128], lhsT=u_tok[:, src, :],
                                        rhs=C("band", 128, blk * 128, (blk + 1) * 128), start=(i == 0), stop=(i == nt - 1)),
                                        reads=[B_u, B_cstb], writes=[pbf])
                            k = g % 2
                            S.op("dve", lambda pb=pb, k=k, n=n: V.tensor_copy(out=mT[k][:, 0:n], in_=pb[0:64, 0:n]), reads=[pbf], writes=[B_m[k]])
                            p2, p2f = bank()
                            S.op("pe", lambda p2=p2, k=k, n=n: P.matmul(p2[0:64, 0:n], lhsT=hpb[0:64, li, H_PW:H_PW + 64], rhs=mT[k][:, 0:n],
                                                                        start=True, stop=True), reads=[B_hp, B_m[k]], writes=[p2f])
                            store_y(0, t0, n, lambda dst, db, p2=p2, p2f=p2f, n=n: S.op("act", lambda: A.activation(
                                out=dst, in_=p2[0:64, 0:n], func=AF.Copy, scale=hpf[0:64, li, H_PS:H_PS + 1]),
                                reads=[p2f, B_hp], writes=[db]))
                    collective(yxp[0], ygp[0], B_yxp[0], B_ygp[0])
                    if stop == "pool":
                        return

                    with Scope() as st:
                        Ur = sb("Ur", [64, 64, 128], BF16, st)
                        Ui = sb("Ui", [64, 64, 128], BF16, st)
                        B_U = Buf("U")
                        S.dma(ch_f[0], Ur[:], fsc[0].rearrange("c (h l) -> h c l", l=128), reads=[B_fsc], writes=[B_U])
                        S.dma(ch_f[1], Ui[:], fsc[1].rearrange("c (h l) -> h c l", l=128), reads=[B_fsc], writes=[B_U])
                        Ypr = sb("Ypr", [128, 64, 64], BF16, st)
                        Ypi = sb("Ypi", [128, 64, 64], BF16, st)
                        B_Y = Buf("Y")
                        tw = [sb("tw%d" % i, [128, 4, 64], F32, st) for i in range(4)]
                        B_tw = [Buf("tw%d" % i) for i in range(4)]
                        TcB = CF("Tc", 128, 0, 64).unsqueeze(1).to_broadcast([128, 4, 64])
                        TsB = CF("Ts", 128, 0, 64).unsqueeze(1).to_broadcast([128, 4, 64])
                        for grp in range(16):
                            pb, pbf = bank()
                            for ci in range(4):
                                c = grp * 4 + ci
                                S.op("pe", lambda c=c, ci=ci, pb=pb: P.matmul(pb[:, ci * 128:(ci + 1) * 128], lhsT=Ur[:, c, :], rhs=C("dA1", 64),
                                                                              start=True, stop=False), reads=[B_U, B_cstb], writes=[pbf])
                                S.op("pe", lambda c=c, ci=ci, pb=pb: P.matmul(pb[:, ci * 128:(ci + 1) * 128], lhsT=Ui[:, c, :], rhs=C("dA2", 64),
                                                                              start=False, stop=True), reads=[B_U, B_cstb], writes=[pbf])
                            pv4 = pb[:, :].rearrange("p (c r k) -> p c r k", c=4, r=2)
                            Yr, Yi = pv4[:, :, 0, :], pv4[:, :, 1, :]
                            S.op("dve", lambda Yr=Yr: V.tensor_tensor(out=tw[0][:], in0=Yr, in1=TcB, op=ALU.mult), reads=[pbf, B_cstf], writes=[B_tw[0]])
                            S.op("dve", lambda Yi=Yi: V.tensor_tensor(out=tw[1][:], in0=Yi, in1=TsB, op=ALU.mult), reads=[pbf, B_cstf], writes=[B_tw[1]])
                            S.op("pool", lambda grp=grp: G.tensor_tensor(out=Ypr[:, grp * 4:grp * 4 + 4, :], in0=tw[0][:], in1=tw[1][:], op=ALU.add),
                                 reads=[B_tw[0], B_tw[1]], writes=[B_Y])
                            S.op("dve", lambda Yi=Yi: V.tensor_tensor(out=tw[2][:], in0=Yi, in1=TcB, op=ALU.mult), reads=[pbf, B_cstf], writes=[B_tw[2]])
                            S.op("dve", lambda Yr=Yr: V.tensor_tensor(out=tw[3][:], in0=Yr, in1=TsB, op=ALU.mult), reads=[pbf, B_cstf], writes=[B_tw[3]])
                            S.op("pool", lambda grp=grp: G.tensor_tensor(out=Ypi[:, grp * 4:grp * 4 + 4, :], in0=tw[2][:], in1=tw[3][:], op=ALU.subtract),
                                 reads=[B_tw[2], B_tw[3]], writes=[B_Y])
                        Fst = [sb("Fst%d" % i, [128, 8, 64], BF16, st) for i in range(2)]
                        B_F = [Buf("F0"), Buf("F1")]
                        for blk in range(8):
                            k = blk % 2
                            pb, pbf = bank()
                            S.op("pe", lambda pb=pb, blk=blk: P.matmul(pb[:, :], lhsT=C("C128"), rhs=Ypr[:, blk * 8:blk * 8 + 8, :].rearrange("p c k -> p (c k)"),
                                                                       start=True, stop=False), reads=[B_Y, B_cstb], writes=[pbf])
                            S.op("pe", lambda pb=pb, blk=blk: P.matmul(pb[:, :], lhsT=C("S128"), rhs=Ypi[:, blk * 8:blk * 8 + 8, :].rearrange("p c k -> p (c k)"),
                                                                       start=False, stop=True), reads=[B_Y, B_cstb], writes=[pbf])
                            S.op("act", lambda pb=pb, k=k: A.activation(out=Fst[k][:].rearrange("p c k -> p (c k)"), in_=pb[:, :], func=AF.Copy,
                                                                        scale=float(1.0 / np.sqrt(8192.0 * 64.0))), reads=[pbf], writes=[B_F[k]])
                            S.dma(ch_f[k], yxp[3][blk * 8:blk * 8 + 8, :].rearrange("c (a b) -> a c b", b=64),
                                  Fst[k][:, :, :], reads=[B_F[k]], writes=[B_yxp[3]])
                    collective(yxp[3], ygp[3], B_yxp[3], B_ygp[3])
                    if stop == "fnet":
                        return


                with Scope() as st:
                    qg_all = sb("qg_all", [64, NTOK], BF16, st)
                    kg_all = sb("kg_all", [64, NTOK], BF16, st)
                    v_tok = sb("v_tok", [128, 66, 64], BF16, st)
                    KV_all = sb("KV_all", [64, 66, 64], F32, st)
                    ET_all = sb("ET_all", [64, 66], F32, st)
                    Sprev = sb("Sprev", [64, 66, 64], BF16, st)
                    Srun = sb("Srun", [64, 64], F32, st)
                    negba = sb("negba", [64, 1], F32, st)
                    B_qg, B_kg, B_vt, B_KV, B_ET, B_Sp, B_Sr = (Buf(n_) for n_ in ["qg", "kg", "vt", "KV", "ET", "Sp", "Sr"])

                    S.op("dve", lambda: V.tensor_scalar(out=negba[:], in0=hpf[0:64, li, H_BA:H_BA + 1], scalar1=-1.0, scalar2=None, op0=ALU.mult),
                         reads=[B_hp], writes=[B_hp])
                    with Scope() as st2:
                        hsrc = HStream(st2)
                        f32t = {nm: sb("g_" + nm, [64, 512], F32, st2) for nm in ["gl", "Pc", "W", "b", "dk"]}
                        Bf = {nm: Buf("g_" + nm) for nm in f32t}
                        for new_, old_ in (("eq", "gl"), ("ek", "Pc"), ("ed", "W")):
                            f32t[new_] = f32t[old_]
                            Bf[new_] = Bf[old_]
                        rT = sb("rT", [32, 512], BF16, st2)
                        B_rT = Buf("rT")
                        khT = sb("khT", [64, 512], BF16, st2)
                        B_kh = Buf("khT")
                        khtok = [sb("khtok%d" % i, [128, 64], BF16, st2) for i in range(2)]
                        B_kt = [Buf("kt0"), Buf("kt1")]
                        nxt = hsrc.load(GORDER[0])
                        for gi_, g in enumerate(GORDER):
                            ht, hb = nxt
                            if gi_ + 1 < NG:
                                nxt = hsrc.load(GORDER[gi_ + 1])
                            t0, n = gtile(g)
                            nsub = n // 128
                            pv, pvf = bank()
                            proj_tm(ht, hb, O_GV, 64, nsub, pv, pvf)
                            S.op("act", lambda pv=pv, g=g, nsub=nsub: A.copy(
                                out=v_tok[:, g * 4:g * 4 + nsub, :], in_=pv[:, 0:nsub * 64].rearrange("p (s c) -> p s c", c=64)),
                                reads=[pvf], writes=[B_vt])
                            pr, prf = bank()
                            proj_fm(ht, hb, O_R, 32, n, pr, prf)
                            S.op("dve", lambda pr=pr, n=n: V.tensor_copy(out=rT[:, 0:n], in_=pr[0:32, 0:n]), reads=[prf], writes=[B_rT])
                            pgt, pgtf = bank()
                            S.op("pe", lambda pgt=pgt, n=n: P.matmul(pgt[0:64, 0:n], lhsT=hpb[0:32, li, H_WA:H_WA + 64], rhs=rT[:, 0:n], start=True, stop=True),
                                 reads=[B_hp, B_rT], writes=[pgtf])
                            S.op("act", lambda pgt=pgt, n=n: A.activation(out=f32t["gl"][:, 0:n], in_=pgt[0:64, 0:n], func=AF.Exp, bias=negba[:, :], scale=-1.0),
                                 reads=[pgtf, B_hp], writes=[Bf["gl"]])
                            S.op("act", lambda n=n: A.activation(out=f32t["gl"][:, 0:n], in_=f32t["gl"][:, 0:n], func=AF.Ln, bias=onesf[0:64, 0:1], scale=1.0),
                                 reads=[Bf["gl"], B_cstf], writes=[Bf["gl"]])
                            S.op("dve", lambda n=n: V.tensor_scalar(out=f32t["gl"][:, 0:n], in0=f32t["gl"][:, 0:n], scalar1=-1.0 / 16.0, scalar2=None, op0=ALU.mult),
                                 reads=[Bf["gl"]], writes=[Bf["gl"]])
                            for sub in range(nsub):
                                sl = slice(sub * 128, (sub + 1) * 128)
                                S.op("dve", lambda sl=sl: V.tensor_tensor_scan(out=f32t["Pc"][:, sl], data0=onesf[0:64, :], data1=f32t["gl"][:, sl],
                                                                               initial=0.0, op0=ALU.mult, op1=ALU.add),
                                     reads=[Bf["gl"], B_cstf], writes=[Bf["Pc"]])
                            for sub in range(nsub):
                                sl = slice(sub * 128, (sub + 1) * 128)
                                last = sub * 128 + 127
                                S.op("dve", lambda sl=sl, last=last: V.tensor_scalar(out=f32t["W"][:, sl], in0=f32t["Pc"][:, sl], scalar1=f32t["Pc"][:, last:last + 1],
                                                                                    scalar2=None, op0=ALU.subtract), reads=[Bf["Pc"]], writes=[Bf["W"]])
                            S.op("act", lambda g=g, nsub=nsub, n=n: A.activation(
                                out=ET_all[:, g * 4:g * 4 + nsub], in_=f32t["Pc"][:, 0:n].rearrange("p (s t) -> p s t", t=128)[:, :, 127], func=AF.Exp),
                                reads=[Bf["Pc"]], writes=[B_ET])
                            S.op("dve", lambda n=n: V.tensor_tensor(out=f32t["b"][32:64, 0:n], in0=f32t["gl"][32:64, 0:n], in1=f32t["W"][32:64, 0:n], op=ALU.subtract),
                                 reads=[Bf["gl"], Bf["W"]], writes=[Bf["b"]])
                            S.op("dve", lambda n=n: V.tensor_tensor(out=f32t["dk"][32:64, 0:n], in0=f32t["Pc"][32:64, 0:n], in1=f32t["gl"][32:64, 0:n], op=ALU.subtract),
                                 reads=[Bf["Pc"], Bf["gl"]], writes=[Bf["dk"]])
                            S.op("act", lambda n=n: A.activation(out=f32t["eq"][32:64, 0:n], in_=f32t["b"][32:64, 0:n], func=AF.Exp), reads=[Bf["b"], Bf["dk"]], writes=[Bf["eq"]])
                            S.op("act", lambda n=n: A.activation(out=f32t["eq"][0:32, 0:n], in_=f32t["Pc"][0:32, 0:n], func=AF.Exp), reads=[Bf["Pc"]], writes=[Bf["eq"]])
                            S.op("act", lambda n=n: A.activation(out=f32t["ed"][0:32, 0:n], in_=f32t["W"][0:32, 0:n], func=AF.Exp, scale=-1.0), reads=[Bf["W"]], writes=[Bf["ed"]])
                            S.op("act", lambda n=n: A.activation(out=f32t["ed"][32:64, 0:n], in_=f32t["dk"][32:64, 0:n], func=AF.Exp), reads=[Bf["dk"]], writes=[Bf["ed"]])
                            S.op("act", lambda n=n: A.activation(out=f32t["ek"][0:32, 0:n], in_=f32t["Pc"][0:32, 0:n], func=AF.Exp, scale=-1.0), reads=[Bf["Pc"]], writes=[Bf["ek"]])
                            S.op("act", lambda n=n: A.activation(out=f32t["ek"][32:64, 0:n], in_=f32t["b"][32:64, 0:n], func=AF.Exp, scale=-1.0), reads=[Bf["b"]], writes=[Bf["ek"]])
                            pq, pqf = bank()
                            proj_fm(ht, hb, O_GQ, 64, n, pq, pqf)
                            S.op("dve", lambda pq=pq, t0=t0, n=n: V.scalar_tensor_tensor(out=qg_all[:, t0:t0 + n], in0=pq[0:64, 0:n], scalar=float(32 ** -0.5),
                                                                                        in1=f32t["eq"][:, 0:n], op0=ALU.mult, op1=ALU.mult),
                                 reads=[pqf, Bf["eq"]], writes=[B_qg])
                            pk, pkf = bank()
                            proj_fm(ht, hb, O_GK, 64, n, pk, pkf)
                            S.op("dve", lambda pk=pk, t0=t0, n=n: V.tensor_tensor(out=kg_all[:, t0:t0 + n], in0=pk[0:64, 0:n], in1=f32t["ek"][:, 0:n], op=ALU.mult),
                                 reads=[pkf, Bf["ek"]], writes=[B_kg])
                            S.op("dve", lambda pk=pk, n=n: V.tensor_tensor(out=khT[:, 0:n], in0=pk[0:64, 0:n], in1=f32t["ed"][:, 0:n], op=ALU.mult),
                                 reads=[pkf, Bf["ed"]], writes=[B_kh])
                            pkv, pkvf = bank()
                            for sub in range(nsub):
                                k2 = sub % 2
                                pt, ptf = bank()
                                ptb = pt[:, :].bitcast(BF16)
                                S.op("pe", lambda sub=sub, ptb=ptb: P.transpose(out=ptb[:, 0:64], in_=khT[:, sub * 128:(sub + 1) * 128], identity=C("ident", 64, 0, 64)),
                                     reads=[B_kh, B_cstb], writes=[ptf])
                                S.op("act", lambda ptb=ptb, k2=k2: A.copy(out=khtok[k2][:], in_=ptb[:, 0:64]), reads=[ptf], writes=[B_kt[k2]])
                                S.op("pe", lambda sub=sub, k2=k2, pkv=pkv, g=g: P.matmul(pkv[0:64, sub * 64:(sub + 1) * 64], lhsT=khtok[k2][:], rhs=v_tok[:, g * 4 + sub, :],
                                                                                       start=True, stop=True), reads=[B_kt[k2], B_vt], writes=[pkvf])
                            S.op("dve", lambda pkv=pkv, g=g, nsub=nsub: V.tensor_copy(out=KV_all[:, g * 4:g * 4 + nsub, :],
                                                                                      in_=pkv[0:64, 0:nsub * 64].rearrange("p (s c) -> p s c", c=64)),
                                 reads=[pkvf], writes=[B_KV])
                    order_f = [64, 65] + list(range(64))
                    order_b = [65, 64] + list(range(63, -1, -1))
                    for s_ in range(1, 66):
                        for (lo, hi_, order) in ((0, 32, order_f), (32, 64, order_b)):
                            cidx, pidx = order[s_], order[s_ - 1]
                            S.op("dve", lambda lo=lo, hi_=hi_, cidx=cidx, pidx=pidx: V.scalar_tensor_tensor(
                                out=KV_all[lo:hi_, cidx, :], in0=KV_all[lo:hi_, pidx, :], scalar=ET_all[lo:hi_, cidx:cidx + 1], in1=KV_all[lo:hi_, cidx, :],
                                op0=ALU.mult, op1=ALU.add), reads=[B_ET, B_KV], writes=[B_KV])
                    S.op("act", lambda: A.copy(out=Sprev[0:32, 1:64, :], in_=KV_all[0:32, 0:63, :]), reads=[B_KV], writes=[B_Sp])
                    S.op("act", lambda: A.copy(out=Sprev[0:32, 0, :], in_=KV_all[0:32, 65, :]), reads=[B_KV], writes=[B_Sp])
                    S.op("act", lambda: A.copy(out=Sprev[0:32, 65, :], in_=KV_all[0:32, 64, :]), reads=[B_KV], writes=[B_Sp])
                    S.op("pool", lambda: G.memset(Sprev[0:32, 64, :], 0.0), writes=[B_Sp])
                    S.op("dve", lambda: V.tensor_copy(out=Sprev[32:64, 0:63, :], in_=KV_all[32:64, 1:64, :]), reads=[B_KV], writes=[B_Sp])
                    S.op("dve", lambda: V.tensor_copy(out=Sprev[32:64, 63, :], in_=KV_all[32:64, 64, :]), reads=[B_KV], writes=[B_Sp])
                    S.op("dve", lambda: V.tensor_copy(out=Sprev[32:64, 64, :], in_=KV_all[32:64, 65, :]), reads=[B_KV], writes=[B_Sp])
                    S.op("pool", lambda: G.memset(Sprev[32:64, 65, :], 0.0), writes=[B_Sp])
                    with Scope() as st2:
                        hsrc = HStream(st2)
                        qFt = [sb("qFt%d" % i, [64, 128], BF16, st2) for i in range(2)]
                        qBt = [sb("qBt%d" % i, [64, 128], BF16, st2) for i in range(2)]
                        B_qF = [Buf("qF0"), Buf("qF1")]
                        B_qB = [Buf("qB0"), Buf("qB1")]
                        for i in range(2):
                            S.op("pool", lambda i=i: G.memset(qFt[i][:], 0.0), writes=[B_qF[i]])
                            S.op("pool", lambda i=i: G.memset(qBt[i][:], 0.0), writes=[B_qB[i]])
                        Af = [sb("Af%d" % i, [128, 128], BF16, st2) for i in range(2)]
                        Ab = [sb("Ab%d" % i, [128, 128], BF16, st2) for i in range(2)]
                        B_Af = [Buf("Af0"), Buf("Af1")]
                        B_Ab = [Buf("Ab0"), Buf("Ab1")]
                        ogs = sb("ogs", [64, 512], F32, st2)
                        yn = sb("yn", [64, 512], F32, st2)
                        B_og, B_yn = Buf("ogs"), Buf("yn")
                        nxt = hsrc.load(qtiles[0])
                        for gi_, g in enumerate(qtiles):
                            ht, hb = nxt
                            if gi_ + 1 < len(qtiles):
                                nxt = hsrc.load(qtiles[gi_ + 1])
                            t0, n = gtile(g)
                            nsub = n // 128
                            pog, pogf = bank()
                            proj_fm(ht, hb, O_OG, 64, n, pog, pogf)
                            S.op("act", lambda pog=pog, n=n: A.activation(out=ogs[:, 0:n], in_=pog[0:64, 0:n], func=AF.Silu), reads=[pogf], writes=[B_og])
                            po, pof = bank()
                            for sub in range(nsub):
                                cidx = g * 4 + sub
                                tk = slice(t0 + sub * 128, t0 + (sub + 1) * 128)
                                k2 = sub % 2
                                pa, paf = bank()
                                S.op("act", lambda k2=k2, tk=tk: A.copy(out=qFt[k2][0:32, :], in_=qg_all[0:32, tk]), reads=[B_qg], writes=[B_qF[k2]])
                                S.op("act", lambda k2=k2, tk=tk: A.copy(out=qBt[k2][32:64, :], in_=qg_all[32:64, tk]), reads=[B_qg], writes=[B_qB[k2]])
                                S.op("pe", lambda pa=pa, tk=tk, k2=k2: P.matmul(pa[:, 0:128], lhsT=kg_all[:, tk], rhs=qFt[k2][:], start=True, stop=True),
                                     reads=[B_kg, B_qF[k2]], writes=[paf])
                                S.op("pe", lambda pa=pa, tk=tk, k2=k2: P.matmul(pa[:, 128:256], lhsT=kg_all[:, tk], rhs=qBt[k2][:], start=True, stop=True),
                                     reads=[B_kg, B_qB[k2]], writes=[paf])
                                S.op("dve", lambda pa=pa, k2=k2: V.tensor_tensor(out=Af[k2][:], in0=pa[:, 0:128], in1=C("maskF"), op=ALU.mult),
                                     reads=[paf, B_cstb], writes=[B_Af[k2]])
                                S.op("dve", lambda pa=pa, k2=k2: V.tensor_tensor(out=Ab[k2][:], in0=pa[:, 128:256], in1=C("maskB"), op=ALU.mult),
                                     reads=[paf, B_cstb], writes=[B_Ab[k2]])
                                osl = po[0:64, sub * 128:(sub + 1) * 128]
                                S.op("pe", lambda osl=osl, cidx=cidx, k2=k2: P.matmul(osl, lhsT=v_tok[:, cidx, :], rhs=Af[k2][:], start=True, stop=False),
                                     reads=[B_vt, B_Af[k2]], writes=[pof])
                                S.op("pe", lambda osl=osl, cidx=cidx, k2=k2: P.matmul(osl, lhsT=v_tok[:, cidx, :], rhs=Ab[k2][:], start=False, stop=False),
                                     reads=[B_vt, B_Ab[k2]], writes=[pof])
                                S.op("pe", lambda osl=osl, cidx=cidx, tk=tk: P.matmul(osl, lhsT=Sprev[:, cidx, :], rhs=qg_all[:, tk], start=False, stop=True),
                                     reads=[B_Sp, B_qg], writes=[pof])
                            col_rstd(lambda c, po=po, n=n: po[0:64, 0:n], [pof], 64, 1, n, 1.0 / 64)
                            S.op("dve", lambda po=po, n=n: V.scalar_tensor_tensor(out=yn[:, 0:n], in0=po[0:64, 0:n], scalar=hpf[0:64, li, H_GN:H_GN + 1],
                                                                                in1=rstd[0:64, 0:n], op0=ALU.mult, op1=ALU.mult),
                                 reads=[pof, B_hp, B_rstd], writes=[B_yn])
                            store_y(64, t0, n, lambda dst, db, n=n: S.op("pool", lambda: G.tensor_tensor(
                                out=dst, in0=yn[:, 0:n], in1=ogs[:, 0:n], op=ALU.mult), reads=[B_yn, B_og], writes=[db]))
                collective(yxp[1], ygp[1], B_yxp[1], B_ygp[1])
                if stop == "gla":
                    return


        oh = sb("oh", [128, 4], F32)
        selI = sb("selI", [128, 4, 128], BF16)
        B_sel = Buf("sel")
        S.dma(ch_c[5], oh[:], oh_d, writes=[B_sel])
        for d_ in range(4):
            S.op("dve", lambda d_=d_: V.tensor_scalar(out=selI[:, d_, :], in0=CF("ident", 128, 0, 128), scalar1=oh[:, d_:d_ + 1],
                                                   scalar2=None, op0=ALU.mult), reads=[B_sel, B_cstf], writes=[B_sel])

        def stage_c(li, tiles):
            with Scope() as st:
                wob = sb("wob", [128, 8, D], BF16, st)
                fwb = sb("fwb", [128, 2, 256], BF16, st)
                B_wo, B_fw = Buf("wob"), Buf("fwb")
                with Scope() as s2:
                    wos = sb("wos", [128, 8, D], F32, s2)
                    fws = sb("fws", [128, 2, 256], F32, s2)
                    B_wos, B_fws = Buf("wos"), Buf("fws")
                    S.dma(ch_c[4], wos[:], wout_d[li].rearrange("(k p) n -> p k n", p=128), writes=[B_wos])
                    S.dma(ch_c[5], fws[:], fw_d[li].rearrange("(k p) n -> p k n", p=128), writes=[B_fws])
                    S.op("act", lambda: A.copy(out=wob[:, 0:4, :], in_=wos[:, 0:4, :]), reads=[B_wos], writes=[B_wo])
                    S.op("dve", lambda: V.tensor_copy(out=wob[:, 4:8, :], in_=wos[:, 4:8, :]), reads=[B_wos], writes=[B_wo])
                    S.op("act", lambda: A.copy(out=fwb[:], in_=fws[:]), reads=[B_fws], writes=[B_fw])
                yc_all = [sb("yc%d" % i, [128, 8, 512], BF16, st) for i in range(8)]
                B_yc_all = [Buf("yc%d" % i) for i in range(8)]
                ch_y_all = [S.chan("y%d" % i) for i in range(8)]
                ysel = sb("ysel", [128, 8, 512], BF16, st)
                B_ysel = Buf("ysel")
                yf = sb("yf", [128, 2, 512], BF16, st)
                B_yf = Buf("yf")
                for tci, ti in enumerate(tiles):
                    t0, n, kind = TT[ti]
                    yc = yc_all[(tci % 2) * 4:(tci % 2) * 4 + 4]
                    B_yc = B_yc_all[(tci % 2) * 4:(tci % 2) * 4 + 4]
                    ch_y = ch_y_all[(tci % 2) * 4:(tci % 2) * 4 + 4]
                    for d_ in range(4):
                        for m_ in range(4):
                            for qp in range(2):
                                kc = m_ * 2 + qp
                                if kind == 0:
                                    S.dma(ch_y[d_], yc[d_][:, kc, 0:n], ygp[m_][qp * 128:(qp + 1) * 128, d_ * 2048 + t0:d_ * 2048 + t0 + n],
                                          reads=[B_ygp[m_]], writes=[B_yc[d_]], nowait=(kc > 0))
                                else:
                                    for ql in range(2):
                                        r0_ = (2 * qp + ql) * 256 + m_ * 64
                                        S.dma(ch_y[d_], yc[d_][ql * 64:(ql + 1) * 64, kc, 0:n], ygp[4][r0_:r0_ + 64, d_ * 64:d_ * 64 + n],
                                              reads=[B_ygp[4]], writes=[B_yc[d_]], nowait=(kc > 0 or ql > 0))
                    for c in range(8):
                        pb, pbf = bank()
                        for d_ in range(4):
                            S.op("pe", lambda pb=pb, d_=d_, c=c: P.matmul(pb[:, 0:n], lhsT=selI[:, d_, :], rhs=yc[d_][:, c, 0:n],
                                                                          start=(d_ == 0), stop=(d_ == 3)), reads=[B_sel, B_yc[d_]], writes=[pbf])
                        if c % 2 == 0:
                            S.op("act", lambda pb=pb, c=c: A.copy(out=ysel[:, c, 0:n], in_=pb[:, 0:n]), reads=[pbf], writes=[B_ysel])
                        else:
                            S.op("dve", lambda pb=pb, c=c: V.tensor_copy(out=ysel[:, c, 0:n], in_=pb[:, 0:n]), reads=[pbf], writes=[B_ysel])
                    for dc in range(2):
                        pb, pbf = bank()
                        for kc in range(2):
                            S.op("pe", lambda pb=pb, kc=kc, dc=dc: P.matmul(pb[:, 0:n], lhsT=fwb[:, kc, dc * 128:(dc + 1) * 128], rhs=ysel[:, 6 + kc, 0:n],
                                                                            start=(kc == 0), stop=(kc == 1)), reads=[B_fw, B_ysel], writes=[pbf])
                        S.op("act", lambda pb=pb, dc=dc: A.copy(out=yf[:, dc, 0:n], in_=pb[:, 0:n]), reads=[pbf], writes=[B_yf])
                    for c in range(8):
                        po, pof = bank()
                        for kc in range(8):
                            rhs = ysel[:, kc, 0:n] if kc < 6 else yf[:, kc - 6, 0:n]
                            S.op("pe", lambda po=po, kc=kc, c=c, rhs=rhs: P.matmul(po[:, 0:n], lhsT=wob[:, kc, c * 128:(c + 1) * 128], rhs=rhs,
                                                                                  start=(kc == 0), stop=(kc == 7)), reads=[B_wo, B_ysel, B_yf], writes=[pof])
                        S.op("dve", lambda po=po, c=c: V.scalar_tensor_tensor(
                            out=xT[:, c, t0:t0 + n], in0=po[:, 0:n], scalar=modG[:, li, 1, c, kind:kind + 1],
                            in1=xT[:, c, t0:t0 + n], op0=ALU.mult, op1=ALU.add), reads=[pof, B_mod, B_x[ti][c]], writes=[B_x[ti][c]])

        done = False
        for li in range(nlayers):
            lastl = (li == nlayers - 1)
            with Scope() as stA:
                hT = sb("hT", [128, 8, NT], BF16, stA)
                B_h = [Buf("h%d" % t) for t in range(len(TT))]
                half_ffn(li, 0, range(5), hT, B_h, stA)
                if li == 0:
                    dump("x_ffn1", xT[:], sum(B_x, []))
                for ti in range(5):
                    t0, n, kind = TT[ti]
                    adaln(li, 1, ti, lambda c, t0=t0, n=n: hT[:, c, t0:t0 + n], [B_h[ti]])
                    S.dma(ch_st[ti % 2], hxp[ti].rearrange("(c p) n -> p c n", p=128), hT[:, :, t0:t0 + n],
                          reads=[B_h[ti]], writes=[B_hxp[ti]])
                    collective(hxp[ti], hgp[ti], B_hxp[ti], B_hgp[ti])
            if li == 0:
                dump("hg0", hgp[0], [B_hgp[0]])
            if stop == "A":
                break
            stage_b(li)
            if not lastl or nlayers == 1:
                collective(yxp[4], ygp[4], B_yxp[4], B_ygp[4])
            if stop is not None and stop != "C":
                break
            ctiles = range(5) if (not lastl or nlayers == 1) else range(4)
            stage_c(li, ctiles)
            if li == 0:
                dump("x_mix", xT[:], sum(B_x, []))
            if stop == "C":
                break
            with Scope() as stA:
                hT = sb("hT", [128, 8, NT], BF16, stA)
                B_h = [Buf("h%d" % t) for t in range(len(TT))]
                half_ffn(li, 1, ctiles, hT, B_h, stA)
            if li == 0:
                dump("x_out", xT[:], sum(B_x, []))
            done = lastl

        B_out = Buf("yout")
        if done:
            with Scope() as st:
                fin = sb("fin", [128, 8, 512], F32, st)
                B_fin = Buf("fin")
                ot = [sb("ot%d" % i, [128, D], F32, st) for i in range(2)]
                B_ot = [Buf("ot0"), Buf("ot1")]
                for ti in range(4):
                    t0, n, kind = TT[ti]
                    col_rstd(lambda c: xT[:, c, t0:t0 + n], list(B_x[ti]), 128, 8, n, 1.0 / D)
                    for c in range(8):
                        S.op("dve", lambda c=c: V.scalar_tensor_tensor(
                            out=fin[:, c, 0:n], in0=xT[:, c, t0:t0 + n], scalar=vecs[:, V_FN + c:V_FN + c + 1], in1=rstd[:, 0:n],
                            op0=ALU.mult, op1=ALU.mult), reads=[B_x[ti][c], B_vecs, B_rstd], writes=[B_fin])
                    for sub in range(4):
                        k = (ti * 4 + sub) % 2
                        for half in range(2):
                            pb, pbf = bank()
                            for c4 in range(4):
                                c = half * 4 + c4
                                S.op("pe", lambda c=c, c4=c4, pb=pb, sub=sub: P.transpose(
                                    out=pb[:, c4 * 128:(c4 + 1) * 128], in_=fin[:, c, sub * 128:(sub + 1) * 128],
                                    identity=CF("ident", 128, 0, 128)), reads=[B_fin, B_cstf], writes=[pbf])
                            if half == 0:
                                S.op("dve", lambda pb=pb, k=k: V.tensor_copy(out=ot[k][:, 0:512], in_=pb[:, :]), reads=[pbf], writes=[B_ot[k]])
                            else:
                                S.op("act", lambda pb=pb, k=k: A.copy(out=ot[k][:, 512:1024], in_=pb[:, :]), reads=[pbf], writes=[B_ot[k]])
                        r0 = t0 + sub * 128
                        S.dma(ch_st[2 + k], yout[r0:r0 + 128, :], ot[k][:, :], reads=[B_ot[k]], writes=[B_out])

        S.barrier()
        print("instructions", S.n_ins, "waits", S.n_wait, "chans", S.nchan)
    return nc


def _consts(r):
    c = np.zeros((128, NCST), np.float32)

    def put(name, arr, parts=None):
        o, w = _c[name]
        arr = np.asarray(arr, np.float32)
        c[0:arr.shape[0], o:o + arr.shape[1]] = arr

    put("ident", np.eye(128))
    put("ones", np.ones((128, 128)))
    jj, ii = np.meshgrid(np.arange(128), np.arange(128), indexing="ij")
    put("maskF", (jj <= ii))
    put("maskB", (jj >= ii))
    w = (2, 4, 8, 16)[r]
    Nn = 512
    t = np.arange(Nn)
    lo = np.clip(t - w // 2, 0, Nn)
    hi = np.clip(t + w - w // 2, 0, Nn)
    s = np.arange(Nn)[:, None]
    band = ((s >= lo[None, :]) & (s < hi[None, :])) / (hi - lo)[None, :] - np.eye(Nn)
    blocks = [band[0:128, 128:256], band[128:256, 128:256], band[256:384, 128:256], band[0:128, 0:128],
              band[384:512, 384:512]]
    put("band", np.concatenate(blocks, axis=1))
    a64 = 2 * np.pi * np.outer(np.arange(64), np.arange(64)) / 64
    a128 = 2 * np.pi * np.outer(np.arange(128), np.arange(128)) / 128
    put("dA1", np.concatenate([np.cos(a64), -np.sin(a64)], 1))
    put("dA2", np.concatenate([np.sin(a64), np.cos(a64)], 1))
    put("C128", np.cos(a128))
    put("S128", np.sin(a128))
    tw = 2 * np.pi * np.outer(np.arange(128), np.arange(64)) / 8192
    put("Tc", np.cos(tw))
    put("Ts", np.sin(tw))
    put("chC", np.cos(a64))
    put("chS", -np.sin(a64))
    pm = np.zeros((64, 64))
    for d in range(64):
        sec, q = divmod(d, 32)
        partner = sec * 32 + (q + 16) % 32
        pm[partner, d] = 1.0
    put("perm", pm)
    a256 = 2 * np.pi * np.outer(np.arange(256), np.arange(256)) / 256
    put("C256", np.cos(a256).reshape(2, 128, 256).transpose(1, 0, 2).reshape(128, 512))
    put("S256", np.sin(a256).reshape(2, 128, 256).transpose(1, 0, 2).reshape(128, 512))
    return c


def _rope():
    freqs = 10000.0 ** (-np.arange(16, dtype=np.float32) / 16)
    tok = np.arange(NLAT)
    row = (tok // 64).astype(np.float32)
    col = (tok % 64).astype(np.float32)
    out = np.zeros((2, 64, NLAT), np.float32)
    for d in range(64):
        sec, q = divmod(d, 32)
        pos = row if sec == 0 else col
        ang = pos * freqs[q % 16]
        out[0, d] = np.cos(ang)
        out[1, d] = np.sin(ang) * (-1.0 if q < 16 else 1.0)
    return out


def _fm(v):
    return np.ascontiguousarray(np.asarray(v, np.float32).reshape(8, 128).T)


def make_inputs(x, c, ctx, c_ctx, w_mod, b_mod, norm_g, ffn_wg, ffn_wu, ffn_wd, w_in, w_out,
                pool_w, pool_scale, gla_wa, gla_ba, gla_norm, att_qnorm, att_knorm, fnet_w, final_norm):
    f = lambda a: np.ascontiguousarray(np.asarray(a, np.float32))
    x, c, ctx, c_ctx, w_mod, b_mod, norm_g = map(f, (x, c, ctx, c_ctx, w_mod, b_mod, norm_g))
    ffn_wg, ffn_wu, ffn_wd, w_in, w_out = map(f, (ffn_wg, ffn_wu, ffn_wd, w_in, w_out))
    pool_w, pool_scale, gla_wa, gla_ba, gla_norm = map(f, (pool_w, pool_scale, gla_wa, gla_ba, gla_norm))
    att_qnorm, att_knorm, fnet_w, final_norm = map(f, (att_qnorm, att_knorm, fnet_w, final_norm))
    rope = _rope()
    maps = []
    for core in range(8):
        b, r = divmod(core, 4)
        xin = np.concatenate([x[b, r * TL:(r + 1) * TL], ctx[b, r * TC:(r + 1) * TC], np.zeros((64, D), np.float32)], 0)
        vecs = np.zeros((128, NVEC), np.float32)
        vecs[:, V_C:V_C + 16] = np.stack([_fm(c[b]), _fm(c_ctx)], -1).reshape(128, 16)
        for li in range(2):
            vecs[:, V_BMOD + li * 18:V_BMOD + (li + 1) * 18] = b_mod[li, r * 2304:(r + 1) * 2304].reshape(18, 128).T
            for n3 in range(3):
                vecs[:, V_NG + (li * 3 + n3) * 8:V_NG + (li * 3 + n3) * 8 + 8] = _fm(norm_g[li, n3])
        vecs[:, V_FN:V_FN + 8] = _fm(final_norm)
        kv = r // 2
        cols = np.concatenate([
            np.arange(r * 64, r * 64 + 64),
            256 + r * 32 + np.arange(32), 256 + r * 32 + np.arange(32),
            384 + r * 32 + np.arange(32), 384 + r * 32 + np.arange(32),
            768 + np.arange(32),
            800 + r * 64 + np.arange(64),
            512 + r * 64 + np.arange(64),
            1056 + r * 64 + np.arange(64),
            1312 + kv * 64 + np.arange(64),
            1440 + kv * 64 + np.arange(64),
            1568 + r * 64 + np.arange(64)])
        assert cols.size == NHC
        hp = np.zeros((2, 128, NHP), np.float32)
        for li in range(2):
            hp[li, 0:64, H_PW:H_PW + 64] = pool_w[li, r]
            hp[li, 0:64, H_PS] = pool_scale[li, r * 64:(r + 1) * 64]
            hp[li, 0:16, H_WA:H_WA + 32] = gla_wa[li, 0][:, r * 32:(r + 1) * 32]
            hp[li, 16:32, H_WA + 32:H_WA + 64] = gla_wa[li, 1][:, r * 32:(r + 1) * 32]
            hp[li, 0:32, H_BA] = gla_ba[li, 0, r * 32:(r + 1) * 32]
            hp[li, 32:64, H_BA] = gla_ba[li, 1, r * 32:(r + 1) * 32]
            hp[li, 0:64, H_GN] = gla_norm[li]
            hp[li, 0:64, H_QN] = att_qnorm[li]
            hp[li, 0:64, H_KN] = att_knorm[li]
        maps.append({
            "oh": np.tile(np.eye(4, dtype=np.float32)[r][None, :], (128, 1)),
            "xin": np.ascontiguousarray(xin), "vecs": vecs, "cst": _consts(r), "rope": rope, "hp": hp,
            "w_mod_s": np.ascontiguousarray(w_mod[:, :, r * 2304:(r + 1) * 2304]), "ffn_wg": ffn_wg, "ffn_wu": ffn_wu, "ffn_wd": ffn_wd,
            "w_in_h": np.ascontiguousarray(w_in[:, :, cols]), "w_out": w_out, "fnet_w": fnet_w,
        })
    return maps


_NC = None


def kernel(**inputs):
    global _NC
    if _NC is None:
        _NC = build()
    maps = make_inputs(**inputs)
    res = run_bass_kernel_spmd(_NC, maps, core_ids=list(range(8)))
    out = np.zeros((2, NLAT, D), np.float32)
    for core in range(8):
        b, r = divmod(core, 4)
        out[b, r * TL:(r + 1) * TL] = res.results[core]["yout"]
    return out
```

```python
import numpy as np
from contextlib import ExitStack
import concourse.bass as bass
import concourse.mybir as mybir
from concourse.bass_utils import run_bass_kernel_spmd

F32 = mybir.dt.float32
BF16 = mybir.dt.bfloat16
F32R = mybir.dt.float32r
AF = mybir.ActivationFunctionType
ALU = mybir.AluOpType

D = 1024
NLAT = 8192
NCTX = 256
NTOK = NLAT + NCTX
TL = 2048
TC = 64
NT = TL + TC
DFF = 2816
NJ = DFF // 128
EPS = 1e-6
NHC = 608
TT = [(0, 512, 0), (512, 512, 0), (1024, 512, 0), (1536, 512, 0), (2048, 64, 1)]

_c = {}
_o = 0
for _n, _w in [("ident", 128), ("ones", 128), ("maskF", 128), ("maskB", 128), ("band", 640),
               ("dA1", 128), ("dA2", 128), ("C128", 128), ("S128", 128), ("Tc", 64), ("Ts", 64),
               ("chC", 64), ("chS", 64), ("perm", 64), ("C256", 512), ("S256", 512)]:
    _c[_n] = (_o, _w)
    _o += _w
NCST = _o
V_C = 0
V_BMOD = 32
V_NG = 160
V_FN = 208
NVEC = 216
H_PW = 0
H_PS = 64
H_WA = 65
H_BA = 129
H_GN = 130
H_QN = 131
H_KN = 132
NHP = 133


class Buf:
    __slots__ = ("name", "w", "r")

    def __init__(self, name=""):
        self.name = name
        self.w = None
        self.r = []


class Chan:
    __slots__ = ("sem", "count", "key")

    def __init__(self, sem, key):
        self.sem = sem
        self.count = 0
        self.key = key


class Sched:
    COMPUTE = ("pe", "act", "dve", "pool")

    def __init__(self, nc, es):
        self.nc = nc
        self.es = es
        self.eng = {"pe": nc.tensor, "act": nc.scalar, "dve": nc.vector, "pool": nc.gpsimd, "sp": nc.sync}
        self.sem = {}
        self.cnt = {}
        for e in self.COMPUTE:
            self.sem[e] = es.enter_context(nc.semaphore("s_" + e))
            self.cnt[e] = 0
        self.waited = {e: {} for e in self.eng}
        self.nchan = 0
        self.chans = []
        self.extra = []
        self.need_barrier = False
        self.n_wait = 0
        self.n_ins = 0

    def chan(self, name):
        s = self.es.enter_context(self.nc.semaphore("c_%s_%d" % (name, self.nchan)))
        self.nchan += 1
        c = Chan(s, "c%d" % self.nchan)
        self.chans.append(c)
        return c

    def barrier(self):
        for e in self.eng:
            for e2 in self.COMPUTE:
                self._wait(e, (e2, self.sem[e2], self.cnt[e2]))
            for c in self.chans:
                self._wait(e, (c.key, c.sem, c.count))

    def _wait(self, e, tok):
        if tok is None:
            return
        key, sem, val = tok
        if val <= 0 or self.waited[e].get(key, 0) >= val:
            return
        self.eng[e].wait_ge(sem, val)
        self.waited[e][key] = val
        self.n_wait += 1

    def _deps(self, e, reads, writes, skip=None):
        for b in reads:
            if b.w is not None and b.w[0] != skip:
                self._wait(e, b.w)
        for b in writes:
            if b.w is not None and b.w[0] != skip:
                self._wait(e, b.w)
            for t in b.r:
                if t[0] != skip:
                    self._wait(e, t)

    def _mark(self, tok, reads, writes):
        for b in reads:
            b.r.append(tok)
            if len(b.r) > 16:
                d = {}
                for t in b.r:
                    if t[0] not in d or d[t[0]][2] < t[2]:
                        d[t[0]] = t
                b.r = list(d.values())
        for b in writes:
            b.w = tok
            b.r = []

    def op(self, e, fn, reads=(), writes=()):
        self._deps(e, reads, writes, skip="pe" if e == "pe" else None)
        ins = fn()
        self.cnt[e] += 1
        ins.then_inc(self.sem[e], 1)
        tok = (e, self.sem[e], self.cnt[e])
        self._mark(tok, reads, writes)
        self.n_ins += 1
        return tok

    def dma(self, ch, out, in_, reads=(), writes=(), q="sp", nowait=False, **kw):
        if not nowait:
            self._wait(q, (ch.key, ch.sem, ch.count))
        self._deps(q, reads, writes, skip=ch.key if nowait else None)
        ins = self.eng[q].dma_start(out=out, in_=in_, **kw)
        ch.count += 16
        ins.then_inc(ch.sem, 16)
        tok = (ch.key, ch.sem, ch.count)
        self._mark(tok, reads, writes)
        self.n_ins += 1
        return tok

    def finish(self, q, bufs):
        for b in bufs:
            self._wait(q, b.w)
            for t in b.r:
                self._wait(q, t)


def build(nlayers=2, stop=None, dbg=()):
    nc = bass.Bass("TRN2", target_bir_lowering=False)

    def din(name, shape, dt=F32):
        return nc.dram_tensor(name, list(shape), dt, kind="ExternalInput").ap()

    xin = din("xin", [NT + 64, D])
    vecs_d = din("vecs", [128, NVEC])
    cst_d = din("cst", [128, NCST])
    rope_d = din("rope", [2, 64, NLAT])
    hp_d = din("hp", [2, 128, NHP])
    wmod_d = din("w_mod_s", [2, D, 2304])
    wg_d = din("ffn_wg", [2, 2, D, DFF])
    wu_d = din("ffn_wu", [2, 2, D, DFF])
    wd_d = din("ffn_wd", [2, 2, DFF, D])
    winh_d = din("w_in_h", [2, D, NHC])
    wout_d = din("w_out", [2, D, D])
    fw_d = din("fnet_w", [2, 256, 256])
    yout = nc.dram_tensor("yout", [TL, D], F32, kind="ExternalOutput").ap()
    dbg_out = {}
    for name, shape, dt in dbg:
        dbg_out[name] = nc.dram_tensor("dbg_" + name, list(shape), dt, kind="ExternalOutput").ap()
    HN = [512, 512, 512, 512, 64]
    hxp = [nc.dram_tensor("hx%d" % t, [D, HN[t]], BF16).ap() for t in range(5)]
    hgp = [nc.dram_tensor("hg%d" % t, [4 * D, HN[t]], BF16).ap() for t in range(5)]
    yxp = [nc.dram_tensor("yx%d" % t, [64, NLAT], BF16).ap() for t in range(4)] + [nc.dram_tensor("yx4", [256, NCTX], BF16).ap()]
    ygp = [nc.dram_tensor("yg%d" % t, [4 * 64, NLAT], BF16).ap() for t in range(4)] + [nc.dram_tensor("yg4", [4 * 256, NCTX], BF16).ap()]
    fsc = nc.dram_tensor("fsc", [2, 64, NLAT], BF16).ap()
    B_hxp = [Buf("hx%d" % t) for t in range(5)]
    B_hgp = [Buf("hg%d" % t) for t in range(5)]
    B_yxp = [Buf("yx%d" % t) for t in range(5)]
    B_ygp = [Buf("yg%d" % t) for t in range(5)]
    B_fsc = Buf("fsc")
    oh_d = din("oh", [128, 4])

    es = ExitStack()

    class Scope(ExitStack):
        def __exit__(self, *a):
            S.need_barrier = True
            return super().__exit__(*a)

    with es:
        S = Sched(nc, es)
        cc_sem = es.enter_context(nc.semaphore("cc"))
        cc_n = [0]

        uid = [0]

        def sb(name, shape, dt, stack=es):
            uid[0] += 1
            if S.need_barrier:
                S.barrier()
                S.need_barrier = False
            return stack.enter_context(nc.sbuf_tensor("s%d_%s" % (uid[0], name), list(shape), dt))

        V, A, P, G = nc.vector, nc.scalar, nc.tensor, nc.gpsimd

        pbank = [es.enter_context(nc.psum_tensor("pb%d" % i, [128, 512], F32)) for i in range(8)]
        pbuf = [Buf("pb%d" % i) for i in range(8)]
        prr = [0]
        nrot = [8]

        def bank():
            i = prr[0] % nrot[0]
            prr[0] += 1
            return pbank[i], pbuf[i]

        xT = sb("xT", [128, 8, NT], F32)
        B_x = [[Buf("x%d_%d" % (t, c)) for c in range(8)] for t in range(len(TT))]
        vecs = sb("vecs", [128, NVEC], F32)
        B_vecs = Buf("vecs")
        cstb = sb("cstb", [128, NCST], BF16)
        B_cstb = Buf("cstb")
        cstf = sb("cstf", [128, 320], F32)
        B_cstf = Buf("cstf")
        onesf = sb("onesf", [128, 128], F32)
        hpf = sb("hpf", [128, 2, NHP], F32)
        hpb = sb("hpb", [128, 2, NHP], BF16)
        B_hp = Buf("hp")
        modT = sb("modT", [128, 2, 72, 2], F32)
        B_mod = Buf("mod")
        epsb = sb("epsb", [128, 1], F32)
        B_eps = Buf("eps")
        modA = sb("modA", [128, 2, 3, 8, 2], F32)
        modG = sb("modG", [128, 2, 3, 8, 2], F32)
        sq = sb("sq", [128, 2, 512], BF16)
        B_sq = [Buf("sq0"), Buf("sq1")]
        rstd = sb("rstd", [128, 512], F32)
        B_rstd = Buf("rstd")
        hs = sb("hs", [128, 2, 512], F32)
        B_hs = [Buf("hs0"), Buf("hs1")]
        ch_c = [S.chan("k%d" % i) for i in range(6)]
        ch_st = [S.chan("st%d" % i) for i in range(4)]
        CFO = {"ident": 0, "Tc": 128, "Ts": 192, "perm": 256}

        S.dma(ch_c[0], vecs[:], vecs_d, writes=[B_vecs])
        S.dma(ch_c[2], hpf[:], hp_d.rearrange("l p n -> p l n"), writes=[B_hp])
        for i, nm in enumerate(["ident", "Tc", "Ts", "perm"]):
            o, w = _c[nm]
            S.dma(ch_c[3], cstf[:, CFO[nm]:CFO[nm] + w], cst_d[:, o:o + w], writes=[B_cstf])
        with Scope() as st:
            cst = sb("cst", [128, NCST], F32, st)
            B_cst = Buf("cst")
            S.dma(ch_c[1], cst[:], cst_d, writes=[B_cst])
            S.op("dve", lambda: V.tensor_copy(out=cstb[:], in_=cst[:]), reads=[B_cst], writes=[B_cstb])
        S.op("dve", lambda: V.tensor_copy(out=hpb[:], in_=hpf[:]), reads=[B_hp], writes=[B_hp])
        S.op("pool", lambda: G.memset(epsb[:], EPS), writes=[B_eps])
        S.op("pool", lambda: G.memset(onesf[:], 1.0), writes=[B_cstf])

        def C(name, parts=128, lo=0, hi=None):
            o, w = _c[name]
            hi = w if hi is None else hi
            return cstb[0:parts, o + lo:o + hi]

        def CF(name, parts=128, lo=0, hi=None):
            o = CFO[name]
            return cstf[0:parts, o + lo:o + hi]

        with Scope() as st:
            xs = [sb("xs%d" % i, [128, D], F32, st) for i in range(2)]
            B_xs = [Buf("xs0"), Buf("xs1")]
            ch_x = [S.chan("x0"), S.chan("x1")]
            nblk = (NT + 127) // 128
            for bi in range(nblk):
                r0 = bi * 128
                n = min(128, NT - r0)
                k = bi % 2
                S.dma(ch_x[k], xs[k][:, :], xin[r0:r0 + 128, :], writes=[B_xs[k]])
                ti = min(r0 // 512, 4)
                for half in range(2):
                    pb, pbf = bank()
                    for c4 in range(4):
                        c = half * 4 + c4
                        S.op("pe", lambda c=c, c4=c4, pb=pb, k=k, n=n: P.transpose(
                            out=pb[:, c4 * 128:c4 * 128 + 128], in_=xs[k][:, c * 128:(c + 1) * 128],
                            identity=CF("ident", 128, 0, 128)), reads=[B_xs[k], B_cstf], writes=[pbf])
                    src = pb[:, :].rearrange("p (c n) -> p c n", c=4)[:, :, 0:n]
                    dst = xT[:, half * 4:half * 4 + 4, r0:r0 + n]
                    if half == 0:
                        S.op("dve", lambda src=src, dst=dst: V.tensor_copy(out=dst, in_=src), reads=[pbf], writes=B_x[ti][0:4])
                    else:
                        S.op("act", lambda src=src, dst=dst: A.copy(out=dst, in_=src), reads=[pbf], writes=B_x[ti][4:8])

        def collective(src, dst, Bsrc, Bdst):
            S._deps("pool", [Bsrc], [Bdst])
            ins = G.collective_compute("AllGather", ALU.bypass, replica_groups=[[0, 1, 2, 3], [4, 5, 6, 7]],
                                       ins=[src], outs=[dst], dma_qos="P3")
            cc_n[0] += 1
            ins.then_inc(cc_sem, 1)
            S._mark(("cc", cc_sem, cc_n[0]), [Bsrc], [Bdst])
            S.extra = [("cc", cc_sem, cc_n[0])]

        sc = sb("sc", [128, 8, 2], F32)
        S.op("act", lambda: A.activation(out=sc[:].rearrange("p a b -> p (a b)"), in_=vecs[:, V_C:V_C + 16], func=AF.Silu),
             reads=[B_vecs], writes=[B_mod])
        mx = nc.dram_tensor("mx", [128, 72], F32).ap()
        mg = nc.dram_tensor("mg", [4 * 128, 72], F32).ap()
        B_mx, B_mg = Buf("mx"), Buf("mg")
        with Scope() as st:
            wm = [sb("wm%d" % i, [128, 8, 1152], F32, st) for i in range(2)]
            B_wm = [Buf("wm0"), Buf("wm1")]
            ch_w = [S.chan("wm0"), S.chan("wm1")]
            modS = sb("modS", [128, 2, 18, 2], F32, st)
            B_ms = Buf("modS")
            modAll = sb("modAll", [128, 4, 72], F32, st)
            B_ma = Buf("modAll")
            it = 0
            for li in range(2):
                for hf in range(2):
                    k = it % 2
                    it += 1
                    S.dma(ch_w[k], wm[k][:], wmod_d[li, :, hf * 1152:(hf + 1) * 1152].rearrange("(k p) n -> p k n", p=128), writes=[B_wm[k]])
                    pb, pbf = bank()
                    for cb in range(9):
                        for kk in range(8):
                            S.op("pe", lambda cb=cb, kk=kk, pb=pb, k=k: P.matmul(
                                pb[:, cb * 2:cb * 2 + 2], lhsT=wm[k][:, kk, cb * 128:(cb + 1) * 128], rhs=sc[:, kk, :],
                                start=(kk == 0), stop=(kk == 7)), reads=[B_wm[k], B_mod], writes=[pbf])
                    bm = vecs[:, V_BMOD + li * 18 + hf * 9:V_BMOD + li * 18 + hf * 9 + 9]
                    S.op("dve", lambda pb=pb, li=li, hf=hf, bm=bm: V.tensor_tensor(
                        out=modS[:, li, hf * 9:(hf + 1) * 9, :], in0=pb[:, 0:18].rearrange("p (c t) -> p c t", t=2),
                        in1=bm.unsqueeze(2).to_broadcast([128, 9, 2]), op=ALU.add), reads=[pbf, B_vecs], writes=[B_ms])
            S.dma(ch_c[3], mx, modS[:].rearrange("p a b c -> p (a b c)"), reads=[B_ms], writes=[B_mx])
            collective(mx, mg, B_mx, B_mg)
            S.dma(ch_c[3], modAll[:], mg.rearrange("(r p) n -> p r n", p=128), reads=[B_mg], writes=[B_ma])
            for li in range(nlayers):
                S.op("dve", lambda li=li: V.tensor_copy(
                    out=modT[:, li, :, :].rearrange("p (r c) v -> p r (c v)", r=4), in_=modAll[:, :, li * 36:(li + 1) * 36]),
                    reads=[B_ma], writes=[B_mod])

        def mod(li, j, c, kind):
            return modT[:, li, j * 8 + c, kind:kind + 1]

        for li in range(nlayers):
            for n3 in range(3):
                ng = vecs[:, V_NG + (li * 3 + n3) * 8:V_NG + (li * 3 + n3) * 8 + 8]
                jscale = n3 * 3 + 1
                jgate = n3 * 3 + 2
                S.op("dve", lambda li=li, n3=n3, ng=ng, jscale=jscale: V.scalar_tensor_tensor(
                    out=modA[:, li, n3, :, :], in0=modT[:, li, jscale * 8:(jscale + 1) * 8, :], scalar=1.0,
                    in1=ng.unsqueeze(2).to_broadcast([128, 8, 2]), op0=ALU.add, op1=ALU.mult),
                    reads=[B_mod, B_vecs], writes=[B_mod])
                S.op("dve", lambda li=li, n3=n3, jgate=jgate: V.tensor_scalar(
                    out=modG[:, li, n3, :, :], in0=modT[:, li, jgate * 8:(jgate + 1) * 8, :],
                    scalar1=(1.0 if n3 == 1 else 0.5), scalar2=None, op0=ALU.mult), reads=[B_mod], writes=[B_mod])

        sqi = [0]

        def col_rstd(src_fn, srcbufs, nparts, nch, n, inv_dim, rstd=rstd, B_rstd=B_rstd):
            pb, pbf = bank()
            for c in range(nch):
                s2 = sqi[0] % 2
                sqi[0] += 1
                S.op("act", lambda c=c, s2=s2: A.activation(out=sq[0:nparts, s2, 0:n], in_=src_fn(c), func=AF.Square),
                     reads=srcbufs, writes=[B_sq[s2]])
                S.op("pe", lambda c=c, pb=pb, s2=s2: P.matmul(pb[0:nparts, 0:n], lhsT=C("ones", nparts, 0, nparts),
                                                              rhs=sq[0:nparts, s2, 0:n], start=(c == 0), stop=(c == nch - 1)),
                     reads=[B_sq[s2], B_cstb], writes=[pbf])
            S.op("act", lambda pb=pb: A.activation(out=rstd[0:nparts, 0:n], in_=pb[0:nparts, 0:n], func=AF.Ln,
                                                   bias=epsb[0:nparts, :], scale=inv_dim),
                 reads=[pbf, B_eps], writes=[B_rstd])
            S.op("act", lambda: A.activation(out=rstd[0:nparts, 0:n], in_=rstd[0:nparts, 0:n], func=AF.Exp, scale=-0.5),
                 reads=[B_rstd], writes=[B_rstd])

        hsi = [0]

        def adaln(li, n3, ti, dst_fn, dstbufs):
            t0, n, kind = TT[ti]
            col_rstd(lambda c: xT[:, c, t0:t0 + n], list(B_x[ti]), 128, 8, n, 1.0 / D)
            for c in range(8):
                h2 = hsi[0] % 2
                hsi[0] += 1
                S.op("dve", lambda c=c, h2=h2: V.scalar_tensor_tensor(
                    out=hs[:, h2, 0:n], in0=xT[:, c, t0:t0 + n], scalar=modA[:, li, n3, c, kind:kind + 1],
                    in1=rstd[:, 0:n], op0=ALU.mult, op1=ALU.mult), reads=[B_x[ti][c], B_rstd, B_mod], writes=[B_hs[h2]])
                S.op("act", lambda c=c, h2=h2: A.activation(out=dst_fn(c), in_=hs[:, h2, 0:n], func=AF.Identity,
                                                            bias=mod(li, n3 * 3, c, kind), scale=1.0),
                     reads=[B_hs[h2], B_mod], writes=dstbufs)

        def half_ffn(li, hi, tiles, hT, B_h, st):
            n3 = 0 if hi == 0 else 2
            NB = 2
            wgs = sb("wgs", [128, 8, NB * 128], F32, st)
            wus = sb("wus", [128, 8, NB * 128], F32, st)
            wds = sb("wds", [128, NB, D], F32, st)
            wgb = [sb("wgb%d" % i, [128, 8, NB * 128], BF16, st) for i in range(2)]
            wub = [sb("wub%d" % i, [128, 8, NB * 128], BF16, st) for i in range(2)]
            wdb = [sb("wdb%d" % i, [128, NB, D], BF16, st) for i in range(2)]
            sg = [sb("sg%d" % i, [128, 512], F32, st) for i in range(2)]
            aT = [sb("aT%d" % i, [128, NB, 512], BF16, st) for i in range(2)]
            Bw = {k: Buf(k) for k in ["wgs", "wus", "wds"]}
            Bs = {k: [Buf(k + "0"), Buf(k + "1")] for k in ["wgb", "wub", "wdb", "sg", "aT"]}
            chs = {k: S.chan(k) for k in ["wgs", "wus", "wds"]}
            nblk = NJ // NB

            def load(jb):
                j0 = jb * NB * 128
                S.dma(chs["wgs"], wgs[:], wg_d[li, hi, :, j0:j0 + NB * 128].rearrange("(k p) n -> p k n", p=128), writes=[Bw["wgs"]])
                S.dma(chs["wus"], wus[:], wu_d[li, hi, :, j0:j0 + NB * 128].rearrange("(k p) n -> p k n", p=128), writes=[Bw["wus"]])
                S.dma(chs["wds"], wds[:], wd_d[li, hi, j0:j0 + NB * 128, :].rearrange("(j p) n -> p j n", p=128), writes=[Bw["wds"]])

            def cast(jb):
                k = jb % 2
                S.op("act", lambda: A.copy(out=wgb[k][:], in_=wgs[:]), reads=[Bw["wgs"]], writes=[Bs["wgb"][k]])
                S.op("act", lambda: A.copy(out=wub[k][:], in_=wus[:]), reads=[Bw["wus"]], writes=[Bs["wub"][k]])
                S.op("act", lambda: A.copy(out=wdb[k][:], in_=wds[:]), reads=[Bw["wds"]], writes=[Bs["wdb"][k]])

            load(0)
            cast(0)
            it = 0
            for jb in range(nblk):
                k = jb % 2
                if jb + 1 < nblk:
                    load(jb + 1)
                def stage1(ti, a):
                    t0, n, kind = TT[ti]
                    if jb == 0:
                        adaln(li, n3, ti, lambda c, t0=t0, n=n: hT[:, c, t0:t0 + n], [B_h[ti]])
                    for jj in range(NB):
                        pg, pgf = bank()
                        pu, puf = bank()
                        for kk in range(8):
                            S.op("pe", lambda kk=kk, pg=pg, jj=jj: P.matmul(
                                pg[:, 0:n], lhsT=wgb[k][:, kk, jj * 128:(jj + 1) * 128], rhs=hT[:, kk, t0:t0 + n],
                                start=(kk == 0), stop=(kk == 7)), reads=[Bs["wgb"][k], B_h[ti]], writes=[pgf])
                        for kk in range(8):
                            S.op("pe", lambda kk=kk, pu=pu, jj=jj: P.matmul(
                                pu[:, 0:n], lhsT=wub[k][:, kk, jj * 128:(jj + 1) * 128], rhs=hT[:, kk, t0:t0 + n],
                                start=(kk == 0), stop=(kk == 7)), reads=[Bs["wub"][k], B_h[ti]], writes=[puf])
                        s2 = jj % 2
                        S.op("act", lambda pg=pg, s2=s2: A.activation(out=sg[s2][:, 0:n], in_=pg[:, 0:n], func=AF.Silu),
                             reads=[pgf], writes=[Bs["sg"][s2]])
                        S.op("dve", lambda pu=pu, s2=s2, jj=jj: V.tensor_tensor(
                            out=aT[a][:, jj, 0:n], in0=pu[:, 0:n], in1=sg[s2][:, 0:n], op=ALU.mult),
                            reads=[puf, Bs["sg"][s2]], writes=[Bs["aT"][a]])

                def stage2(ti, a):
                    t0, n, kind = TT[ti]
                    for c in range(8):
                        po, pof = bank()
                        for jj in range(NB):
                            S.op("pe", lambda jj=jj, po=po, c=c: P.matmul(
                                po[:, 0:n], lhsT=wdb[k][:, jj, c * 128:(c + 1) * 128], rhs=aT[a][:, jj, 0:n],
                                start=(jj == 0), stop=(jj == NB - 1)), reads=[Bs["wdb"][k], Bs["aT"][a]], writes=[pof])
                        S.op("dve", lambda po=po, c=c: V.scalar_tensor_tensor(
                            out=xT[:, c, t0:t0 + n], in0=po[:, 0:n], scalar=modG[:, li, n3, c, kind:kind + 1],
                            in1=xT[:, c, t0:t0 + n], op0=ALU.mult, op1=ALU.add),
                            reads=[pof, B_mod, B_x[ti][c]], writes=[B_x[ti][c]])

                tl = list(tiles)
                stage1(tl[0], it % 2)
                for i_, ti in enumerate(tl):
                    a = it % 2
                    it += 1
                    if i_ + 1 < len(tl):
                        stage1(tl[i_ + 1], it % 2)
                    stage2(ti, a)
                    if i_ == min(1, len(tl) - 1) and jb + 1 < nblk:
                        cast(jb + 1)

        def dump(name, src_ap, bufs):
            if name in dbg_out:
                S.dma(ch_st[3], dbg_out[name], src_ap, reads=bufs, writes=[Buf()])

        O_PU, O_GQ, O_GK, O_R, O_OG, O_GV, O_AQ, O_AK, O_AV, O_FZ = 0, 64, 128, 192, 224, 288, 352, 416, 480, 544
        NG = 17
        GORDER = [q * 4 + t for t in range(4) for q in range(4)] + [16]

        def gtile(g):
            return (g * 512, 512) if g < 16 else (NLAT, NCTX)

        class HStream:
            def __init__(self, st):
                self.t = [sb("hst%d" % i, [128, 8, 512], BF16, st) for i in range(2)]
                self.b = [Buf("hst0"), Buf("hst1")]
                self.ch = [S.chan("hst0"), S.chan("hst1")]
                self.i = 0

            def load(self, g):
                k = self.i % 2
                self.i += 1
                if g < 16:
                    q, tt_ = g // 4, g % 4
                    S.dma(self.ch[k], self.t[k][:, :, :], hgp[tt_][q * D:(q + 1) * D, :].rearrange("(c p) n -> p c n", p=128),
                          reads=[B_hgp[tt_]], writes=[self.b[k]])
                else:
                    for q in range(4):
                        S.dma(self.ch[k], self.t[k][:, :, q * 64:(q + 1) * 64],
                              hgp[4][q * D:(q + 1) * D, :].rearrange("(c p) n -> p c n", p=128),
                              reads=[B_hgp[4]], writes=[self.b[k]], nowait=(q > 0))
                return self.t[k], self.b[k]

        def stage_b(li):
            ctx_out = (li < nlayers - 1) or (nlayers == 1)
            qtiles = list(range(NG)) if ctx_out else list(range(16))
            with Scope() as sB:
                winb = sb("winb", [128, 8, NHC], BF16, sB)
                B_win = Buf("winb")
                with Scope() as st:
                    wst = sb("wst", [128, 8, NHC], F32, st)
                    B_wst = Buf("wst")
                    S.dma(ch_c[4], wst[:], winh_d[li].rearrange("(k p) n -> p k n", p=128), writes=[B_wst])
                    S.op("act", lambda: A.copy(out=winb[:], in_=wst[:]), reads=[B_wst], writes=[B_win])
                ystage = [sb("ystage%d" % i, [128, 512], BF16, sB) for i in range(2)]
                B_ys = [Buf("ys0"), Buf("ys1")]
                ysi = [0]

                def proj_fm(ht, hb, col0, ncol, n, pb, pbf):
                    for kk in range(8):
                        S.op("pe", lambda kk=kk: P.matmul(pb[0:ncol, 0:n], lhsT=winb[:, kk, col0:col0 + ncol], rhs=ht[:, kk, 0:n],
                                                          start=(kk == 0), stop=(kk == 7)), reads=[B_win, hb], writes=[pbf])

                def proj_tm(ht, hb, col0, ncol, nsub, pb, pbf):
                    for sub in range(nsub):
                        for kk in range(8):
                            S.op("pe", lambda kk=kk, sub=sub: P.matmul(
                                pb[:, sub * ncol:(sub + 1) * ncol], lhsT=ht[:, kk, sub * 128:(sub + 1) * 128],
                                rhs=winb[:, kk, col0:col0 + ncol], start=(kk == 0), stop=(kk == 7)),
                                reads=[B_win, hb], writes=[pbf])

                def store_y(row0, tok0, n, fn):
                    k = ysi[0] % 2
                    ysi[0] += 1
                    fn(ystage[k][0:64, 0:n], B_ys[k])
                    m_ = row0 // 64
                    if tok0 < NLAT:
                        S.dma(ch_st[k], yxp[m_][0:64, tok0:tok0 + n], ystage[k][0:64, 0:n], reads=[B_ys[k]], writes=[B_yxp[m_]])
                    else:
                        S.dma(ch_st[k], yxp[4][row0:row0 + 64, 0:n], ystage[k][0:64, 0:n], reads=[B_ys[k]], writes=[B_yxp[4]])

                with Scope() as sAB:
                    u_tok = sb("u_tok", [128, 66, 64], BF16, sAB)
                    B_u = Buf("u_tok")
                    mT = [sb("mT%d" % i, [64, 512], BF16, sAB) for i in range(2)]
                    B_m = [Buf("mT0"), Buf("mT1")]
                    zT = [sb("zT%d" % i, [64, 512], BF16, sAB) for i in range(2)]
                    B_z = [Buf("z0"), Buf("z1")]
                    ust = [sb("ust%d" % i, [64, 2, 512], BF16, sAB) for i in range(2)]
                    B_us = [Buf("us0"), Buf("us1")]
                    ch_f = [S.chan("f0"), S.chan("f1")]
                    utok = sb("utok", [128, 2, 2, 64], BF16, sAB)
                    B_ut = Buf("utok")
                    st = sAB

                    def pool_proj(g, ht, hb):
                        t0, n = gtile(g)
                        nsub = n // 128
                        pb, pbf = bank()
                        proj_tm(ht, hb, O_PU, 64, nsub, pb, pbf)
                        S.op("act", lambda: A.copy(
                            out=u_tok[:, g * 4:g * 4 + nsub, :], in_=pb[:, 0:nsub * 64].rearrange("p (s c) -> p s c", c=64)),
                            reads=[pbf], writes=[B_u])

                    def fnet_proj(g, gi_, ht, hb):
                        if g == 16 and not ctx_out:
                            return
                        t0, n = gtile(g)
                        k = gi_ % 2
                        pb, pbf = bank()
                        proj_fm(ht, hb, O_FZ, 64, n, pb, pbf)
                        S.op("act", lambda pb=pb, k=k, n=n: A.copy(out=zT[k][:, 0:n], in_=pb[0:64, 0:n]), reads=[pbf], writes=[B_z[k]])
                        pr, prf = bank()
                        pi, pif = bank()
                        S.op("pe", lambda pr=pr, k=k, n=n: P.matmul(pr[0:64, 0:n], lhsT=C("chC", 64), rhs=zT[k][:, 0:n], start=True, stop=True),
                             reads=[B_cstb, B_z[k]], writes=[prf])
                        S.op("pe", lambda pi=pi, k=k, n=n: P.matmul(pi[0:64, 0:n], lhsT=C("chS", 64), rhs=zT[k][:, 0:n], start=True, stop=True),
                             reads=[B_cstb, B_z[k]], writes=[pif])
                        S.op("dve", lambda pr=pr, k=k, n=n: V.tensor_copy(out=ust[k][:, 0, 0:n], in_=pr[0:64, 0:n]), reads=[prf], writes=[B_us[k]])
                        S.op("act", lambda pi=pi, k=k, n=n: A.copy(out=ust[k][:, 1, 0:n], in_=pi[0:64, 0:n]), reads=[pif], writes=[B_us[k]])
                        if g < 16:
                            S.dma(ch_f[k], fsc[:, :, t0:t0 + n].rearrange("r c t -> c r t"), ust[k][:, :, 0:n], reads=[B_us[k]], writes=[B_fsc])
                        else:
                            pt, ptf = bank()
                            ptb = pt[:, :].bitcast(BF16)
                            for sub in range(2):
                                for ri in range(2):
                                    S.op("pe", lambda sub=sub, ri=ri: P.transpose(
                                        out=ptb[:, (sub * 2 + ri) * 64:(sub * 2 + ri + 1) * 64], in_=ust[k][:, ri, sub * 128:(sub + 1) * 128],
                                        identity=C("ident", 64, 0, 64)), reads=[B_us[k], B_cstb], writes=[ptf])
                            S.op("dve", lambda: V.tensor_copy(out=utok[:].rearrange("p a b c -> p (a b c)"), in_=ptb[:, 0:256]),
                                 reads=[ptf], writes=[B_ut])
                            px, pxf = bank()
                            i = 0
                            for sub in range(2):
                                for ri, nm in ((0, "C256"), (1, "S256")):
                                    S.op("pe", lambda sub=sub, ri=ri, nm=nm, i=i: P.matmul(
                                        px[0:64, 0:256], lhsT=utok[:, sub, ri, :], rhs=C(nm, 128, sub * 256, (sub + 1) * 256),
                                        start=(i == 0), stop=(i == 3)), reads=[B_ut, B_cstb], writes=[pxf])
                                    i += 1
                            store_y(192, NLAT, 256, lambda dst, db, px=px, pxf=pxf: S.op("act", lambda: A.activation(
                                out=dst, in_=px[0:64, 0:256], func=AF.Copy, scale=1.0 / 128.0), reads=[pxf], writes=[db]))

                    with Scope() as st:
                        qT_all = sb("qT_all", [128, NTOK], BF16, st)
                        kT_all = sb("kT_all", [128, NTOK], BF16, st)
                        Vext = sb("Vext", [128, 66, 128], BF16, st)
                        B_q, B_k, B_v = Buf("qT"), Buf("kT"), Buf("Vext")
                        S.op("dve", lambda: V.memset(Vext[:, :, 64:128], 1.0), writes=[B_v])
                        S.op("dve", lambda: V.memset(qT_all[64:128, :], 0.0), writes=[B_q])
                        S.op("dve", lambda: V.memset(kT_all[64:128, :], 0.0), writes=[B_k])
                        with Scope() as st2:
                            hsrc = HStream(st2)
                            cs = [sb("cs%d" % i, [64, 2, 512], F32, st2) for i in range(2)]
                            B_cs = [Buf("cs0"), Buf("cs1")]
                            ch_r = [S.chan("r0"), S.chan("r1")]
                            qn = [sb("qn%d" % i, [64, 512], F32, st2) for i in range(2)]
                            B_qn = [Buf("qn0"), Buf("qn1")]
                            tt4 = [sb("tt%d" % i, [64, 512], F32, st2) for i in range(4)]
                            B_tt4 = [Buf("tt%d" % i) for i in range(4)]
                            rstd_k = sb("rstd_k", [64, 512], F32, st2)
                            B_rstd_k = Buf("rstd_k")
                            nxt = hsrc.load(GORDER[0])
                            for gi_, g in enumerate(GORDER):
                                ht, hb = nxt
                                if gi_ + 1 < NG:
                                    nxt = hsrc.load(GORDER[gi_ + 1])
                                t0, n = gtile(g)
                                nsub = n // 128
                                k = gi_ % 2
                                if g < 16:
                                    S.dma(ch_r[k], cs[k][:], rope_d[:, :, t0:t0 + 512].rearrange("r d t -> d r t"), writes=[B_cs[k]])
                                pv, pvf = bank()
                                proj_tm(ht, hb, O_AV, 64, nsub, pv, pvf)
                                S.op("act", lambda pv=pv, g=g, nsub=nsub: A.copy(
                                    out=Vext[:, g * 4:g * 4 + nsub, 0:64], in_=pv[:, 0:nsub * 64].rearrange("p (s c) -> p s c", c=64)),
                                    reads=[pvf], writes=[B_v])
                                for which, (col0, gcol, dstT, dstB) in enumerate([(O_AQ, H_QN, qT_all, B_q), (O_AK, H_KN, kT_all, B_k)]):
                                    if which == 0 and g == 16 and not ctx_out:
                                        continue
                                    pq, pqf = bank()
                                    proj_fm(ht, hb, col0, 64, n, pq, pqf)
                                    rs_, Brs_ = (rstd, B_rstd) if which == 0 else (rstd_k, B_rstd_k)
                                    tt, B_tt = tt4[which * 2:which * 2 + 2], B_tt4[which * 2:which * 2 + 2]
                                    col_rstd(lambda c, pq=pq, n=n: pq[0:64, 0:n], [pqf], 64, 1, n, 1.0 / 64, rstd=rs_, B_rstd=Brs_)
                                    S.op("dve", lambda pq=pq, n=n, which=which, gcol=gcol: V.scalar_tensor_tensor(
                                        out=qn[which][:, 0:n], in0=pq[0:64, 0:n], scalar=hpf[0:64, li, gcol:gcol + 1], in1=rs_[0:64, 0:n],
                                        op0=ALU.mult, op1=ALU.mult), reads=[pqf, B_hp, Brs_], writes=[B_qn[which]])
                                    if g < 16:
                                        pp, ppf = bank()
                                        S.op("pe", lambda pp=pp, which=which, n=n: P.matmul(pp[0:64, 0:n], lhsT=CF("perm", 64, 0, 64), rhs=qn[which][:, 0:n],
                                                                                            start=True, stop=True), reads=[B_cstf, B_qn[which]], writes=[ppf])
                                        S.op("dve", lambda which=which, k=k, n=n: V.tensor_tensor(out=tt[0][:, 0:n], in0=qn[which][:, 0:n], in1=cs[k][:, 0, 0:n], op=ALU.mult),
                                             reads=[B_qn[which], B_cs[k]], writes=[B_tt[0]])
                                        S.op("dve", lambda pp=pp, k=k, n=n: V.tensor_tensor(out=tt[1][:, 0:n], in0=pp[0:64, 0:n], in1=cs[k][:, 1, 0:n], op=ALU.mult),
                                             reads=[ppf, B_cs[k]], writes=[B_tt[1]])
                                        S.op("dve", lambda dstT=dstT, t0=t0, n=n: V.tensor_tensor(out=dstT[0:64, t0:t0 + n], in0=tt[0][:, 0:n], in1=tt[1][:, 0:n], op=ALU.add),
                                             reads=[B_tt[0], B_tt[1]], writes=[dstB])
                                    else:
                                        S.op("act", lambda dstT=dstT, which=which, t0=t0, n=n: A.copy(out=dstT[0:64, t0:t0 + n], in_=qn[which][:, 0:n]),
                                             reads=[B_qn[which]], writes=[dstB])
                                pool_proj(g, ht, hb)
                                fnet_proj(g, gi_, ht, hb)
                        pT = [sb("pT%d" % i, [128, 512], BF16, st) for i in range(4)]
                        B_pT = [Buf("pT%d" % i) for i in range(4)]
                        pcount = [0]
                        pvs = sb("pvs", [128, 512], F32, st)
                        rec = sb("rec", [64, 512], F32, st)
                        B_pvs, B_rec = Buf("pvs"), Buf("rec")
                        pi_ = 0
                        for qi, g in enumerate(qtiles):
                            t0, n = gtile(g)
                            kbs = list(range(66)) if g < 16 else [64, 65]
                            pacc, paccf = pbank[6 + qi % 2], pbuf[6 + qi % 2]
                            nrot[0] = 6
                            pend = []

                            def issue_s(kb):
                                ps, psf = bank()
                                S.op("pe", lambda: P.matmul(ps[:, 0:n], lhsT=kT_all[:, kb * 128:(kb + 1) * 128], rhs=qT_all[:, t0:t0 + n],
                                                            start=True, stop=True), reads=[B_k, B_q], writes=[psf])
                                k3 = pcount[0] % 4
                                pcount[0] += 1
                                S.op("act", lambda: A.activation(out=pT[k3][:, 0:n], in_=ps[:, 0:n], func=AF.Exp, scale=0.125),
                                     reads=[psf], writes=[B_pT[k3]])
                                pend.append(k3)

                            LOOK = 3
                            for i in range(min(LOOK, len(kbs))):
                                issue_s(kbs[i])
                            for i, kb in enumerate(kbs):
                                if i + LOOK < len(kbs):
                                    issue_s(kbs[i + LOOK])
                                k3 = pend.pop(0)
                                S.op("pe", lambda kb=kb, k3=k3, i=i, nk=len(kbs): P.matmul(
                                    pacc[:, 0:n], lhsT=Vext[:, kb, :], rhs=pT[k3][:, 0:n], start=(i == 0), stop=(i == nk - 1)),
                                    reads=[B_v, B_pT[k3]], writes=[paccf])
                            S.op("dve", lambda pacc=pacc: V.tensor_copy(out=pvs[:, 0:n], in_=pacc[:, 0:n]), reads=[paccf], writes=[B_pvs])
                            pd, pdf = bank()
                            S.op("pe", lambda pd=pd: P.matmul(pd[0:64, 0:n], lhsT=CF("ident", 128, 64, 128), rhs=pvs[:, 0:n], start=True, stop=True),
                                 reads=[B_cstf, B_pvs], writes=[pdf])
                            S.op("dve", lambda pd=pd: V.reciprocal(out=rec[:, 0:n], in_=pd[0:64, 0:n]), reads=[pdf], writes=[B_rec])
                            store_y(128, t0, n, lambda dst, db: S.op("pool", lambda: G.tensor_tensor(
                                out=dst, in0=pvs[0:64, 0:n], in1=rec[:, 0:n], op=ALU.mult), reads=[B_pvs, B_rec], writes=[db]))
                    nrot[0] = 8
                    collective(yxp[2], ygp[2], B_yxp[2], B_ygp[2])
                    if stop == "att":
                        return

                    with Scope() as st:
                        for g in qtiles:
                            t0, n = gtile(g)
                            nsub = n // 128
                            s0, s1 = (0, 63) if g < 16 else (64, 65)
                            pb, pbf = bank()
                            for sub in range(nsub):
                                a = g * 4 + sub
                                terms = []
                                if a > s0:
                                    terms.append((a - 1, 0))
                                terms.append((a, 3 if a == s0 else (4 if a == s1 else 1)))
                                if a < s1:
                                    terms.append((a + 1, 2))
                                for i, (src, blk) in enumerate(terms):
                                    S.op("pe", lambda src=src, blk=blk, i=i, sub=sub, nt=len(terms): P.matmul(
                                        pb[0:64, sub * 128:(sub + 1) * 128], lhsT=u_tok[:, src, :],
                                        rhs=C("band", 128, blk * 128, (blk + 1) * 128), start=(i == 0), stop=(i == nt - 1)),
                                        reads=[B_u, B_cstb], writes=[pbf])
                            k = g % 2
                            S.op("dve", lambda pb=pb, k=k, n=n: V.tensor_copy(out=mT[k][:, 0:n], in_=pb[0:64, 0:n]), reads=[pbf], writes=[B_m[k]])
                            p2, p2f = bank()
                            S.op("pe", lambda p2=p2, k=k, n=n: P.matmul(p2[0:64, 0:n], lhsT=hpb[0:64, li, H_PW:H_PW + 64], rhs=mT[k][:, 0:n],
                                                                        start=True, stop=True), reads=[B_hp, B_m[k]], writes=[p2f])
                            store_y(0, t0, n, lambda dst, db, p2=p2, p2f=p2f, n=n: S.op("act", lambda: A.activation(
                                out=dst, in_=p2[0:64, 0:n], func=AF.Copy, scale=hpf[0:64, li, H_PS:H_PS + 1]),
                                reads=[p2f, B_hp], writes=[db]))
                    collective(yxp[0], ygp[0], B_yxp[0], B_ygp[0])
                    if stop == "pool":
                        return

                    with Scope() as st:
                        Ur = sb("Ur", [64, 64, 128], BF16, st)
                        Ui = sb("Ui", [64, 64, 128], BF16, st)
                        B_U = Buf("U")
                        S.dma(ch_f[0], Ur[:], fsc[0].rearrange("c (h l) -> h c l", l=128), reads=[B_fsc], writes=[B_U])
                        S.dma(ch_f[1], Ui[:], fsc[1].rearrange("c (h l) -> h c l", l=128), reads=[B_fsc], writes=[B_U])
                        Ypr = sb("Ypr", [128, 64, 64], BF16, st)
                        Ypi = sb("Ypi", [128, 64, 64], BF16, st)
                        B_Y = Buf("Y")
                        tw = [sb("tw%d" % i, [128, 4, 64], F32, st) for i in range(4)]
                        B_tw = [Buf("tw%d" % i) for i in range(4)]
                        TcB = CF("Tc", 128, 0, 64).unsqueeze(1).to_broadcast([128, 4, 64])
                        TsB = CF("Ts", 128, 0, 64).unsqueeze(1).to_broadcast([128, 4, 64])
                        for grp in range(16):
                            pb, pbf = bank()
                            for ci in range(4):
                                c = grp * 4 + ci
                                S.op("pe", lambda c=c, ci=ci, pb=pb: P.matmul(pb[:, ci * 128:(ci + 1) * 128], lhsT=Ur[:, c, :], rhs=C("dA1", 64),
                                                                              start=True, stop=False), reads=[B_U, B_cstb], writes=[pbf])
                                S.op("pe", lambda c=c, ci=ci, pb=pb: P.matmul(pb[:, ci * 128:(ci + 1) * 128], lhsT=Ui[:, c, :], rhs=C("dA2", 64),
                                                                              start=False, stop=True), reads=[B_U, B_cstb], writes=[pbf])
                            pv4 = pb[:, :].rearrange("p (c r k) -> p c r k", c=4, r=2)
                            Yr, Yi = pv4[:, :, 0, :], pv4[:, :, 1, :]
                            S.op("dve", lambda Yr=Yr: V.tensor_tensor(out=tw[0][:], in0=Yr, in1=TcB, op=ALU.mult), reads=[pbf, B_cstf], writes=[B_tw[0]])
                            S.op("dve", lambda Yi=Yi: V.tensor_tensor(out=tw[1][:], in0=Yi, in1=TsB, op=ALU.mult), reads=[pbf, B_cstf], writes=[B_tw[1]])
                            S.op("pool", lambda grp=grp: G.tensor_tensor(out=Ypr[:, grp * 4:grp * 4 + 4, :], in0=tw[0][:], in1=tw[1][:], op=ALU.add),
                                 reads=[B_tw[0], B_tw[1]], writes=[B_Y])
                            S.op("dve", lambda Yi=Yi: V.tensor_tensor(out=tw[2][:], in0=Yi, in1=TcB, op=ALU.mult), reads=[pbf, B_cstf], writes=[B_tw[2]])
                            S.op("dve", lambda Yr=Yr: V.tensor_tensor(out=tw[3][:], in0=Yr, in1=TsB, op=ALU.mult), reads=[pbf, B_cstf], writes=[B_tw[3]])
                            S.op("pool", lambda grp=grp: G.tensor_tensor(out=Ypi[:, grp * 4:grp * 4 + 4, :], in0=tw[2][:], in1=tw[3][:], op=ALU.subtract),
                                 reads=[B_tw[2], B_tw[3]], writes=[B_Y])
                        Fst = [sb("Fst%d" % i, [128, 8, 64], BF16, st) for i in range(2)]
                        B_F = [Buf("F0"), Buf("F1")]
                        for blk in range(8):
                            k = blk % 2
                            pb, pbf = bank()
                            S.op("pe", lambda pb=pb, blk=blk: P.matmul(pb[:, :], lhsT=C("C128"), rhs=Ypr[:, blk * 8:blk * 8 + 8, :].rearrange("p c k -> p (c k)"),
                                                                       start=True, stop=False), reads=[B_Y, B_cstb], writes=[pbf])
                            S.op("pe", lambda pb=pb, blk=blk: P.matmul(pb[:, :], lhsT=C("S128"), rhs=Ypi[:, blk * 8:blk * 8 + 8, :].rearrange("p c k -> p (c k)"),
                                                                       start=False, stop=True), reads=[B_Y, B_cstb], writes=[pbf])
                            S.op("act", lambda pb=pb, k=k: A.activation(out=Fst[k][:].rearrange("p c k -> p (c k)"), in_=pb[:, :], func=AF.Copy,
                                                                        scale=float(1.0 / np.sqrt(8192.0 * 64.0))), reads=[pbf], writes=[B_F[k]])
                            S.dma(ch_f[k], yxp[3][blk * 8:blk * 8 + 8, :].rearrange("c (a b) -> a c b", b=64),
                                  Fst[k][:, :, :], reads=[B_F[k]], writes=[B_yxp[3]])
                    collective(yxp[3], ygp[3], B_yxp[3], B_ygp[3])
                    if stop == "fnet":
                        return


                with Scope() as st:
                    qg_all = sb("qg_all", [64, NTOK], BF16, st)
                    kg_all = sb("kg_all", [64, NTOK], BF16, st)
                    v_tok = sb("v_tok", [128, 66, 64], BF16, st)
                    KV_all = sb("KV_all", [64, 66, 64], F32, st)
                    ET_all = sb("ET_all", [64, 66], F32, st)
                    Sprev = sb("Sprev", [64, 66, 64], BF16, st)
                    Srun = sb("Srun", [64, 64], F32, st)
                    negba = sb("negba", [64, 1], F32, st)
                    B_qg, B_kg, B_vt, B_KV, B_ET, B_Sp, B_Sr = (Buf(n_) for n_ in ["qg", "kg", "vt", "KV", "ET", "Sp", "Sr"])

                    S.op("dve", lambda: V.tensor_scalar(out=negba[:], in0=hpf[0:64, li, H_BA:H_BA + 1], scalar1=-1.0, scalar2=None, op0=ALU.mult),
                         reads=[B_hp], writes=[B_hp])
                    with Scope() as st2:
                        hsrc = HStream(st2)
                        f32t = {nm: sb("g_" + nm, [64, 512], F32, st2) for nm in ["gl", "Pc", "W", "b", "dk"]}
                        Bf = {nm: Buf("g_" + nm) for nm in f32t}
                        for new_, old_ in (("eq", "gl"), ("ek", "Pc"), ("ed", "W")):
                            f32t[new_] = f32t[old_]
                            Bf[new_] = Bf[old_]
                        rT = sb("rT", [32, 512], BF16, st2)
                        B_rT = Buf("rT")
                        khT = sb("khT", [64, 512], BF16, st2)
                        B_kh = Buf("khT")
                        khtok = [sb("khtok%d" % i, [128, 64], BF16, st2) for i in range(2)]
                        B_kt = [Buf("kt0"), Buf("kt1")]
                        nxt = hsrc.load(GORDER[0])
                        for gi_, g in enumerate(GORDER):
                            ht, hb = nxt
                            if gi_ + 1 < NG:
                                nxt = hsrc.load(GORDER[gi_ + 1])
                            t0, n = gtile(g)
                            nsub = n // 128
                            pv, pvf = bank()
                            proj_tm(ht, hb, O_GV, 64, nsub, pv, pvf)
                            S.op("act", lambda pv=pv, g=g, nsub=nsub: A.copy(
                                out=v_tok[:, g * 4:g * 4 + nsub, :], in_=pv[:, 0:nsub * 64].rearrange("p (s c) -> p s c", c=64)),
                                reads=[pvf], writes=[B_vt])
                            pr, prf = bank()
                            proj_fm(ht, hb, O_R, 32, n, pr, prf)
                            S.op("dve", lambda pr=pr, n=n: V.tensor_copy(out=rT[:, 0:n], in_=pr[0:32, 0:n]), reads=[prf], writes=[B_rT])
                            pgt, pgtf = bank()
                            S.op("pe", lambda pgt=pgt, n=n: P.matmul(pgt[0:64, 0:n], lhsT=hpb[0:32, li, H_WA:H_WA + 64], rhs=rT[:, 0:n], start=True, stop=True),
                                 reads=[B_hp, B_rT], writes=[pgtf])
                            S.op("act", lambda pgt=pgt, n=n: A.activation(out=f32t["gl"][:, 0:n], in_=pgt[0:64, 0:n], func=AF.Exp, bias=negba[:, :], scale=-1.0),
                                 reads=[pgtf, B_hp], writes=[Bf["gl"]])
                            S.op("act", lambda n=n: A.activation(out=f32t["gl"][:, 0:n], in_=f32t["gl"][:, 0:n], func=AF.Ln, bias=onesf[0:64, 0:1], scale=1.0),
                                 reads=[Bf["gl"], B_cstf], writes=[Bf["gl"]])
                            S.op("dve", lambda n=n: V.tensor_scalar(out=f32t["gl"][:, 0:n], in0=f32t["gl"][:, 0:n], scalar1=-1.0 / 16.0, scalar2=None, op0=ALU.mult),
                                 reads=[Bf["gl"]], writes=[Bf["gl"]])
                            for sub in range(nsub):
                                sl = slice(sub * 128, (sub + 1) * 128)
                                S.op("dve", lambda sl=sl: V.tensor_tensor_scan(out=f32t["Pc"][:, sl], data0=onesf[0:64, :], data1=f32t["gl"][:, sl],
                                                                               initial=0.0, op0=ALU.mult, op1=ALU.add),
                                     reads=[Bf["gl"], B_cstf], writes=[Bf["Pc"]])
                            for sub in range(nsub):
                                sl = slice(sub * 128, (sub + 1) * 128)
                                last = sub * 128 + 127
                                S.op("dve", lambda sl=sl, last=last: V.tensor_scalar(out=f32t["W"][:, sl], in0=f32t["Pc"][:, sl], scalar1=f32t["Pc"][:, last:last + 1],
                                                                                    scalar2=None, op0=ALU.subtract), reads=[Bf["Pc"]], writes=[Bf["W"]])
                            S.op("act", lambda g=g, nsub=nsub, n=n: A.activation(
                                out=ET_all[:, g * 4:g * 4 + nsub], in_=f32t["Pc"][:, 0:n].rearrange("p (s t) -> p s t", t=128)[:, :, 127], func=AF.Exp),
                                reads=[Bf["Pc"]], writes=[B_ET])
                            S.op("dve", lambda n=n: V.tensor_tensor(out=f32t["b"][32:64, 0:n], in0=f32t["gl"][32:64, 0:n], in1=f32t["W"][32:64, 0:n], op=ALU.subtract),
                                 reads=[Bf["gl"], Bf["W"]], writes=[Bf["b"]])
                            S.op("dve", lambda n=n: V.tensor_tensor(out=f32t["dk"][32:64, 0:n], in0=f32t["Pc"][32:64, 0:n], in1=f32t["gl"][32:64, 0:n], op=ALU.subtract),
                                 reads=[Bf["Pc"], Bf["gl"]], writes=[Bf["dk"]])
                            S.op("act", lambda n=n: A.activation(out=f32t["eq"][32:64, 0:n], in_=f32t["b"][32:64, 0:n], func=AF.Exp), reads=[Bf["b"], Bf["dk"]], writes=[Bf["eq"]])
                            S.op("act", lambda n=n: A.activation(out=f32t["eq"][0:32, 0:n], in_=f32t["Pc"][0:32, 0:n], func=AF.Exp), reads=[Bf["Pc"]], writes=[Bf["eq"]])
                            S.op("act", lambda n=n: A.activation(out=f32t["ed"][0:32, 0:n], in_=f32t["W"][0:32, 0:n], func=AF.Exp, scale=-1.0), reads=[Bf["W"]], writes=[Bf["ed"]])
                            S.op("act", lambda n=n: A.activation(out=f32t["ed"][32:64, 0:n], in_=f32t["dk"][32:64, 0:n], func=AF.Exp), reads=[Bf["dk"]], writes=[Bf["ed"]])
                            S.op("act", lambda n=n: A.activation(out=f32t["ek"][0:32, 0:n], in_=f32t["Pc"][0:32, 0:n], func=AF.Exp, scale=-1.0), reads=[Bf["Pc"]], writes=[Bf["ek"]])
                            S.op("act", lambda n=n: A.activation(out=f32t["ek"][32:64, 0:n], in_=f32t["b"][32:64, 0:n], func=AF.Exp, scale=-1.0), reads=[Bf["b"]], writes=[Bf["ek"]])
                            pq, pqf = bank()
                            proj_fm(ht, hb, O_GQ, 64, n, pq, pqf)
                            S.op("dve", lambda pq=pq, t0=t0, n=n: V.scalar_tensor_tensor(out=qg_all[:, t0:t0 + n], in0=pq[0:64, 0:n], scalar=float(32 ** -0.5),
                                                                                        in1=f32t["eq"][:, 0:n], op0=ALU.mult, op1=ALU.mult),
                                 reads=[pqf, Bf["eq"]], writes=[B_qg])
                            pk, pkf = bank()
                            proj_fm(ht, hb, O_GK, 64, n, pk, pkf)
                            S.op("dve", lambda pk=pk, t0=t0, n=n: V.tensor_tensor(out=kg_all[:, t0:t0 + n], in0=pk[0:64, 0:n], in1=f32t["ek"][:, 0:n], op=ALU.mult),
                                 reads=[pkf, Bf["ek"]], writes=[B_kg])
                            S.op("dve", lambda pk=pk, n=n: V.tensor_tensor(out=khT[:, 0:n], in0=pk[0:64, 0:n], in1=f32t["ed"][:, 0:n], op=ALU.mult),
                                 reads=[pkf, Bf["ed"]], writes=[B_kh])
                            pkv, pkvf = bank()
                            for sub in range(nsub):
                                k2 = sub % 2
                                pt, ptf = bank()
                                ptb = pt[:, :].bitcast(BF16)
                                S.op("pe", lambda sub=sub, ptb=ptb: P.transpose(out=ptb[:, 0:64], in_=khT[:, sub * 128:(sub + 1) * 128], identity=C("ident", 64, 0, 64)),
                                     reads=[B_kh, B_cstb], writes=[ptf])
                                S.op("act", lambda ptb=ptb, k2=k2: A.copy(out=khtok[k2][:], in_=ptb[:, 0:64]), reads=[ptf], writes=[B_kt[k2]])
                                S.op("pe", lambda sub=sub, k2=k2, pkv=pkv, g=g: P.matmul(pkv[0:64, sub * 64:(sub + 1) * 64], lhsT=khtok[k2][:], rhs=v_tok[:, g * 4 + sub, :],
                                                                                       start=True, stop=True), reads=[B_kt[k2], B_vt], writes=[pkvf])
                            S.op("dve", lambda pkv=pkv, g=g, nsub=nsub: V.tensor_copy(out=KV_all[:, g * 4:g * 4 + nsub, :],
                                                                                      in_=pkv[0:64, 0:nsub * 64].rearrange("p (s c) -> p s c", c=64)),
                                 reads=[pkvf], writes=[B_KV])
                    order_f = [64, 65] + list(range(64))
                    order_b = [65, 64] + list(range(63, -1, -1))
                    for s_ in range(1, 66):
                        for (lo, hi_, order) in ((0, 32, order_f), (32, 64, order_b)):
                            cidx, pidx = order[s_], order[s_ - 1]
                            S.op("dve", lambda lo=lo, hi_=hi_, cidx=cidx, pidx=pidx: V.scalar_tensor_tensor(
                                out=KV_all[lo:hi_, cidx, :], in0=KV_all[lo:hi_, pidx, :], scalar=ET_all[lo:hi_, cidx:cidx + 1], in1=KV_all[lo:hi_, cidx, :],
                                op0=ALU.mult, op1=ALU.add), reads=[B_ET, B_KV], writes=[B_KV])
                    S.op("act", lambda: A.copy(out=Sprev[0:32, 1:64, :], in_=KV_all[0:32, 0:63, :]), reads=[B_KV], writes=[B_Sp])
                    S.op("act", lambda: A.copy(out=Sprev[0:32, 0, :], in_=KV_all[0:32, 65, :]), reads=[B_KV], writes=[B_Sp])
                    S.op("act", lambda: A.copy(out=Sprev[0:32, 65, :], in_=KV_all[0:32, 64, :]), reads=[B_KV], writes=[B_Sp])
                    S.op("pool", lambda: G.memset(Sprev[0:32, 64, :], 0.0), writes=[B_Sp])
                    S.op("dve", lambda: V.tensor_copy(out=Sprev[32:64, 0:63, :], in_=KV_all[32:64, 1:64, :]), reads=[B_KV], writes=[B_Sp])
                    S.op("dve", lambda: V.tensor_copy(out=Sprev[32:64, 63, :], in_=KV_all[32:64, 64, :]), reads=[B_KV], writes=[B_Sp])
                    S.op("dve", lambda: V.tensor_copy(out=Sprev[32:64, 64, :], in_=KV_all[32:64, 65, :]), reads=[B_KV], writes=[B_Sp])
                    S.op("pool", lambda: G.memset(Sprev[32:64, 65, :], 0.0), writes=[B_Sp])
                    with Scope() as st2:
                        hsrc = HStream(st2)
                        qFt = [sb("qFt%d" % i, [64, 128], BF16, st2) for i in range(2)]
                        qBt = [sb("qBt%d" % i, [64, 128], BF16, st2) for i in range(2)]
                        B_qF = [Buf("qF0"), Buf("qF1")]
                        B_qB = [Buf("qB0"), Buf("qB1")]
                        for i in range(2):
                            S.op("pool", lambda i=i: G.memset(qFt[i][:], 0.0), writes=[B_qF[i]])
                            S.op("pool", lambda i=i: G.memset(qBt[i][:], 0.0), writes=[B_qB[i]])
                        Af = [sb("Af%d" % i, [128, 128], BF16, st2) for i in range(2)]
                        Ab = [sb("Ab%d" % i, [128, 128], BF16, st2) for i in range(2)]
                        B_Af = [Buf("Af0"), Buf("Af1")]
                        B_Ab = [Buf("Ab0"), Buf("Ab1")]
                        ogs = sb("ogs", [64, 512], F32, st2)
                        yn = sb("yn", [64, 512], F32, st2)
                        B_og, B_yn = Buf("ogs"), Buf("yn")
                        nxt = hsrc.load(qtiles[0])
                        for gi_, g in enumerate(qtiles):
                            ht, hb = nxt
                            if gi_ + 1 < len(qtiles):
                                nxt = hsrc.load(qtiles[gi_ + 1])
                            t0, n = gtile(g)
                            nsub = n // 128
                            pog, pogf = bank()
                            proj_fm(ht, hb, O_OG, 64, n, pog, pogf)
                            S.op("act", lambda pog=pog, n=n: A.activation(out=ogs[:, 0:n], in_=pog[0:64, 0:n], func=AF.Silu), reads=[pogf], writes=[B_og])
                            po, pof = bank()
                            for sub in range(nsub):
                                cidx = g * 4 + sub
                                tk = slice(t0 + sub * 128, t0 + (sub + 1) * 128)
                                k2 = sub % 2
                                pa, paf = bank()
                                S.op("dve", lambda k2=k2, tk=tk: V.tensor_copy(out=qFt[k2][0:32, :], in_=qg_all[0:32, tk]), reads=[B_qg], writes=[B_qF[k2]])
                                S.op("dve", lambda k2=k2, tk=tk: V.tensor_copy(out=qBt[k2][32:64, :], in_=qg_all[32:64, tk]), reads=[B_qg], writes=[B_qB[k2]])
                                S.op("pe", lambda pa=pa, tk=tk, k2=k2: P.matmul(pa[:, 0:128], lhsT=kg_all[:, tk], rhs=qFt[k2][:], start=True, stop=True),
                                     reads=[B_kg, B_qF[k2]], writes=[paf])
                                S.op("pe", lambda pa=pa, tk=tk, k2=k2: P.matmul(pa[:, 128:256], lhsT=kg_all[:, tk], rhs=qBt[k2][:], start=True, stop=True),
                                     reads=[B_kg, B_qB[k2]], writes=[paf])
                                S.op("dve", lambda pa=pa, k2=k2: V.tensor_tensor(out=Af[k2][:], in0=pa[:, 0:128], in1=C("maskF"), op=ALU.mult),
                                     reads=[paf, B_cstb], writes=[B_Af[k2]])
                                S.op("dve", lambda pa=pa, k2=k2: V.tensor_tensor(out=Ab[k2][:], in0=pa[:, 128:256], in1=C("maskB"), op=ALU.mult),
                                     reads=[paf, B_cstb], writes=[B_Ab[k2]])
                                osl = po[0:64, sub * 128:(sub + 1) * 128]
                                S.op("pe", lambda osl=osl, cidx=cidx, k2=k2: P.matmul(osl, lhsT=v_tok[:, cidx, :], rhs=Af[k2][:], start=True, stop=False),
                                     reads=[B_vt, B_Af[k2]], writes=[pof])
                                S.op("pe", lambda osl=osl, cidx=cidx, k2=k2: P.matmul(osl, lhsT=v_tok[:, cidx, :], rhs=Ab[k2][:], start=False, stop=False),
                                     reads=[B_vt, B_Ab[k2]], writes=[pof])
                                S.op("pe", lambda osl=osl, cidx=cidx, tk=tk: P.matmul(osl, lhsT=Sprev[:, cidx, :], rhs=qg_all[:, tk], start=False, stop=True),
                                     reads=[B_Sp, B_qg], writes=[pof])
                            col_rstd(lambda c, po=po, n=n: po[0:64, 0:n], [pof], 64, 1, n, 1.0 / 64)
                            S.op("dve", lambda po=po, n=n: V.scalar_tensor_tensor(out=yn[:, 0:n], in0=po[0:64, 0:n], scalar=hpf[0:64, li, H_GN:H_GN + 1],
                                                                                in1=rstd[0:64, 0:n], op0=ALU.mult, op1=ALU.mult),
                                 reads=[pof, B_hp, B_rstd], writes=[B_yn])
                            store_y(64, t0, n, lambda dst, db, n=n: S.op("pool", lambda: G.tensor_tensor(
                                out=dst, in0=yn[:, 0:n], in1=ogs[:, 0:n], op=ALU.mult), reads=[B_yn, B_og], writes=[db]))
                collective(yxp[1], ygp[1], B_yxp[1], B_ygp[1])
                if stop == "gla":
                    return


        oh = sb("oh", [128, 4], F32)
        selI = sb("selI", [128, 4, 128], BF16)
        B_sel = Buf("sel")
        S.dma(ch_c[5], oh[:], oh_d, writes=[B_sel])
        for d_ in range(4):
            S.op("dve", lambda d_=d_: V.tensor_scalar(out=selI[:, d_, :], in0=CF("ident", 128, 0, 128), scalar1=oh[:, d_:d_ + 1],
                                                   scalar2=None, op0=ALU.mult), reads=[B_sel, B_cstf], writes=[B_sel])

        def stage_c(li, tiles):
            with Scope() as st:
                wob = sb("wob", [128, 8, D], BF16, st)
                fwb = sb("fwb", [128, 2, 256], BF16, st)
                B_wo, B_fw = Buf("wob"), Buf("fwb")
                with Scope() as s2:
                    wos = sb("wos", [128, 8, D], F32, s2)
                    fws = sb("fws", [128, 2, 256], F32, s2)
                    B_wos, B_fws = Buf("wos"), Buf("fws")
                    S.dma(ch_c[4], wos[:], wout_d[li].rearrange("(k p) n -> p k n", p=128), writes=[B_wos])
                    S.dma(ch_c[5], fws[:], fw_d[li].rearrange("(k p) n -> p k n", p=128), writes=[B_fws])
                    S.op("act", lambda: A.copy(out=wob[:, 0:4, :], in_=wos[:, 0:4, :]), reads=[B_wos], writes=[B_wo])
                    S.op("dve", lambda: V.tensor_copy(out=wob[:, 4:8, :], in_=wos[:, 4:8, :]), reads=[B_wos], writes=[B_wo])
                    S.op("act", lambda: A.copy(out=fwb[:], in_=fws[:]), reads=[B_fws], writes=[B_fw])
                yc_all = [sb("yc%d" % i, [128, 8, 512], BF16, st) for i in range(8)]
                B_yc_all = [Buf("yc%d" % i) for i in range(8)]
                ch_y_all = [S.chan("y%d" % i) for i in range(8)]
                ysel = sb("ysel", [128, 8, 512], BF16, st)
                B_ysel = Buf("ysel")
                yf = sb("yf", [128, 2, 512], BF16, st)
                B_yf = Buf("yf")
                for tci, ti in enumerate(tiles):
                    t0, n, kind = TT[ti]
                    yc = yc_all[(tci % 2) * 4:(tci % 2) * 4 + 4]
                    B_yc = B_yc_all[(tci % 2) * 4:(tci % 2) * 4 + 4]
                    ch_y = ch_y_all[(tci % 2) * 4:(tci % 2) * 4 + 4]
                    for d_ in range(4):
                        for m_ in range(4):
                            for qp in range(2):
                                kc = m_ * 2 + qp
                                if kind == 0:
                                    S.dma(ch_y[d_], yc[d_][:, kc, 0:n], ygp[m_][qp * 128:(qp + 1) * 128, d_ * 2048 + t0:d_ * 2048 + t0 + n],
                                          reads=[B_ygp[m_]], writes=[B_yc[d_]], nowait=(kc > 0))
                                else:
                                    for ql in range(2):
                                        r0_ = (2 * qp + ql) * 256 + m_ * 64
                                        S.dma(ch_y[d_], yc[d_][ql * 64:(ql + 1) * 64, kc, 0:n], ygp[4][r0_:r0_ + 64, d_ * 64:d_ * 64 + n],
                                              reads=[B_ygp[4]], writes=[B_yc[d_]], nowait=(kc > 0 or ql > 0))
                    for c in range(8):
                        pb, pbf = bank()
                        for d_ in range(4):
                            S.op("pe", lambda pb=pb, d_=d_, c=c: P.matmul(pb[:, 0:n], lhsT=selI[:, d_, :], rhs=yc[d_][:, c, 0:n],
                                                                          start=(d_ == 0), stop=(d_ == 3)), reads=[B_sel, B_yc[d_]], writes=[pbf])
                        if c % 2 == 0:
                            S.op("act", lambda pb=pb, c=c: A.copy(out=ysel[:, c, 0:n], in_=pb[:, 0:n]), reads=[pbf], writes=[B_ysel])
                        else:
                            S.op("dve", lambda pb=pb, c=c: V.tensor_copy(out=ysel[:, c, 0:n], in_=pb[:, 0:n]), reads=[pbf], writes=[B_ysel])
                    for dc in range(2):
                        pb, pbf = bank()
                        for kc in range(2):
                            S.op("pe", lambda pb=pb, kc=kc, dc=dc: P.matmul(pb[:, 0:n], lhsT=fwb[:, kc, dc * 128:(dc + 1) * 128], rhs=ysel[:, 6 + kc, 0:n],
                                                                            start=(kc == 0), stop=(kc == 1)), reads=[B_fw, B_ysel], writes=[pbf])
                        S.op("act", lambda pb=pb, dc=dc: A.copy(out=yf[:, dc, 0:n], in_=pb[:, 0:n]), reads=[pbf], writes=[B_yf])
                    for c in range(8):
                        po, pof = bank()
                        for kc in range(8):
                            rhs = ysel[:, kc, 0:n] if kc < 6 else yf[:, kc - 6, 0:n]
                            S.op("pe", lambda po=po, kc=kc, c=c, rhs=rhs: P.matmul(po[:, 0:n], lhsT=wob[:, kc, c * 128:(c + 1) * 128], rhs=rhs,
                                                                                  start=(kc == 0), stop=(kc == 7)), reads=[B_wo, B_ysel, B_yf], writes=[pof])
                        S.op("dve", lambda po=po, c=c: V.scalar_tensor_tensor(
                            out=xT[:, c, t0:t0 + n], in0=po[:, 0:n], scalar=modG[:, li, 1, c, kind:kind + 1],
                            in1=xT[:, c, t0:t0 + n], op0=ALU.mult, op1=ALU.add), reads=[pof, B_mod, B_x[ti][c]], writes=[B_x[ti][c]])

        done = False
        for li in range(nlayers):
            lastl = (li == nlayers - 1)
            with Scope() as stA:
                hT = sb("hT", [128, 8, NT], BF16, stA)
                B_h = [Buf("h%d" % t) for t in range(len(TT))]
                half_ffn(li, 0, range(5), hT, B_h, stA)
                if li == 0:
                    dump("x_ffn1", xT[:], sum(B_x, []))
                for ti in range(5):
                    t0, n, kind = TT[ti]
                    adaln(li, 1, ti, lambda c, t0=t0, n=n: hT[:, c, t0:t0 + n], [B_h[ti]])
                    S.dma(ch_st[ti % 2], hxp[ti].rearrange("(c p) n -> p c n", p=128), hT[:, :, t0:t0 + n],
                          reads=[B_h[ti]], writes=[B_hxp[ti]])
                    collective(hxp[ti], hgp[ti], B_hxp[ti], B_hgp[ti])
            if li == 0:
                dump("hg0", hgp[0], [B_hgp[0]])
            if stop == "A":
                break
            stage_b(li)
            if not lastl or nlayers == 1:
                collective(yxp[4], ygp[4], B_yxp[4], B_ygp[4])
            if stop is not None and stop != "C":
                break
            ctiles = range(5) if (not lastl or nlayers == 1) else range(4)
            stage_c(li, ctiles)
            if li == 0:
                dump("x_mix", xT[:], sum(B_x, []))
            if stop == "C":
                break
            with Scope() as stA:
                hT = sb("hT", [128, 8, NT], BF16, stA)
                B_h = [Buf("h%d" % t) for t in range(len(TT))]
                half_ffn(li, 1, ctiles, hT, B_h, stA)
            if li == 0:
                dump("x_out", xT[:], sum(B_x, []))
            done = lastl

        B_out = Buf("yout")
        if done:
            with Scope() as st:
                fin = sb("fin", [128, 8, 512], F32, st)
                B_fin = Buf("fin")
                ot = [sb("ot%d" % i, [128, D], F32, st) for i in range(2)]
                B_ot = [Buf("ot0"), Buf("ot1")]
                for ti in range(4):
                    t0, n, kind = TT[ti]
                    col_rstd(lambda c: xT[:, c, t0:t0 + n], list(B_x[ti]), 128, 8, n, 1.0 / D)
                    for c in range(8):
                        S.op("dve", lambda c=c: V.scalar_tensor_tensor(
                            out=fin[:, c, 0:n], in0=xT[:, c, t0:t0 + n], scalar=vecs[:, V_FN + c:V_FN + c + 1], in1=rstd[:, 0:n],
                            op0=ALU.mult, op1=ALU.mult), reads=[B_x[ti][c], B_vecs, B_rstd], writes=[B_fin])
                    for sub in range(4):
                        k = (ti * 4 + sub) % 2
                        for half in range(2):
                            pb, pbf = bank()
                            for c4 in range(4):
                                c = half * 4 + c4
                                S.op("pe", lambda c=c, c4=c4, pb=pb, sub=sub: P.transpose(
                                    out=pb[:, c4 * 128:(c4 + 1) * 128], in_=fin[:, c, sub * 128:(sub + 1) * 128],
                                    identity=CF("ident", 128, 0, 128)), reads=[B_fin, B_cstf], writes=[pbf])
                            if half == 0:
                                S.op("dve", lambda pb=pb, k=k: V.tensor_copy(out=ot[k][:, 0:512], in_=pb[:, :]), reads=[pbf], writes=[B_ot[k]])
                            else:
                                S.op("act", lambda pb=pb, k=k: A.copy(out=ot[k][:, 512:1024], in_=pb[:, :]), reads=[pbf], writes=[B_ot[k]])
                        r0 = t0 + sub * 128
                        S.dma(ch_st[2 + k], yout[r0:r0 + 128, :], ot[k][:, :], reads=[B_ot[k]], writes=[B_out])

        S.barrier()
        print("instructions", S.n_ins, "waits", S.n_wait, "chans", S.nchan)
    return nc


def _consts(r):
    c = np.zeros((128, NCST), np.float32)

    def put(name, arr, parts=None):
        o, w = _c[name]
        arr = np.asarray(arr, np.float32)
        c[0:arr.shape[0], o:o + arr.shape[1]] = arr

    put("ident", np.eye(128))
    put("ones", np.ones((128, 128)))
    jj, ii = np.meshgrid(np.arange(128), np.arange(128), indexing="ij")
    put("maskF", (jj <= ii))
    put("maskB", (jj >= ii))
    w = (2, 4, 8, 16)[r]
    Nn = 512
    t = np.arange(Nn)
    lo = np.clip(t - w // 2, 0, Nn)
    hi = np.clip(t + w - w // 2, 0, Nn)
    s = np.arange(Nn)[:, None]
    band = ((s >= lo[None, :]) & (s < hi[None, :])) / (hi - lo)[None, :] - np.eye(Nn)
    blocks = [band[0:128, 128:256], band[128:256, 128:256], band[256:384, 128:256], band[0:128, 0:128],
              band[384:512, 384:512]]
    put("band", np.concatenate(blocks, axis=1))
    a64 = 2 * np.pi * np.outer(np.arange(64), np.arange(64)) / 64
    a128 = 2 * np.pi * np.outer(np.arange(128), np.arange(128)) / 128
    put("dA1", np.concatenate([np.cos(a64), -np.sin(a64)], 1))
    put("dA2", np.concatenate([np.sin(a64), np.cos(a64)], 1))
    put("C128", np.cos(a128))
    put("S128", np.sin(a128))
    tw = 2 * np.pi * np.outer(np.arange(128), np.arange(64)) / 8192
    put("Tc", np.cos(tw))
    put("Ts", np.sin(tw))
    put("chC", np.cos(a64))
    put("chS", -np.sin(a64))
    pm = np.zeros((64, 64))
    for d in range(64):
        sec, q = divmod(d, 32)
        partner = sec * 32 + (q + 16) % 32
        pm[partner, d] = 1.0
    put("perm", pm)
    a256 = 2 * np.pi * np.outer(np.arange(256), np.arange(256)) / 256
    put("C256", np.cos(a256).reshape(2, 128, 256).transpose(1, 0, 2).reshape(128, 512))
    put("S256", np.sin(a256).reshape(2, 128, 256).transpose(1, 0, 2).reshape(128, 512))
    return c


def _rope():
    freqs = 10000.0 ** (-np.arange(16, dtype=np.float32) / 16)
    tok = np.arange(NLAT)
    row = (tok // 64).astype(np.float32)
    col = (tok % 64).astype(np.float32)
    out = np.zeros((2, 64, NLAT), np.float32)
    for d in range(64):
        sec, q = divmod(d, 32)
        pos = row if sec == 0 else col
        ang = pos * freqs[q % 16]
        out[0, d] = np.cos(ang)
        out[1, d] = np.sin(ang) * (-1.0 if q < 16 else 1.0)
    return out


def _fm(v):
    return np.ascontiguousarray(np.asarray(v, np.float32).reshape(8, 128).T)


def make_inputs(x, c, ctx, c_ctx, w_mod, b_mod, norm_g, ffn_wg, ffn_wu, ffn_wd, w_in, w_out,
                pool_w, pool_scale, gla_wa, gla_ba, gla_norm, att_qnorm, att_knorm, fnet_w, final_norm):
    f = lambda a: np.ascontiguousarray(np.asarray(a, np.float32))
    x, c, ctx, c_ctx, w_mod, b_mod, norm_g = map(f, (x, c, ctx, c_ctx, w_mod, b_mod, norm_g))
    ffn_wg, ffn_wu, ffn_wd, w_in, w_out = map(f, (ffn_wg, ffn_wu, ffn_wd, w_in, w_out))
    pool_w, pool_scale, gla_wa, gla_ba, gla_norm = map(f, (pool_w, pool_scale, gla_wa, gla_ba, gla_norm))
    att_qnorm, att_knorm, fnet_w, final_norm = map(f, (att_qnorm, att_knorm, fnet_w, final_norm))
    rope = _rope()
    maps = []
    for core in range(8):
        b, r = divmod(core, 4)
        xin = np.concatenate([x[b, r * TL:(r + 1) * TL], ctx[b, r * TC:(r + 1) * TC], np.zeros((64, D), np.float32)], 0)
        vecs = np.zeros((128, NVEC), np.float32)
        vecs[:, V_C:V_C + 16] = np.stack([_fm(c[b]), _fm(c_ctx)], -1).reshape(128, 16)
        for li in range(2):
            vecs[:, V_BMOD + li * 18:V_BMOD + (li + 1) * 18] = b_mod[li, r * 2304:(r + 1) * 2304].reshape(18, 128).T
            for n3 in range(3):
                vecs[:, V_NG + (li * 3 + n3) * 8:V_NG + (li * 3 + n3) * 8 + 8] = _fm(norm_g[li, n3])
        vecs[:, V_FN:V_FN + 8] = _fm(final_norm)
        kv = r // 2
        cols = np.concatenate([
            np.arange(r * 64, r * 64 + 64),
            256 + r * 32 + np.arange(32), 256 + r * 32 + np.arange(32),
            384 + r * 32 + np.arange(32), 384 + r * 32 + np.arange(32),
            768 + np.arange(32),
            800 + r * 64 + np.arange(64),
            512 + r * 64 + np.arange(64),
            1056 + r * 64 + np.arange(64),
            1312 + kv * 64 + np.arange(64),
            1440 + kv * 64 + np.arange(64),
            1568 + r * 64 + np.arange(64)])
        assert cols.size == NHC
        hp = np.zeros((2, 128, NHP), np.float32)
        for li in range(2):
            hp[li, 0:64, H_PW:H_PW + 64] = pool_w[li, r]
            hp[li, 0:64, H_PS] = pool_scale[li, r * 64:(r + 1) * 64]
            hp[li, 0:16, H_WA:H_WA + 32] = gla_wa[li, 0][:, r * 32:(r + 1) * 32]
            hp[li, 16:32, H_WA + 32:H_WA + 64] = gla_wa[li, 1][:, r * 32:(r + 1) * 32]
            hp[li, 0:32, H_BA] = gla_ba[li, 0, r * 32:(r + 1) * 32]
            hp[li, 32:64, H_BA] = gla_ba[li, 1, r * 32:(r + 1) * 32]
            hp[li, 0:64, H_GN] = gla_norm[li]
            hp[li, 0:64, H_QN] = att_qnorm[li]
            hp[li, 0:64, H_KN] = att_knorm[li]
        maps.append({
            "oh": np.tile(np.eye(4, dtype=np.float32)[r][None, :], (128, 1)),
            "xin": np.ascontiguousarray(xin), "vecs": vecs, "cst": _consts(r), "rope": rope, "hp": hp,
            "w_mod_s": np.ascontiguousarray(w_mod[:, :, r * 2304:(r + 1) * 2304]), "ffn_wg": ffn_wg, "ffn_wu": ffn_wu, "ffn_wd": ffn_wd,
            "w_in_h": np.ascontiguousarray(w_in[:, :, cols]), "w_out": w_out, "fnet_w": fnet_w,
        })
    return maps


_NC = None


def kernel(**inputs):
    global _NC
    if _NC is None:
        _NC = build()
    maps = make_inputs(**inputs)
    res = run_bass_kernel_spmd(_NC, maps, core_ids=list(range(8)))
    out = np.zeros((2, NLAT, D), np.float32)
    for core in range(8):
        b, r = divmod(core, 4)
        out[b, r * TL:(r + 1) * TL] = res.results[core]["yout"]
    return out
```
